# Optimizing a Trainium2 kernel written in Bass

```python
import math
import jax
import jax.numpy as jnp
from jax import lax
import numpy as np

D_MODEL = 1024
BATCH = 4
SEQ = 4096
DEPTH = 4

CTX_LEN = 256
GRID_W = 64
ATTN_WIDTH = D_MODEL // 2
SSM_WIDTH = D_MODEL - ATTN_WIDTH
HEAD_DIM = 64
N_HEADS = ATTN_WIDTH // HEAD_DIM
N_KV_HEADS = 2
GQA_GROUP = N_HEADS // N_KV_HEADS
KV_WIDTH = N_KV_HEADS * HEAD_DIM
IN_COLS = ATTN_WIDTH + 2 * KV_WIDTH + SSM_WIDTH
SSM_P = 16
SSM_G = SSM_WIDTH // SSM_P
SSM_N = 64
DT_MIN = 0.001
DT_MAX = 0.1
D_FF = 2816
CONV_W = 3
Q_BLOCK = 128
ROPE_THETA = 10000.0
NORM_EPS = 1e-6

kernel_name = 'hybrid_gqa_s5_convffn_prefix_trunk'


def _rms_norm(x, g):
    xf = x.astype(jnp.float32)
    y = xf * lax.rsqrt(jnp.mean(xf * xf, axis=-1, keepdims=True) + NORM_EPS)
    return y.astype(x.dtype) * g


def _modulate(h, shift, scale):
    return h * (1 + scale) + shift


def _heads(t, n):
    return t.reshape(t.shape[:-1] + (n, HEAD_DIM))


def _axial_rope_tables(n_tokens):
    rows = n_tokens // GRID_W
    r, col = jnp.meshgrid(jnp.arange(rows), jnp.arange(GRID_W), indexing='ij')
    pos = jnp.stack([r.reshape(-1), col.reshape(-1)], axis=-1).astype(jnp.float32)
    n_freq = HEAD_DIM // 4
    freqs = ROPE_THETA ** (-jnp.arange(n_freq, dtype=jnp.float32) / n_freq)
    ang = pos[:, :, None] * freqs
    return jnp.cos(ang), jnp.sin(ang)


def _apply_rope(t, cos, sin):
    b, s, h, _ = t.shape
    tr = t.reshape(b, s, h, 2, 2, HEAD_DIM // 4)
    t1, t2 = tr[..., 0, :], tr[..., 1, :]
    cs = cos[None, :, None].astype(t.dtype)
    sn = sin[None, :, None].astype(t.dtype)
    out = jnp.stack([t1 * cs - t2 * sn, t2 * cs + t1 * sn], axis=-2)
    return out.reshape(t.shape)


def _softmax_attend(q, k, v):
    s = jnp.einsum('bqkgd,bskd->bkgqs', q, k).astype(jnp.float32) * (HEAD_DIM ** -0.5)
    p = jax.nn.softmax(s, axis=-1).astype(v.dtype)
    return jnp.einsum('bkgqs,bskd->bqkgd', p, v)


def _latent_attention(q, k_lat, v_lat, k_ctx, v_ctx):
    b, s, _, _ = q.shape
    k_all = jnp.concatenate([k_ctx, k_lat], axis=1)
    v_all = jnp.concatenate([v_ctx, v_lat], axis=1)
    n_blk = s // Q_BLOCK
    qb = q.reshape(b, n_blk, Q_BLOCK, N_KV_HEADS, GQA_GROUP, HEAD_DIM).transpose(1, 0, 2, 3, 4, 5)
    o = lax.map(lambda qblk: _softmax_attend(qblk, k_all, v_all), qb)
    return o.transpose(1, 0, 2, 3, 4, 5).reshape(b, s, ATTN_WIDTH)


def _context_attention(q, k, v):
    b, n, _, _ = q.shape
    o = _softmax_attend(q.reshape(b, n, N_KV_HEADS, GQA_GROUP, HEAD_DIM), k, v)
    return o.reshape(b, n, ATTN_WIDTH)


def _zoh(lam_re, lam_im, log_step, b_re, b_im):
    dt = jnp.exp(log_step.astype(jnp.float32))[:, None]
    lr = lam_re.astype(jnp.float32)
    li = lam_im.astype(jnp.float32)
    mag = jnp.exp(lr * dt)
    a_re = mag * jnp.cos(li * dt)
    a_im = mag * jnp.sin(li * dt)
    den = lr * lr + li * li
    f_re = ((a_re - 1) * lr + a_im * li) / den
    f_im = (a_im * lr - (a_re - 1) * li) / den
    br = b_re.astype(jnp.float32)
    bi = b_im.astype(jnp.float32)
    bb_re = f_re[..., None] * br - f_im[..., None] * bi
    bb_im = f_re[..., None] * bi + f_im[..., None] * br
    return a_re, a_im, bb_re, bb_im


def _complex_affine_combine(e1, e2):
    a1r, a1i, b1r, b1i = e1
    a2r, a2i, b2r, b2i = e2
    return (a2r * a1r - a2i * a1i, a2r * a1i + a2i * a1r,
            a2r * b1r - a2i * b1i + b2r, a2r * b1i + a2i * b1r + b2i)


def _ssm_states(u, a_re, a_im, bb_re, bb_im, h0, reverse):
    bu_re = jnp.einsum('blgp,gnp->lbgn', u, bb_re)
    bu_im = jnp.einsum('blgp,gnp->lbgn', u, bb_im)
    if h0 is not None:
        h0_re, h0_im = h0
        first = -1 if reverse else 0
        bu_re = bu_re.at[first].add(a_re * h0_re - a_im * h0_im)
        bu_im = bu_im.at[first].add(a_re * h0_im + a_im * h0_re)
    n = u.shape[1]
    ar = jnp.broadcast_to(a_re, (n, 1) + a_re.shape)
    ai = jnp.broadcast_to(a_im, (n, 1) + a_im.shape)
    _, _, h_re, h_im = lax.associative_scan(_complex_affine_combine, (ar, ai, bu_re, bu_im),
                                            reverse=reverse, axis=0)
    return h_re, h_im


def _ssm_readout(h_re, h_im, c_re, c_im):
    return (jnp.einsum('lbgn,gpn->blgp', h_re, c_re.astype(jnp.float32))
            - jnp.einsum('lbgn,gpn->blgp', h_im, c_im.astype(jnp.float32)))


def _ssm_glu(y, w_glu, b_glu):
    z = jax.nn.gelu(y)
    return z * jax.nn.sigmoid(z @ w_glu + b_glu)


def _s5_mixer(u_lat, u_ctx, p, need_ctx_out):
    b, s, _ = u_lat.shape
    n_c = u_ctx.shape[1]
    ul = u_lat.astype(jnp.float32).reshape(b, s, SSM_G, SSM_P)
    uc = u_ctx.astype(jnp.float32).reshape(b, n_c, SSM_G, SSM_P)
    d = p['ssm_d'].astype(jnp.float32)
    y_lat = d * ul
    y_ctx = d * uc if need_ctx_out else None
    for direction, reverse in ((0, False), (1, True)):
        a_re, a_im, bb_re, bb_im = _zoh(p['ssm_lambda_re'][direction], p['ssm_lambda_im'][direction],
                                        p['ssm_log_step'][direction], p['ssm_b_re'][direction],
                                        p['ssm_b_im'][direction])
        hc_re, hc_im = _ssm_states(uc, a_re, a_im, bb_re, bb_im, None, reverse)
        last = 0 if reverse else -1
        hl_re, hl_im = _ssm_states(ul, a_re, a_im, bb_re, bb_im, (hc_re[last], hc_im[last]), reverse)
        y_lat = y_lat + _ssm_readout(hl_re, hl_im, p['ssm_c_re'][direction], p['ssm_c_im'][direction])
        if need_ctx_out:
            y_ctx = y_ctx + _ssm_readout(hc_re, hc_im, p['ssm_c_re'][direction], p['ssm_c_im'][direction])
    out_lat = _ssm_glu(y_lat.reshape(b, s, SSM_WIDTH).astype(u_lat.dtype), p['w_glu'], p['b_glu'])
    out_ctx = None
    if need_ctx_out:
        out_ctx = _ssm_glu(y_ctx.reshape(b, n_c, SSM_WIDTH).astype(u_ctx.dtype), p['w_glu'], p['b_glu'])
    return out_lat, out_ctx


def _split_in(proj):
    o1 = ATTN_WIDTH
    o2 = o1 + KV_WIDTH
    o3 = o2 + KV_WIDTH
    return proj[..., :o1], proj[..., o1:o2], proj[..., o2:o3], proj[..., o3:]


def _merge_groups(attn, ssm, out_norm_g, w_out):
    a = _rms_norm(attn, out_norm_g[:ATTN_WIDTH])
    s = _rms_norm(ssm, out_norm_g[ATTN_WIDTH:])
    return jnp.concatenate([a, s], axis=-1) @ w_out


def _conv_ffn(h, w_up, conv_w, conv_b, w_down):
    ag = h @ w_up
    pad = jnp.pad(ag, ((0, 0), (1, 1), (0, 0)))
    ag = pad[:, :-2] * conv_w[0] + pad[:, 1:-1] * conv_w[1] + pad[:, 2:] * conv_w[2] + conv_b
    a, g = jnp.split(ag, 2, axis=-1)
    return (jax.nn.silu(g) * a) @ w_down


def _layer(x, ctx, c, c_ctx, cos, sin, p, need_ctx_out):
    mod_lat = (jax.nn.silu(c) @ p['w_ada'] + p['b_ada'])[:, None, :]
    mod_ctx = jax.nn.silu(c_ctx) @ p['w_ada'] + p['b_ada']
    sh1, sc1, g1, sh2, sc2, g2 = jnp.split(mod_lat, 6, axis=-1)
    csh1, csc1, cg1, csh2, csc2, cg2 = jnp.split(mod_ctx, 6, axis=-1)

    h_lat = _modulate(_rms_norm(x, p['norm1_g']), sh1, sc1)
    h_ctx = _modulate(_rms_norm(ctx, p['norm1_g']), csh1, csc1)
    q_l, k_l, v_l, u_l = _split_in(h_lat @ p['w_in'])
    q_c, k_c, v_c, u_c = _split_in(h_ctx @ p['w_in'])

    q_l = _apply_rope(_rms_norm(_heads(q_l, N_HEADS), p['q_norm_g']), cos, sin)
    k_l = _apply_rope(_rms_norm(_heads(k_l, N_KV_HEADS), p['k_norm_g']), cos, sin)
    v_l = _heads(v_l, N_KV_HEADS)
    k_c = _rms_norm(_heads(k_c, N_KV_HEADS), p['k_norm_g'])
    v_c = _heads(v_c, N_KV_HEADS)

    attn_lat = _latent_attention(q_l, k_l, v_l, k_c, v_c)
    ssm_lat, ssm_ctx = _s5_mixer(u_l, u_c, p, need_ctx_out)

    x = x + g1 * _merge_groups(attn_lat, ssm_lat, p['out_norm_g'], p['w_out'])
    h2 = _modulate(_rms_norm(x, p['norm2_g']), sh2, sc2)
    x = x + g2 * _conv_ffn(h2, p['w_up'], p['conv_w'], p['conv_b'], p['w_down'])

    if need_ctx_out:
        q_c = _rms_norm(_heads(q_c, N_HEADS), p['q_norm_g'])
        attn_ctx = _context_attention(q_c, k_c, v_c)
        ctx = ctx + cg1 * _merge_groups(attn_ctx, ssm_ctx, p['out_norm_g'], p['w_out'])
        hc2 = _modulate(_rms_norm(ctx, p['norm2_g']), csh2, csc2)
        ctx = ctx + cg2 * _conv_ffn(hc2, p['w_up'], p['conv_w'], p['conv_b'], p['w_down'])
    return x, ctx


def setup_inputs(seed: int = 0) -> dict:
    key = jax.random.key(seed)
    ks = jax.random.split(key, 32)
    f32 = jnp.float32

    def nrm(k, shape, std):
        return std * jax.random.normal(k, shape, f32)

    lam_im_base = jnp.pi * jnp.arange(SSM_N, dtype=f32)
    return {
        'x': nrm(ks[0], (BATCH, SEQ, D_MODEL), 1.0),
        'c': nrm(ks[1], (BATCH, D_MODEL), 1.0),
        'ctx': nrm(ks[2], (BATCH, CTX_LEN, D_MODEL), 1.0),
        'c_ctx': nrm(ks[3], (D_MODEL,), 1.0),
        'w_ada': nrm(ks[4], (DEPTH, D_MODEL, 6 * D_MODEL), D_MODEL ** -0.5),
        'b_ada': nrm(ks[5], (DEPTH, 6 * D_MODEL), 0.02),
        'norm1_g': 1.0 + nrm(ks[6], (DEPTH, D_MODEL), 0.02),
        'w_in': nrm(ks[7], (DEPTH, D_MODEL, IN_COLS), D_MODEL ** -0.5),
        'q_norm_g': 1.0 + nrm(ks[8], (DEPTH, HEAD_DIM), 0.02),
        'k_norm_g': 1.0 + nrm(ks[9], (DEPTH, HEAD_DIM), 0.02),
        'ssm_lambda_re': -0.5 + nrm(ks[10], (DEPTH, 2, SSM_G, SSM_N), 0.01),
        'ssm_lambda_im': lam_im_base + nrm(ks[11], (DEPTH, 2, SSM_G, SSM_N), 0.01),
        'ssm_log_step': jax.random.uniform(ks[12], (DEPTH, 2, SSM_G), f32,
                                           minval=math.log(DT_MIN), maxval=math.log(DT_MAX)),
        'ssm_b_re': nrm(ks[13], (DEPTH, 2, SSM_G, SSM_N, SSM_P), (2 * SSM_P) ** -0.5),
        'ssm_b_im': nrm(ks[14], (DEPTH, 2, SSM_G, SSM_N, SSM_P), (2 * SSM_P) ** -0.5),
        'ssm_c_re': nrm(ks[15], (DEPTH, 2, SSM_G, SSM_P, SSM_N), (2 * SSM_N) ** -0.5),
        'ssm_c_im': nrm(ks[16], (DEPTH, 2, SSM_G, SSM_P, SSM_N), (2 * SSM_N) ** -0.5),
        'ssm_d': nrm(ks[17], (DEPTH, SSM_G, SSM_P), 0.3),
        'w_glu': nrm(ks[18], (DEPTH, SSM_WIDTH, SSM_WIDTH), SSM_WIDTH ** -0.5),
        'b_glu': nrm(ks[19], (DEPTH, SSM_WIDTH), 0.02),
        'out_norm_g': 1.0 + nrm(ks[20], (DEPTH, D_MODEL), 0.02),
        'w_out': nrm(ks[21], (DEPTH, D_MODEL, D_MODEL), D_MODEL ** -0.5),
        'norm2_g': 1.0 + nrm(ks[22], (DEPTH, D_MODEL), 0.02),
        'w_up': nrm(ks[23], (DEPTH, D_MODEL, 2 * D_FF), D_MODEL ** -0.5),
        'conv_w': nrm(ks[24], (DEPTH, CONV_W, 2 * D_FF), CONV_W ** -0.5),
        'conv_b': nrm(ks[25], (DEPTH, 2 * D_FF), 0.02),
        'w_down': nrm(ks[26], (DEPTH, D_FF, D_MODEL), D_FF ** -0.5),
    }


def reference(x, c, ctx, c_ctx, w_ada, b_ada, norm1_g, w_in, q_norm_g, k_norm_g,
              ssm_lambda_re, ssm_lambda_im, ssm_log_step, ssm_b_re, ssm_b_im, ssm_c_re, ssm_c_im,
              ssm_d, w_glu, b_glu, out_norm_g, w_out, norm2_g, w_up, conv_w, conv_b, w_down):
    cos, sin = _axial_rope_tables(x.shape[1])
    for layer in range(DEPTH):
        p = {
            'w_ada': w_ada[layer], 'b_ada': b_ada[layer], 'norm1_g': norm1_g[layer],
            'w_in': w_in[layer], 'q_norm_g': q_norm_g[layer], 'k_norm_g': k_norm_g[layer],
            'ssm_lambda_re': ssm_lambda_re[layer], 'ssm_lambda_im': ssm_lambda_im[layer],
            'ssm_log_step': ssm_log_step[layer], 'ssm_b_re': ssm_b_re[layer], 'ssm_b_im': ssm_b_im[layer],
            'ssm_c_re': ssm_c_re[layer], 'ssm_c_im': ssm_c_im[layer], 'ssm_d': ssm_d[layer],
            'w_glu': w_glu[layer], 'b_glu': b_glu[layer], 'out_norm_g': out_norm_g[layer],
            'w_out': w_out[layer], 'norm2_g': norm2_g[layer], 'w_up': w_up[layer],
            'conv_w': conv_w[layer], 'conv_b': conv_b[layer], 'w_down': w_down[layer],
        }
        x, ctx = _layer(x, ctx, c, c_ctx, cos, sin, p, layer < DEPTH - 1)
    return x
```

```python
import numpy as np
from contextlib import ExitStack
import concourse.bass as bass
import concourse.mybir as mybir
from concourse.bass_utils import run_bass_kernel_spmd

F32 = mybir.dt.float32
BF16 = mybir.dt.bfloat16
AF = mybir.ActivationFunctionType
ALU = mybir.AluOpType
AX = mybir.AxisListType

D = 1024
S = 4096
NCTX = 256
T = S + NCTX
NT = T // 128
DEPTH = 4
DFF = 2816
EPS = 1e-6
HEAD_PERM = [0, 4, 1, 5, 2, 6, 3, 7]


class Dep:
    __slots__ = ("w", "r")

    def __init__(self):
        self.w = None
        self.r = {}


class KB:
    def __init__(self, nc, es, ndma=24):
        self.nc = nc
        self.es = es
        self.E = {"pe": nc.tensor, "act": nc.scalar, "dve": nc.vector, "pool": nc.gpsimd, "sp": nc.sync}
        self.sems = []
        self.esem = {}
        self.cnt = {}
        self.seen = {e: {} for e in self.E}
        for e in self.E:
            self._newsem(e)
        self.dq = []
        for i in range(ndma):
            self.sems.append(es.enter_context(nc.semaphore(f"dq{i}")))
            self.dq.append([len(self.sems) - 1, 0])
        self.dn = 0
        self.nins = 0

    def _newsem(self, e):
        self.sems.append(self.es.enter_context(self.nc.semaphore(f"e{e}{len(self.sems)}")))
        self.esem[e] = len(self.sems) - 1
        self.cnt[e] = 0

    def wait(self, e, tk):
        si, v = tk
        if self.seen[e].get(si, 0) >= v:
            return
        self.E[e].wait_ge(self.sems[si], v)
        self.seen[e][si] = v
        self.nins += 1

    def _deps(self, e, reads, writes):
        for d in reads:
            if d.w is not None:
                self.wait(e, d.w)
        for d in writes:
            if d.w is not None:
                self.wait(e, d.w)
            for si, v in d.r.items():
                self.wait(e, (si, v))

    def _mark(self, tk, reads, writes):
        for d in reads:
            if d.r.get(tk[0], 0) < tk[1]:
                d.r[tk[0]] = tk[1]
        for d in writes:
            d.w = tk
            d.r = {}

    def op(self, e, fn, reads=(), writes=()):
        self._deps(e, reads, writes)
        if self.cnt[e] >= 30000:
            self._newsem(e)
        inst = fn(self.E[e])
        self.cnt[e] += 1
        inst.then_inc(self.sems[self.esem[e]], 1)
        tk = (self.esem[e], self.cnt[e])
        self._mark(tk, reads, writes)
        self.nins += 1
        return tk

    def barrier(self):
        for e in self.E:
            for f in self.E:
                if f != e and self.cnt[f] > 0:
                    self.wait(e, (self.esem[f], self.cnt[f]))
            for si, v in self.dq:
                if v > 0:
                    self.wait(e, (si, v))

    def dma(self, q, out, in_, reads=(), writes=()):
        k = self.dn
        self.dn = (self.dn + 1) % len(self.dq)
        si, v = self.dq[k]
        if v > 0:
            self.wait(q, (si, v))
        self._deps(q, reads, writes)
        inst = self.E[q].dma_start(out=out, in_=in_)
        v += 16
        self.dq[k][1] = v
        inst.then_inc(self.sems[si], 16)
        tk = (si, v)
        self._mark(tk, reads, writes)
        self.nins += 1
        return tk


class Ctx:
    pass


def build(depth=DEPTH, dbg=None):
    dbg = dbg or set()
    nc = bass.Bass("TRN2", target_bir_lowering=False)
    es = ExitStack()
    c = Ctx()
    c.nc = nc
    c.es = es

    def din(name, shape, dt=F32):
        return nc.dram_tensor(name, list(shape), dt, kind="ExternalInput").ap()

    def dout(name, shape, dt=F32):
        return nc.dram_tensor(name, list(shape), dt, kind="ExternalOutput").ap()

    def dscr(name, shape, dt=F32):
        return nc.dram_tensor(name, list(shape), dt, kind="Internal").ap()

    c.xin = din("xin", [T, D])
    c.cT = din("cT", [128, 8, 2])
    c.w_ada = din("w_ada", [DEPTH, D, 6 * D])
    c.b_ada = din("b_ada", [DEPTH, 6 * D])
    c.n1g = din("n1g", [DEPTH, D])
    c.n2g = din("n2g", [DEPTH, D])
    c.w_in = din("w_in", [DEPTH, D, 1280])
    c.qg = din("qg", [DEPTH, 64])
    c.kg = din("kg", [DEPTH, 64])
    c.cos_t = din("cos_t", [T, 64])
    c.sin_t = din("sin_t", [T, 64])
    c.ident = din("ident", [128, 128])
    c.xres = dscr("xres", [T, D])
    c.modv = dscr("modv", [2, 6 * D])
    c.uT_d = dscr("uT_d", [512, T], BF16)
    c.aT_d = dscr("aT_d", [512, T], BF16)
    c.ong = din("ong", [DEPTH, D])
    c.sT_d = dscr("sT_d", [512, T], BF16)
    c.zT_d = dscr("zT_d", [512, T], BF16)
    c.lamN_re = din("lamN_re", [DEPTH, 128, 64])
    c.lamN_im = din("lamN_im", [DEPTH, 128, 64])
    c.lsN = din("lsN", [DEPTH, 128, 64])
    c.lamQ_re = din("lamQ_re", [DEPTH, 128, 512])
    c.lamQ_im = din("lamQ_im", [DEPTH, 128, 512])
    c.lsQ = din("lsQ", [DEPTH, 128, 512])
    c.BQ_re = din("BQ_re", [DEPTH, 128, 512])
    c.BQ_im = din("BQ_im", [DEPTH, 128, 512])
    c.CN = din("CN", [DEPTH, 128, 8, 8, 16])
    c.dQ = din("dQ", [DEPTH, 128, 4])
    c.w_glu = din("w_glu", [DEPTH, 512, 512])
    c.bgluT = din("bgluT", [DEPTH, 128, 4])
    c.ongT = din("ongT", [DEPTH, 128, 4])
    c.consts = din("consts", [128, 16])
    c.h2T_d = dscr("h2T_d", [D, T], BF16)
    c.w_out = din("w_out", [DEPTH, D, D])
    c.w_up = din("w_up", [DEPTH, D, 2 * DFF])
    c.w_down = din("w_down", [DEPTH, DFF, D])
    c.cwT = din("cwT", [DEPTH, 128, 44, 3])
    c.cbT = din("cbT", [DEPTH, 128, 44])
    c.out = dout("out", [S, D])
    c.dbg = {}

    with es:
        kb = KB(nc, es)
        c.kb = kb

        def sb(name, shape, dt=F32):
            return es.enter_context(nc.sbuf_tensor(name, list(shape), dt))

        c.sb = sb
        c.identf = sb("identf", [128, 128], F32)
        c.identb = sb("identb", [128, 128], BF16)
        c.d_ident = Dep()
        kb.dma("sp", c.identf[:], c.ident[:], writes=[c.d_ident])
        kb.op("dve", lambda e: e.tensor_copy(out=c.identb[:], in_=c.identf[:]), reads=[c.d_ident], writes=[c.d_ident])
        c.cst = sb("cst", [128, 16], F32)
        c.onesf = sb("onesf", [128, 128], F32)
        c.d_const = Dep()
        kb.dma("sp", c.cst[:], c.consts[:], writes=[c.d_const])
        kb.op("pool", lambda e: e.memset(c.onesf[:], 1.0), writes=[c.d_const])
        c.halfpi, c.mE, c.mO, c.sgnC = c.cst[:, 0:1], c.cst[:, 2:3], c.cst[:, 3:4], c.cst[:, 4:5]
        c.sT = sb("sT", [128, 8, 2], F32)
        c.d_sT = Dep()
        kb.dma("sp", c.sT[:], c.cT[:], writes=[c.d_sT])
        kb.op("act", lambda e: e.activation(out=c.sT[:], in_=c.sT[:], func=AF.Silu), reads=[c.d_sT], writes=[c.d_sT])
        c.d_xres = [Dep() for _ in range(NT)]
        for i in range(NT):
            kb.dma("sp", c.xres[i * 128:(i + 1) * 128, :], c.xin[i * 128:(i + 1) * 128, :], writes=[c.d_xres[i]])

        for L in range(depth):
            layer(c, L, dbg)

        tks = []
        for i in range(2, NT):
            tks.append(kb.dma("sp", c.out[(i - 2) * 128:(i - 1) * 128, :], c.xres[i * 128:(i + 1) * 128, :],
                              reads=[c.d_xres[i]]))
        for tk in tks:
            kb.wait("sp", tk)
        for e in ("pe", "act", "dve", "pool"):
            if kb.cnt[e] > 0:
                kb.wait("sp", (kb.esem[e], kb.cnt[e]))
    return nc, c


def layer(c, L, dbg):
    nc, kb = c.nc, c.kb
    c.d_qT, c.d_kT, c.d_V, c.d_uT_d = Dep(), Dep(), Dep(), Dep()
    c.d_aT_d, c.d_sT_d, c.d_h2T_d = Dep(), Dep(), Dep()
    phase0(c, L, None, None)
    if "p0" in dbg:
        o = nc.dram_tensor("dbg_modv", [2, 6 * D], F32, kind="ExternalOutput").ap()
        tk = kb.dma("sp", o, c.modv[:, :], reads=[c.d_modv])
        kb.wait("sp", tk)
        return
    with ExitStack() as ls:
        def lsb(name, shape, dt=F32):
            return ls.enter_context(nc.sbuf_tensor(f"{name}_{L}", list(shape), dt))
        c.qT = lsb("qT", [128, 4, T], BF16)
        c.kT = lsb("kT", [128, T], BF16)
        c.Vaug = lsb("Vaug", [128, NT, 2, 66], BF16)
        kb.op("pool", lambda e: e.memset(c.Vaug[:, :, :, 64:65], 1.0), writes=[c.d_V])
        phaseA(c, L, dbg)
        if "pA1" in dbg:
            return
        if "pA" in dbg:
            dbg_dump(c, "qT", c.qT, [c.d_qT], BF16)
            dbg_dump(c, "kT", c.kT, [c.d_kT], BF16)
            dbg_dump(c, "Vaug", c.Vaug, [c.d_V], BF16)
            return
        phaseB(c, L, dbg)
        if "pB" in dbg:
            o = nc.dram_tensor("dbg_aT", [512, T], BF16, kind="ExternalOutput").ap()
            tk = kb.dma("sp", o, c.aT_d[:, :], reads=[c.d_aT_d])
            kb.wait("sp", tk)
            return
    if "noS" in dbg:
        with ExitStack() as zs:
            z = zs.enter_context(nc.sbuf_tensor(f"zt_{L}", [128, 4, T], BF16))
            dz = Dep()
            kb.op("pool", lambda e: e.memset(z[:], 0.0), writes=[dz])
            kb.dma("sp", c.sT_d.rearrange("(k p) t -> p k t", p=128), z[:], reads=[dz], writes=[c.d_sT_d])
            kb.barrier()
    else:
        phaseS(c, L, dbg)
        if "pS1" in dbg:
            return
        if "pS" in dbg:
            o = nc.dram_tensor("dbg_sT", [512, T], BF16, kind="ExternalOutput").ap()
            tk = kb.dma("sp", o, c.sT_d[:, :], reads=[c.d_sT_d])
            kb.wait("sp", tk)
            return
    phaseC1(c, L, dbg)
    phaseC2(c, L, dbg)


def dbg_dump(c, name, tile, deps, dt=F32):
    o = c.nc.dram_tensor("dbg_" + name, list(tile.shape), dt, kind="ExternalOutput").ap()
    tk = c.kb.dma("sp", o, tile[:], reads=deps)
    c.kb.wait("sp", tk)


def phase0(c, L, lsb, lps):
    nc, kb = c.nc, c.kb
    with ExitStack() as ps_:
        def tsb(name, shape, dt=F32):
            return ps_.enter_context(nc.sbuf_tensor(f"{name}_{L}", list(shape), dt))
        wa = [tsb(f"wa{i}", [128, 8, 512]) for i in range(2)]
        d_wa = [Dep(), Dep()]
        mod = tsb("mod", [2, 6 * D])
        d_mod = Dep()
        bada = tsb("bada", [2, 6 * D])
        d_bada = Dep()
        ng = tsb("ng", [2, 2, D])
        d_ng = Dep()
        vec = tsb("vec", [2, 6 * D])
        d_vec = Dep()
        ps = ps_.enter_context(nc.psum_tensor(f"p0ps_{L}", [128, 512], F32))
        d_ps = Dep()
        kb.dma("sp", bada[:], c.b_ada[L:L + 1, :].to_broadcast([2, 6 * D]), writes=[d_bada])
        kb.dma("sp", ng[:, 0, :], c.n1g[L:L + 1, :].to_broadcast([2, D]), writes=[d_ng])
        kb.dma("sp", ng[:, 1, :], c.n2g[L:L + 1, :].to_broadcast([2, D]), writes=[d_ng])
        wsrc = c.w_ada[L].rearrange("(kt p) n -> p kt n", p=128)
        for j in range(12):
            b = j % 2
            kb.dma("sp", wa[b][:], wsrc[:, :, j * 512:(j + 1) * 512], writes=[d_wa[b]])
            for kt in range(8):
                kb.op("pe", lambda e: e.matmul(ps[0:2, :], lhsT=c.sT[:, kt, :], rhs=wa[b][:, kt, :],
                                               start=(kt == 0), stop=(kt == 7)),
                      reads=[d_wa[b], c.d_sT], writes=[d_ps])
            kb.op("dve", lambda e: e.tensor_tensor(out=mod[:, j * 512:(j + 1) * 512], in0=ps[0:2, :],
                                                   in1=bada[:, j * 512:(j + 1) * 512], op=ALU.add),
                  reads=[d_ps, d_bada], writes=[d_mod])
        def sl(k):
            return slice(k * D, (k + 1) * D)
        kb.op("dve", lambda e: e.scalar_tensor_tensor(out=vec[:, sl(0)], in0=mod[:, sl(1)], scalar=1.0, in1=ng[:, 0, :],
                                                      op0=ALU.add, op1=ALU.mult), reads=[d_mod, d_ng], writes=[d_vec])
        kb.op("dve", lambda e: e.tensor_copy(out=vec[:, sl(1)], in_=mod[:, sl(0)]), reads=[d_mod], writes=[d_vec])
        kb.op("dve", lambda e: e.tensor_copy(out=vec[:, sl(2)], in_=mod[:, sl(2)]), reads=[d_mod], writes=[d_vec])
        kb.op("dve", lambda e: e.scalar_tensor_tensor(out=vec[:, sl(3)], in0=mod[:, sl(4)], scalar=1.0, in1=ng[:, 1, :],
                                                      op0=ALU.add, op1=ALU.mult), reads=[d_mod, d_ng], writes=[d_vec])
        kb.op("dve", lambda e: e.tensor_copy(out=vec[:, sl(4)], in_=mod[:, sl(3)]), reads=[d_mod], writes=[d_vec])
        kb.op("dve", lambda e: e.tensor_copy(out=vec[:, sl(5)], in_=mod[:, sl(5)]), reads=[d_mod], writes=[d_vec])
        if not hasattr(c, "d_modv"):
            c.d_modv = Dep()
        kb.dma("sp", c.modv[:, :], vec[:], reads=[d_vec], writes=[c.d_modv])
        kb.barrier()


def load_bc(c, tile, row, k, dep):
    c.kb.dma("sp", tile[:], c.modv[row:row + 1, k * D:(k + 1) * D].to_broadcast([128, D]),
             reads=[c.d_modv], writes=[dep])


def rstd_from_ss(c, ss, n, d_ss, shape_cols=1):
    kb = c.kb
    kb.op("dve", lambda e: e.tensor_scalar(out=ss, in0=ss, scalar1=1.0 / n, scalar2=EPS, op0=ALU.mult, op1=ALU.add),
          reads=[d_ss], writes=[d_ss])
    kb.op("act", lambda e: e.activation(out=ss, in_=ss, func=AF.Sqrt), reads=[d_ss], writes=[d_ss])
    kb.op("dve", lambda e: e.reciprocal(out=ss, in_=ss), reads=[d_ss], writes=[d_ss])


def phaseA(c, L, dbg):
    nc, kb = c.nc, c.kb
    with ExitStack() as ps_:
        def tsb(name, shape, dt=F32):
            return ps_.enter_context(nc.sbuf_tensor(f"{name}_{L}", list(shape), dt))

        def tps(name, shape, dt=F32):
            return ps_.enter_context(nc.psum_tensor(f"{name}_{L}", list(shape), dt))
        win = tsb("win", [128, 8, 1280], BF16)
        d_win = Dep()
        wsrc = c.w_in[L].rearrange("(kt p) n -> p kt n", p=128)
        for kt in range(8):
            kb.dma("pool", win[:, kt, :], wsrc[:, kt, :], writes=[d_win])
        bc = {}
        d_bc = Dep()
        for nm, row, k in (("A1l", 0, 0), ("S1l", 0, 1), ("A1c", 1, 0), ("S1c", 1, 1)):
            bc[nm] = tsb(nm, [128, D])
            load_bc(c, bc[nm], row, k, d_bc)
        gq = tsb("gq", [128, 64])
        gk = tsb("gk", [128, 64])
        d_g = Dep()
        kb.dma("sp", gq[:], c.qg[L:L + 1, :].to_broadcast([128, 64]), writes=[d_g])
        kb.dma("sp", gk[:], c.kg[L:L + 1, :].to_broadcast([128, 64]), writes=[d_g])
        kb.op("dve", lambda e: e.tensor_scalar(out=gq[:], in0=gq[:], scalar1=0.125, scalar2=None, op0=ALU.mult),
              reads=[d_g], writes=[d_g])

        NB = 2
        xt = [tsb(f"xt{i}", [128, D]) for i in range(NB)]
        d_xt = [Dep() for _ in range(NB)]
        cs = [tsb(f"cs{i}", [128, 2, 64]) for i in range(NB)]
        d_cs = [Dep() for _ in range(NB)]
        junk = tsb("junk", [128, D])
        d_junk = Dep()
        ss = tsb("ss", [128, 1])
        d_ss = Dep()
        hf = tsb("hf", [128, D])
        d_hf = Dep()
        hb = tsb("hb", [128, D], BF16)
        d_hb = Dep()
        hT = tsb("hT", [128, 8, 128], BF16)
        d_hT = Dep()
        tp = tps("tp", [128, 8, 128], BF16)
        d_tp = Dep()
        pp = [tps(f"pp{i}", [128, 512]) for i in range(3)]
        d_pp = [Dep() for _ in range(3)]
        qf = tsb("qf", [128, 640])
        d_qf = Dep()
        sq = tsb("sq", [128, 640])
        d_sq = Dep()
        ssq = tsb("ssq", [128, 10])
        d_ssq = Dep()
        ra = tsb("ra", [128, 640])
        d_ra = Dep()
        rb = tsb("rb", [128, 640])
        d_rb = Dep()
        qb = tsb("qb", [128, 640], BF16)
        d_qb = Dep()
        ub = tsb("ub", [128, 512], BF16)
        d_ub = Dep()
        tp2 = tps("tp2", [128, 8, 128], BF16)
        d_tp2 = Dep()
        uTs = [tsb(f"uTs{i}", [128, 4, 128], BF16) for i in range(2)]
        d_uTs = [Dep(), Dep()]

        import os
        CUT = int(os.environ.get("CUT", "99"))
        for i in range(int(os.environ.get("NTA", NT))):
            b = i % NB
            isctx = i < 2
            A1 = bc["A1c"] if isctx else bc["A1l"]
            S1 = bc["S1c"] if isctx else bc["S1l"]
            rows = slice(i * 128, (i + 1) * 128)
            kb.dma("sp", xt[b][:], c.xres[rows, :], reads=[c.d_xres[i]], writes=[d_xt[b]])
            kb.dma("sp", cs[b][:, 0, :], c.cos_t[rows, :], writes=[d_cs[b]])
            kb.dma("sp", cs[b][:, 1, :], c.sin_t[rows, :], writes=[d_cs[b]])
            if CUT < 1:
                continue
            kb.op("act", lambda e: e.activation(out=junk[:], in_=xt[b][:], func=AF.Square, accum_out=ss[:]),
                  reads=[d_xt[b]], writes=[d_junk, d_ss])
            rstd_from_ss(c, ss[:], D, d_ss)
            kb.op("dve", lambda e: e.scalar_tensor_tensor(out=hf[:], in0=xt[b][:], scalar=ss[:], in1=A1[:],
                                                          op0=ALU.mult, op1=ALU.mult),
                  reads=[d_xt[b], d_ss, d_bc], writes=[d_hf])
            kb.op("dve", lambda e: e.tensor_tensor(out=hb[:], in0=hf[:], in1=S1[:], op=ALU.add),
                  reads=[d_hf, d_bc], writes=[d_hb])
            if "pA1" in dbg and i == 0:
                dbg_dump(c, "ss", ss, [d_ss]); dbg_dump(c, "hf", hf, [d_hf]); dbg_dump(c, "hb", hb, [d_hb], BF16)
                dbg_dump(c, "xt", xt[b], [d_xt[b]]); dbg_dump(c, "A1", A1, [d_bc])
            if CUT < 2:
                continue
            for kt in range(8):
                kb.op("pe", lambda e: e.transpose(tp[:, kt, :], hb[:, kt * 128:(kt + 1) * 128], c.identb[:]),
                      reads=[d_hb, c.d_ident], writes=[d_tp])
            kb.op("act", lambda e: e.activation(out=hT[:], in_=tp[:], func=AF.Copy), reads=[d_tp], writes=[d_hT])
            SUB = int(os.environ.get("SUB", "9"))
            if SUB < 1:
                continue
            for j, (c0, c1) in enumerate(((0, 512), (512, 1024), (1024, 1280))):
                for kt in range(8):
                    kb.op("pe", lambda e: e.matmul(pp[j][:, 0:c1 - c0], lhsT=hT[:, kt, :], rhs=win[:, kt, c0:c1],
                                                   start=(kt == 0), stop=(kt == 7)),
                          reads=[d_hT, d_win], writes=[d_pp[j]])
            if SUB < 2:
                continue
            kb.op("act", lambda e: e.activation(out=qf[:, 0:512], in_=pp[0][:, :], func=AF.Copy),
                  reads=[d_pp[0]], writes=[d_qf])
            kb.op("act", lambda e: e.activation(out=qf[:, 512:640], in_=pp[1][:, 0:128], func=AF.Copy),
                  reads=[d_pp[1]], writes=[d_qf])
            if SUB < 3:
                continue
            if os.environ.get("NOV") is None:
                kb.op("act", lambda e: e.activation(out=c.Vaug[:, i, :, 0:64],
                                                    in_=pp[1][:, 128:256].rearrange("p (h d) -> p h d", h=2), func=AF.Copy),
                      reads=[d_pp[1]], writes=[c.d_V])
            if os.environ.get("NOU") is not None:
                continue
            kb.op("act", lambda e: e.activation(out=ub[:, 0:256], in_=pp[1][:, 256:512], func=AF.Copy), reads=[d_pp[1]], writes=[d_ub])
            kb.op("act", lambda e: e.activation(out=ub[:, 256:512], in_=pp[2][:, 0:256], func=AF.Copy), reads=[d_pp[2]], writes=[d_ub])
            if CUT < 3:
                continue
            kb.op("dve", lambda e: e.tensor_tensor(out=sq[:], in0=qf[:], in1=qf[:], op=ALU.mult), reads=[d_qf], writes=[d_sq])
            kb.op("dve", lambda e: e.tensor_reduce(out=ssq[:], in_=sq[:].rearrange("p (h d) -> p h d", d=64), axis=AX.X,
                                                   op=ALU.add), reads=[d_sq], writes=[d_ssq])
            rstd_from_ss(c, ssq[:], 64, d_ssq)
            q3 = qf[:].rearrange("p (h d) -> p h d", d=64)
            ra3 = ra[:].rearrange("p (h d) -> p h d", d=64)
            kb.op("dve", lambda e: e.tensor_tensor(out=ra3, in0=q3, in1=ssq[:].unsqueeze(2).to_broadcast([128, 10, 64]),
                                                   op=ALU.mult), reads=[d_qf, d_ssq], writes=[d_ra])
            kb.op("dve", lambda e: e.tensor_tensor(out=ra3[:, 0:8, :], in0=ra3[:, 0:8, :],
                                                   in1=gq[:].unsqueeze(1).to_broadcast([128, 8, 64]), op=ALU.mult),
                  reads=[d_ra, d_g], writes=[d_ra])
            kb.op("dve", lambda e: e.tensor_tensor(out=ra3[:, 8:10, :], in0=ra3[:, 8:10, :],
                                                   in1=gk[:].unsqueeze(1).to_broadcast([128, 2, 64]), op=ALU.mult),
                  reads=[d_ra, d_g], writes=[d_ra])
            if CUT < 4:
                continue
            cosb = cs[b][:, 0, :].unsqueeze(1).to_broadcast([128, 10, 64])
            t6 = ra[:].rearrange("p (h a x f) -> p h a x f", h=10, a=2, x=2)
            r6 = rb[:].rearrange("p (h a x f) -> p h a x f", h=10, a=2, x=2)
            sin4 = cs[b][:, 1, :].rearrange("p (a x f) -> p a x f", a=2, x=2)
            for half in range(2):
                kb.op("pool", lambda e: e.tensor_tensor(
                    out=r6[:, :, :, half, :], in0=t6[:, :, :, 1 - half, :],
                    in1=sin4[:, :, half, :].unsqueeze(1).to_broadcast([128, 10, 2, 16]), op=ALU.mult),
                    reads=[d_ra, d_cs[b]], writes=[d_rb])
            kb.op("dve", lambda e: e.tensor_tensor(out=ra3, in0=ra3, in1=cosb, op=ALU.mult),
                  reads=[d_ra, d_cs[b]], writes=[d_ra])
            kb.op("dve", lambda e: e.tensor_tensor(out=qb[:], in0=ra[:], in1=rb[:], op=ALU.add),
                  reads=[d_ra, d_rb], writes=[d_qb])
            if CUT < 5:
                continue
            for j in range(5):
                kb.op("pe", lambda e: e.transpose(tp2[:, j, :], qb[:, j * 128:(j + 1) * 128], c.identb[:]),
                      reads=[d_qb, c.d_ident], writes=[d_tp2])
            kb.op("act", lambda e: e.activation(out=c.qT[:, :, rows], in_=tp2[:, 0:4, :], func=AF.Copy),
                  reads=[d_tp2], writes=[c.d_qT])
            kb.op("act", lambda e: e.activation(out=c.kT[:, rows], in_=tp2[:, 4, :], func=AF.Copy),
                  reads=[d_tp2], writes=[c.d_kT])
            for j in range(4):
                kb.op("pe", lambda e: e.transpose(tp[:, j, :], ub[:, j * 128:(j + 1) * 128], c.identb[:]),
                      reads=[d_ub, c.d_ident], writes=[d_tp])
            ut = uTs[i % 2]
            kb.op("dve", lambda e: e.tensor_copy(out=ut[:], in_=tp[:, 0:4, :]), reads=[d_tp], writes=[d_uTs[i % 2]])
            kb.dma("sp", c.uT_d.rearrange("(k p) t -> p k t", p=128)[:, :, rows], ut[:], reads=[d_uTs[i % 2]],
                   writes=[c.d_uT_d])
        kb.barrier()


def rope_tables():
    rows = S // 64
    r, col = np.meshgrid(np.arange(rows), np.arange(64), indexing="ij")
    pos = np.stack([r.reshape(-1), col.reshape(-1)], axis=-1).astype(np.float32)
    freqs = (np.float32(10000.0) ** (-np.arange(16, dtype=np.float32) / np.float32(16))).astype(np.float32)
    ang = (pos[:, :, None] * freqs).astype(np.float32)
    cs, sn = np.cos(ang).astype(np.float32), np.sin(ang).astype(np.float32)
    cos64 = np.ones((T, 2, 2, 16), np.float32)
    sin64 = np.zeros((T, 2, 2, 16), np.float32)
    cos64[NCTX:, :, 0, :] = cs
    cos64[NCTX:, :, 1, :] = cs
    sin64[NCTX:, :, 0, :] = -sn
    sin64[NCTX:, :, 1, :] = sn
    return cos64.reshape(T, 64), sin64.reshape(T, 64)


def prep_shared(inp):
    f = lambda a: np.ascontiguousarray(np.asarray(a, dtype=np.float32))
    sh = {}
    sh["w_ada"] = f(inp["w_ada"])
    sh["b_ada"] = f(inp["b_ada"])
    sh["n1g"] = f(inp["norm1_g"])
    sh["n2g"] = f(inp["norm2_g"])
    w_in = f(inp["w_in"]).copy()
    qcols = np.concatenate([np.arange(h * 64, (h + 1) * 64) for h in HEAD_PERM])
    w_in[:, :, 0:512] = w_in[:, :, qcols]
    sh["w_in"] = w_in
    ong = f(inp["out_norm_g"]).copy()
    ong[:, 0:512] = ong[:, qcols]
    sh["ong"] = ong
    w_out = f(inp["w_out"]).copy()
    w_out[:, 0:512, :] = w_out[:, qcols, :]
    sh["w_out"] = w_out
    sh["w_up"] = f(inp["w_up"])
    sh["w_down"] = f(inp["w_down"])
    cw = f(inp["conv_w"])
    sh["cwT"] = np.ascontiguousarray(cw.reshape(DEPTH, 3, 44, 128).transpose(0, 3, 2, 1))
    sh["cbT"] = np.ascontiguousarray(f(inp["conv_b"]).reshape(DEPTH, 44, 128).transpose(0, 2, 1))
    lre, lim, lst = f(inp["ssm_lambda_re"]), f(inp["ssm_lambda_im"]), f(inp["ssm_log_step"])
    def layN(a):
        t = a.reshape(DEPTH, 64, 64).transpose(0, 2, 1)
        return np.ascontiguousarray(np.concatenate([t, t], axis=1))
    sh["lamN_re"], sh["lamN_im"] = layN(lre), layN(lim)
    sh["lsN"] = layN(np.broadcast_to(lst[..., None], (DEPTH, 2, 32, 64)))
    def layQ(a):
        t = a.reshape(DEPTH, 2, 4, 8, 1, 64)
        t = np.broadcast_to(t, (DEPTH, 2, 4, 8, 16, 64)).transpose(0, 3, 4, 1, 2, 5)
        return np.ascontiguousarray(t.reshape(DEPTH, 128, 512))
    sh["lamQ_re"], sh["lamQ_im"] = layQ(lre), layQ(lim)
    sh["lsQ"] = layQ(np.broadcast_to(lst[..., None], (DEPTH, 2, 32, 64)))
    def layB(a):
        t = a.reshape(DEPTH, 2, 4, 8, 64, 16).transpose(0, 3, 5, 1, 2, 4)
        return np.ascontiguousarray(t.reshape(DEPTH, 128, 512))
    sh["BQ_re"], sh["BQ_im"] = layB(f(inp["ssm_b_re"])), layB(f(inp["ssm_b_im"]))
    def layC(a):
        t = a.reshape(DEPTH, 2, 4, 8, 16, 64).transpose(0, 5, 1, 2, 3, 4)
        return t.reshape(DEPTH, 64, 8, 8, 16)
    sh["CN"] = np.ascontiguousarray(np.concatenate([layC(f(inp["ssm_c_re"])), layC(f(inp["ssm_c_im"]))], axis=1))
    sh["dQ"] = np.ascontiguousarray(f(inp["ssm_d"]).reshape(DEPTH, 4, 128).transpose(0, 2, 1))
    sh["w_glu"] = f(inp["w_glu"])
    sh["bgluT"] = np.ascontiguousarray(f(inp["b_glu"]).reshape(DEPTH, 4, 128).transpose(0, 2, 1))
    sh["ongT"] = np.ascontiguousarray(f(inp["out_norm_g"])[:, 512:].reshape(DEPTH, 4, 128).transpose(0, 2, 1))
    cst = np.zeros((128, 16), np.float32)
    for g8_ in range(8):
        cst[:, 8 + g8_] = (np.arange(128) // 16 == g8_)
    cst[:, 0] = np.pi / 2
    cst[:, 1] = EPS
    p = np.arange(128)
    cst[:, 2] = ((p // 16) % 2 == 0)
    cst[:, 3] = ((p // 16) % 2 == 1)
    cst[:, 4] = np.where(p < 64, 1.0, -1.0)
    sh["consts"] = cst
    sh["qg"] = f(inp["q_norm_g"])
    sh["kg"] = f(inp["k_norm_g"])
    cos64, sin64 = rope_tables()
    sh["cos_t"] = cos64
    sh["sin_t"] = sin64
    sh["ident"] = np.eye(128, dtype=np.float32)
    return sh


def prep_core(inp, b):
    m = {}
    m["xin"] = np.ascontiguousarray(np.concatenate([np.asarray(inp["ctx"][b]), np.asarray(inp["x"][b])], axis=0),
                                    dtype=np.float32)
    cv = np.stack([np.asarray(inp["c"][b]), np.asarray(inp["c_ctx"])], axis=0).astype(np.float32)
    m["cT"] = np.ascontiguousarray(cv.reshape(2, 8, 128).transpose(2, 1, 0))
    return m


def phaseB(c, L, dbg):
    import os
    nc, kb = c.nc, c.kb
    with ExitStack() as ps_:
        def tsb(name, shape, dt=F32):
            return ps_.enter_context(nc.sbuf_tensor(f"{name}_{L}", list(shape), dt))

        def tps(name, shape, dt=F32):
            return ps_.enter_context(nc.psum_tensor(f"{name}_{L}", list(shape), dt))
        st = [tps(f"st{i}", [128, 512]) for i in range(2)]
        d_st = [Dep(), Dep()]
        oacc = [tps(f"oa{i}", [128, 512]) for i in range(4)]
        d_oa = [Dep() for _ in range(4)]
        tpb = tps("tpb", [128, 8, 128], BF16)
        d_tpb = Dep()
        pt = [tsb(f"pt{i}", [128, 512], BF16) for i in range(3)]
        d_pt = [Dep() for _ in range(3)]
        osb = [tsb(f"osb{i}", [128, 66]) for i in range(2)]
        d_osb = [Dep(), Dep()]
        rl = [tsb(f"rl{i}", [128, 1]) for i in range(2)]
        d_rl = [Dep(), Dep()]
        attn = tsb("attn", [128, 4, 512])
        d_attn = [Dep() for _ in range(4)]
        ga = tsb("ga", [128, 512])
        d_ga = Dep()
        kb.dma("sp", ga[:], c.ong[L:L + 1, 0:512].to_broadcast([128, 512]), writes=[d_ga])
        junk = tsb("junkb", [128, 512])
        d_junk = Dep()
        ss = tsb("ssb", [128, 1])
        d_ss = Dep()
        ab = tsb("ab", [128, 512], BF16)
        d_ab = Dep()
        aTs = [tsb(f"aTs{i}", [128, 4, 128], BF16) for i in range(2)]
        d_aTs = [Dep(), Dep()]
        aTv = c.aT_d.rearrange("(k p) t -> p k t", p=128)

        qblocks = [(0, 256, [0, 1])] + [(256 + 512 * j, 512, list(range(NT))) for j in range(8)]
        nqb = int(os.environ.get("NQB", len(qblocks)))
        ev = 0
        for (q0, nq, kts) in qblocks[:nqb]:
            nsub = nq // 128
            for hp in range(8):
                cidx, half = hp // 2, hp % 2
                pr = slice(64 * half, 64 * half + 64)

                def qk(n):
                    kt = kts[n]
                    kb.op("pe", lambda e: e.matmul(st[n % 2][:, 0:nq], lhsT=c.kT[pr, kt * 128:(kt + 1) * 128],
                                                   rhs=c.qT[pr, cidx, q0:q0 + nq], start=True, stop=True),
                          reads=[c.d_kT, c.d_qT], writes=[d_st[n % 2]])
                qk(0)
                for n, kt in enumerate(kts):
                    if n + 1 < len(kts):
                        qk(n + 1)
                    pb = pt[n % 3]
                    kb.op("act", lambda e: e.activation(out=pb[:, 0:nq], in_=st[n % 2][:, 0:nq], func=AF.Exp),
                          reads=[d_st[n % 2]], writes=[d_pt[n % 3]])
                    for s in range(nsub):
                        kb.op("pe", lambda e: e.matmul(oacc[s][:, 0:65], lhsT=pb[:, s * 128:(s + 1) * 128],
                                                       rhs=c.Vaug[:, kt, half, 0:65], start=(n == 0),
                                                       stop=(n == len(kts) - 1)),
                              reads=[d_pt[n % 3], c.d_V], writes=[d_oa[s]])
                for s in range(nsub):
                    o = osb[ev % 2]
                    r = rl[ev % 2]
                    kb.op("act", lambda e: e.activation(out=o[:, 0:65], in_=oacc[s][:, 0:65], func=AF.Copy),
                          reads=[d_oa[s]], writes=[d_osb[ev % 2]])
                    kb.op("dve", lambda e: e.reciprocal(out=r[:], in_=o[:, 64:65]), reads=[d_osb[ev % 2]],
                          writes=[d_rl[ev % 2]])
                    kb.op("dve", lambda e: e.tensor_scalar(out=attn[:, s, hp * 64:(hp + 1) * 64], in0=o[:, 0:64],
                                                           scalar1=r[:], scalar2=None, op0=ALU.mult),
                          reads=[d_osb[ev % 2], d_rl[ev % 2]], writes=[d_attn[s]])
                    ev += 1
            for s in range(nsub):
                kb.op("act", lambda e: e.activation(out=junk[:], in_=attn[:, s, :], func=AF.Square, accum_out=ss[:]),
                      reads=[d_attn[s]], writes=[d_junk, d_ss])
                rstd_from_ss(c, ss[:], 512, d_ss)
                kb.op("dve", lambda e: e.scalar_tensor_tensor(out=ab[:], in0=attn[:, s, :], scalar=ss[:], in1=ga[:],
                                                              op0=ALU.mult, op1=ALU.mult),
                      reads=[d_attn[s], d_ss, d_ga], writes=[d_ab])
                for j in range(4):
                    kb.op("pe", lambda e: e.transpose(tpb[:, j, :], ab[:, j * 128:(j + 1) * 128], c.identb[:]),
                          reads=[d_ab, c.d_ident], writes=[d_tpb])
                at = aTs[s % 2]
                kb.op("act", lambda e: e.activation(out=at[:], in_=tpb[:, 0:4, :], func=AF.Copy),
                      reads=[d_tpb], writes=[d_aTs[s % 2]])
                t0 = q0 + s * 128
                kb.dma("sp", aTv[:, :, t0:t0 + 128], at[:], reads=[d_aTs[s % 2]], writes=[c.d_aT_d])
        kb.barrier()


def phaseC1(c, L, dbg):
    nc, kb = c.nc, c.kb
    with ExitStack() as ps_:
        def tsb(name, shape, dt=F32):
            return ps_.enter_context(nc.sbuf_tensor(f"{name}_{L}", list(shape), dt))

        def tps(name, shape, dt=F32):
            return ps_.enter_context(nc.psum_tensor(f"{name}_{L}", list(shape), dt))
        wout = tsb("wout", [128, 8, D], BF16)
        d_w = Dep()
        wsrc = c.w_out[L].rearrange("(kt p) n -> p kt n", p=128)
        for kt in range(8):
            kb.dma("pool", wout[:, kt, :], wsrc[:, kt, :], writes=[d_w])
        bc = {}
        d_bc = Dep()
        for nm, row, k in (("G1l", 0, 2), ("A2l", 0, 3), ("S2l", 0, 4), ("G1c", 1, 2), ("A2c", 1, 3), ("S2c", 1, 4)):
            bc[nm] = tsb(nm, [128, D])
            load_bc(c, bc[nm], row, k, d_bc)
        NB = 2
        xt = [tsb(f"xc{i}", [128, D]) for i in range(NB)]
        d_xt = [Dep() for _ in range(NB)]
        asT = [tsb(f"asT{i}", [128, 8, 128], BF16) for i in range(NB)]
        d_as = [Dep() for _ in range(NB)]
        pp = [tps(f"pc{i}", [128, 512]) for i in range(2)]
        d_pp = [Dep(), Dep()]
        tp = tps("tpc", [128, 8, 128], BF16)
        d_tp = Dep()
        tmp = tsb("tmpc", [128, D])
        d_tmp = Dep()
        x1 = [tsb(f"x1{i}", [128, D]) for i in range(NB)]
        d_x1 = [Dep() for _ in range(NB)]
        junk = tsb("junkc", [128, D])
        d_junk = Dep()
        ss = tsb("ssc", [128, 1])
        d_ss = Dep()
        hf = tsb("hfc", [128, D])
        d_hf = Dep()
        hb = tsb("hbc", [128, D], BF16)
        d_hb = Dep()
        hT = [tsb(f"hTc{i}", [128, 8, 128], BF16) for i in range(NB)]
        d_hT = [Dep() for _ in range(NB)]
        aTv = c.aT_d.rearrange("(k p) t -> p k t", p=128)
        sTv = c.sT_d.rearrange("(k p) t -> p k t", p=128)
        hTv = c.h2T_d.rearrange("(k p) t -> p k t", p=128)
        for i in range(NT):
            b = i % NB
            sfx = "c" if i < 2 else "l"
            rows = slice(i * 128, (i + 1) * 128)
            kb.dma("sp", xt[b][:], c.xres[rows, :], reads=[c.d_xres[i]], writes=[d_xt[b]])
            kb.dma("sp", asT[b][:, 0:4, :], aTv[:, :, rows], reads=[c.d_aT_d], writes=[d_as[b]])
            kb.dma("sp", asT[b][:, 4:8, :], sTv[:, :, rows], reads=[c.d_sT_d], writes=[d_as[b]])
            for j in range(2):
                for kt in range(8):
                    kb.op("pe", lambda e: e.matmul(pp[j][:, :], lhsT=asT[b][:, kt, :], rhs=wout[:, kt, j * 512:(j + 1) * 512],
                                                   start=(kt == 0), stop=(kt == 7)),
                          reads=[d_as[b], d_w], writes=[d_pp[j]])
                kb.op("dve", lambda e: e.tensor_tensor(out=tmp[:, j * 512:(j + 1) * 512], in0=pp[j][:, :],
                                                       in1=bc["G1" + sfx][:, j * 512:(j + 1) * 512], op=ALU.mult),
                      reads=[d_pp[j], d_bc], writes=[d_tmp])
            kb.op("dve", lambda e: e.tensor_tensor(out=x1[b][:], in0=xt[b][:], in1=tmp[:], op=ALU.add),
                  reads=[d_xt[b], d_tmp], writes=[d_x1[b]])
            kb.dma("sp", c.xres[rows, :], x1[b][:], reads=[d_x1[b]], writes=[c.d_xres[i]])
            kb.op("act", lambda e: e.activation(out=junk[:], in_=x1[b][:], func=AF.Square, accum_out=ss[:]),
                  reads=[d_x1[b]], writes=[d_junk, d_ss])
            rstd_from_ss(c, ss[:], D, d_ss)
            kb.op("dve", lambda e: e.scalar_tensor_tensor(out=hf[:], in0=x1[b][:], scalar=ss[:], in1=bc["A2" + sfx][:],
                                                          op0=ALU.mult, op1=ALU.mult),
                  reads=[d_x1[b], d_ss, d_bc], writes=[d_hf])
            kb.op("dve", lambda e: e.tensor_tensor(out=hb[:], in0=hf[:], in1=bc["S2" + sfx][:], op=ALU.add),
                  reads=[d_hf, d_bc], writes=[d_hb])
            for kt in range(8):
                kb.op("pe", lambda e: e.transpose(tp[:, kt, :], hb[:, kt * 128:(kt + 1) * 128], c.identb[:]),
                      reads=[d_hb, c.d_ident], writes=[d_tp])
            kb.op("act", lambda e: e.activation(out=hT[b][:], in_=tp[:], func=AF.Copy), reads=[d_tp], writes=[d_hT[b]])
            kb.dma("sp", hTv[:, :, rows], hT[b][:], reads=[d_hT[b]], writes=[c.d_h2T_d])
        kb.barrier()


def phaseC2(c, L, dbg):
    import os
    nc, kb = c.nc, c.kb
    with ExitStack() as ps_:
        def tsb(name, shape, dt=F32):
            return ps_.enter_context(nc.sbuf_tensor(f"{name}_{L}", list(shape), dt))

        def tps(name, shape, dt=F32):
            return ps_.enter_context(nc.psum_tensor(f"{name}_{L}", list(shape), dt))
        wup = tsb("wup", [128, 8, 2 * DFF], BF16)
        wdn = tsb("wdn", [128, 22, D], BF16)
        d_w = Dep()
        usrc = c.w_up[L].rearrange("(kt p) n -> p kt n", p=128)
        for kt in range(8):
            for q in range(4):
                kb.dma("pool", wup[:, kt, q * 1408:(q + 1) * 1408], usrc[:, kt, q * 1408:(q + 1) * 1408], writes=[d_w])
        dsrc = c.w_down[L].rearrange("(kt p) n -> p kt n", p=128)
        for kt in range(22):
            kb.dma("pool", wdn[:, kt, :], dsrc[:, kt, :], writes=[d_w])
        cw = tsb("cw", [128, 44, 3])
        cb = tsb("cb", [128, 44])
        d_cw = Dep()
        kb.dma("sp", cw[:], c.cwT[L], writes=[d_cw])
        kb.dma("sp", cb[:], c.cbT[L], writes=[d_cw])
        G2 = {}
        d_bc = Dep()
        for nm, row in (("l", 0), ("c", 1)):
            G2[nm] = tsb("G2" + nm, [128, D])
            load_bc(c, G2[nm], row, 5, d_bc)
        NB = 2
        h2 = [tsb(f"h2t{i}", [128, 8, 258], BF16) for i in range(NB)]
        d_h2 = [Dep() for _ in range(NB)]
        psA = [tps(f"psA{i}", [128, 512]) for i in range(2)]
        psG = [tps(f"psG{i}", [128, 512]) for i in range(2)]
        d_pA = [Dep(), Dep()]
        d_pG = [Dep(), Dep()]
        pso = [tps(f"pso{i}", [128, 512]) for i in range(2)]
        d_po = [Dep(), Dep()]
        ca = [tsb(f"ca{i}", [128, 256]) for i in range(2)]
        cg = [tsb(f"cg{i}", [128, 256]) for i in range(2)]
        d_ca = [Dep(), Dep()]
        d_cg = [Dep(), Dep()]
        hid = [tsb(f"hid{i}", [128, 22, 256], BF16) for i in range(NB)]
        d_hid = [Dep() for _ in range(NB)]
        xt = [tsb(f"xf{i}", [128, D]) for i in range(NB)]
        d_xt = [Dep() for _ in range(NB)]
        tmp = tsb("tmpf", [128, D])
        d_tmp = Dep()
        hTv = c.h2T_d.rearrange("(k p) t -> p k t", p=128)
        tiles = [(0, 0, NCTX)] + [(NCTX + 256 * j, NCTX, T) for j in range(S // 256)]
        ntile = int(os.environ.get("NFF", len(tiles)))
        it = 0
        for ti, (t0, s0, s1) in enumerate(tiles[:ntile]):
            b = ti % NB
            lo = t0 - 1
            hi = t0 + 257
            c0 = 0
            if lo < s0:
                kb.op("pool", lambda e: e.memset(h2[b][:, :, 0:1], 0.0), writes=[d_h2[b]])
                lo, c0 = s0, 1
            c1 = 258
            if hi > s1:
                kb.op("pool", lambda e: e.memset(h2[b][:, :, 257:258], 0.0), writes=[d_h2[b]])
                hi, c1 = s1, 257
            kb.dma("sp", h2[b][:, :, c0:c1], hTv[:, :, lo:hi], reads=[c.d_h2T_d], writes=[d_h2[b]])
            for m in range(22):
                pb = it % 2
                it += 1
                for (ps, dps, off) in ((psA[pb], d_pA[pb], 0), (psG[pb], d_pG[pb], DFF)):
                    for kt in range(8):
                        kb.op("pe", lambda e: e.matmul(ps[:, 0:258], lhsT=wup[:, kt, off + m * 128:off + (m + 1) * 128],
                                                       rhs=h2[b][:, kt, :], start=(kt == 0), stop=(kt == 7)),
                              reads=[d_w, d_h2[b]], writes=[dps])
                for (ps, dps, dst, ddst, ch) in ((psA[pb], d_pA[pb], ca[pb], d_ca[pb], m),
                                                 (psG[pb], d_pG[pb], cg[pb], d_cg[pb], 22 + m)):
                    kb.op("act", lambda e: e.activation(out=dst[:], in_=ps[:, 1:257], func=AF.Identity,
                                                        scale=cw[:, ch, 1:2], bias=cb[:, ch:ch + 1]),
                          reads=[dps, d_cw], writes=[ddst])
                    kb.op("dve", lambda e: e.scalar_tensor_tensor(out=dst[:], in0=ps[:, 0:256], scalar=cw[:, ch, 0:1],
                                                                  in1=dst[:], op0=ALU.mult, op1=ALU.add),
                          reads=[dps, d_cw], writes=[ddst])
                    kb.op("dve", lambda e: e.scalar_tensor_tensor(out=dst[:], in0=ps[:, 2:258], scalar=cw[:, ch, 2:3],
                                                                  in1=dst[:], op0=ALU.mult, op1=ALU.add),
                          reads=[dps, d_cw], writes=[ddst])
                kb.op("act", lambda e: e.activation(out=cg[pb][:], in_=cg[pb][:], func=AF.Silu),
                      reads=[d_cg[pb]], writes=[d_cg[pb]])
                kb.op("pool", lambda e: e.tensor_tensor(out=hid[b][:, m, :], in0=cg[pb][:], in1=ca[pb][:], op=ALU.mult),
                      reads=[d_cg[pb], d_ca[pb]], writes=[d_hid[b]])
            sfx = "c" if t0 < NCTX else "l"
            for s in range(2):
                i = (t0 + s * 128) // 128
                rows = slice(i * 128, (i + 1) * 128)
                xb = (2 * ti + s) % NB
                kb.dma("sp", xt[xb][:], c.xres[rows, :], reads=[c.d_xres[i]], writes=[d_xt[xb]])
                for j in range(2):
                    for m in range(22):
                        kb.op("pe", lambda e: e.matmul(pso[j][:, :], lhsT=hid[b][:, m, s * 128:(s + 1) * 128],
                                                       rhs=wdn[:, m, j * 512:(j + 1) * 512], start=(m == 0), stop=(m == 21)),
                              reads=[d_hid[b], d_w], writes=[d_po[j]])
                    kb.op("dve", lambda e: e.tensor_tensor(out=tmp[:, j * 512:(j + 1) * 512], in0=pso[j][:, :],
                                                           in1=G2[sfx][:, j * 512:(j + 1) * 512], op=ALU.mult),
                          reads=[d_po[j], d_bc], writes=[d_tmp])
                kb.op("dve", lambda e: e.tensor_tensor(out=xt[xb][:], in0=xt[xb][:], in1=tmp[:], op=ALU.add),
                      reads=[d_xt[xb], d_tmp], writes=[d_xt[xb]])
                kb.dma("sp", c.xres[rows, :], xt[xb][:], reads=[d_xt[xb]], writes=[c.d_xres[i]])
        kb.barrier()


def zoh(c, kb, mk, lr, li, ls, F, tag):
    d = Dep()
    names = ["dt", "t", "mag", "a16", "s", "cc", "c2", "s2", "sc", "are", "aim", "den", "am1", "fre", "fim", "u1", "u2"]
    t = {n: mk(f"z{tag}_{n}", [128, F]) for n in names}

    def tt(out, a, b, op):
        kb.op("dve", lambda e: e.tensor_tensor(out=t[out][:], in0=t[a][:] if isinstance(a, str) else a,
                                               in1=t[b][:] if isinstance(b, str) else b, op=op), reads=[d], writes=[d])
    kb.op("act", lambda e: e.activation(out=t["dt"][:], in_=ls, func=AF.Exp), reads=[d], writes=[d])
    tt("t", lr, "dt", ALU.mult)
    kb.op("act", lambda e: e.activation(out=t["mag"][:], in_=t["t"][:], func=AF.Exp), reads=[d], writes=[d])
    kb.op("dve", lambda e: e.scalar_tensor_tensor(out=t["a16"][:], in0=li, scalar=1.0 / 16.0, in1=t["dt"][:],
                                                  op0=ALU.mult, op1=ALU.mult), reads=[d], writes=[d])
    kb.op("act", lambda e: e.activation(out=t["s"][:], in_=t["a16"][:], func=AF.Sin), reads=[d], writes=[d])
    kb.op("act", lambda e: e.activation(out=t["cc"][:], in_=t["a16"][:], func=AF.Sin, bias=c.halfpi),
          reads=[d, c.d_const], writes=[d])
    for _ in range(4):
        tt("c2", "cc", "cc", ALU.mult)
        tt("s2", "s", "s", ALU.mult)
        tt("sc", "s", "cc", ALU.mult)
        tt("cc", "c2", "s2", ALU.subtract)
        tt("s", "sc", "sc", ALU.add)
    tt("are", "mag", "cc", ALU.mult)
    tt("aim", "mag", "s", ALU.mult)
    tt("u1", lr, lr, ALU.mult)
    tt("u2", li, li, ALU.mult)
    tt("den", "u1", "u2", ALU.add)
    kb.op("dve", lambda e: e.reciprocal(out=t["den"][:], in_=t["den"][:]), reads=[d], writes=[d])
    kb.op("dve", lambda e: e.tensor_scalar(out=t["am1"][:], in0=t["are"][:], scalar1=-1.0, scalar2=None, op0=ALU.add),
          reads=[d], writes=[d])
    tt("u1", "am1", lr, ALU.mult)
    tt("u2", "aim", li, ALU.mult)
    tt("u1", "u1", "u2", ALU.add)
    tt("fre", "u1", "den", ALU.mult)
    tt("u1", "aim", lr, ALU.mult)
    tt("u2", "am1", li, ALU.mult)
    tt("u1", "u1", "u2", ALU.subtract)
    tt("fim", "u1", "den", ALU.mult)
    return t["are"], t["aim"], t["fre"], t["fim"], d


def phaseS(c, L, dbg):
    import os
    nc, kb = c.nc, c.kb
    NK = 13
    segs = [(0, NCTX)] + [(NCTX + 512 * j, 512) for j in range(S // 512)]
    with ExitStack() as ps_:
        def tsb(name, shape, dt=F32):
            return ps_.enter_context(nc.sbuf_tensor(f"{name}_{L}", list(shape), dt))

        def tps(name, shape, dt=F32):
            return ps_.enter_context(nc.psum_tensor(f"{name}_{L}", list(shape), dt))
        z = tsb("zs", [128, T], BF16)
        d_z = Dep()
        c.d_zT_d = Dep()
        P2r, P2i, P2n = tsb("P2r", [128, NK, 64]), tsb("P2i", [128, NK, 64]), tsb("P2n", [128, NK, 64])
        BT = {}
        for par_ in range(8):
            for kind_ in ("st", "sw"):
                BT[kind_ + str(par_)] = tsb("BT" + kind_ + str(par_), [128, 8, 128], BF16)
        Cpad = tsb("Cpad", [128, 8, 1152], BF16)
        with ExitStack() as pp_:
            def psb(name, shape, dt=F32):
                return pp_.enter_context(nc.sbuf_tensor(f"{name}_{L}", list(shape), dt))
            d_in = Dep()
            lrN, liN, lsN = psb("lrN", [128, 64]), psb("liN", [128, 64]), psb("lsN", [128, 64])
            lrQ, liQ, lsQ = psb("lrQ", [128, 512]), psb("liQ", [128, 512]), psb("lsQ", [128, 512])
            for tl, src in ((lrN, c.lamN_re), (liN, c.lamN_im), (lsN, c.lsN)):
                kb.dma("sp", tl[:], src[L], writes=[d_in])
            for tl, src in ((lrQ, c.lamQ_re), (liQ, c.lamQ_im), (lsQ, c.lsQ)):
                kb.dma("sp", tl[:], src[L], writes=[d_in])
            BQr, BQi = psb("BQr", [128, 512]), psb("BQi", [128, 512])
            kb.dma("sp", BQr[:], c.BQ_re[L], writes=[d_in])
            kb.dma("sp", BQi[:], c.BQ_im[L], writes=[d_in])
            CN = psb("CN", [128, 8, 8, 16])
            kb.dma("sp", CN[:], c.CN[L], writes=[d_in])
            kb.barrier()
            areN, aimN, _, _, dN = zoh(c, kb, psb, lrN[:], liN[:], lsN[:], 64, "N")
            _, _, freQ, fimQ, dQ = zoh(c, kb, psb, lrQ[:], liQ[:], lsQ[:], 512, "Q")
            d_P2 = Dep()
            t1, t2 = psb("pw1", [128, 64]), psb("pw2", [128, 64])
            kb.op("dve", lambda e: e.tensor_copy(out=P2r[:, 0, :], in_=areN[:]), reads=[dN], writes=[d_P2])
            kb.op("dve", lambda e: e.tensor_copy(out=P2i[:, 0, :], in_=aimN[:]), reads=[dN], writes=[d_P2])
            for k in range(1, NK):
                kb.op("dve", lambda e: e.tensor_tensor(out=t1[:], in0=P2r[:, k - 1, :], in1=P2r[:, k - 1, :], op=ALU.mult),
                      reads=[d_P2], writes=[d_P2])
                kb.op("dve", lambda e: e.tensor_tensor(out=t2[:], in0=P2i[:, k - 1, :], in1=P2i[:, k - 1, :], op=ALU.mult),
                      reads=[d_P2], writes=[d_P2])
                kb.op("dve", lambda e: e.tensor_tensor(out=P2r[:, k, :], in0=t1[:], in1=t2[:], op=ALU.subtract),
                      reads=[d_P2], writes=[d_P2])
                kb.op("dve", lambda e: e.tensor_tensor(out=t1[:], in0=P2r[:, k - 1, :], in1=P2i[:, k - 1, :], op=ALU.mult),
                      reads=[d_P2], writes=[d_P2])
                kb.op("dve", lambda e: e.tensor_tensor(out=P2i[:, k, :], in0=t1[:], in1=t1[:], op=ALU.add),
                      reads=[d_P2], writes=[d_P2])
            kb.op("dve", lambda e: e.tensor_scalar(out=P2n[:], in0=P2i[:], scalar1=-1.0, scalar2=None, op0=ALU.mult),
                  reads=[d_P2], writes=[d_P2])
            bre, bim, u1, u2 = psb("bre", [128, 512]), psb("bim", [128, 512]), psb("bu1", [128, 512]), psb("bu2", [128, 512])
            d_b = Dep()

            def tq(out, a, b, op):
                kb.op("dve", lambda e: e.tensor_tensor(out=out[:], in0=a[:], in1=b[:], op=op), reads=[dQ, d_in, d_b],
                      writes=[d_b])
            tq(u1, freQ, BQr, ALU.mult)
            tq(u2, fimQ, BQi, ALU.mult)
            tq(bre, u1, u2, ALU.subtract)
            tq(u1, freQ, BQi, ALU.mult)
            tq(u2, fimQ, BQr, ALU.mult)
            tq(bim, u1, u2, ALU.add)
            d_BT = Dep()
            for par in range(8):
                msk = c.cst[:, 8 + par:9 + par]
                par = str(par)
                bre3 = bre[:].rearrange("p (a n) -> p a n", a=8)
                bim3 = bim[:].rearrange("p (a n) -> p a n", a=8)
                kb.op("dve", lambda e: e.tensor_scalar(out=BT["st" + par][:, :, 0:64], in0=bre3, scalar1=msk, scalar2=None,
                                                       op0=ALU.mult), reads=[d_b, c.d_const], writes=[d_BT])
                kb.op("dve", lambda e: e.tensor_scalar(out=BT["st" + par][:, :, 64:128], in0=bim3, scalar1=msk,
                                                       scalar2=None, op0=ALU.mult), reads=[d_b, c.d_const], writes=[d_BT])
                kb.op("dve", lambda e: e.tensor_scalar(out=BT["sw" + par][:, :, 0:64], in0=bim3, scalar1=msk, scalar2=-1.0,
                                                       op0=ALU.mult, op1=ALU.mult), reads=[d_b, c.d_const], writes=[d_BT])
                kb.op("dve", lambda e: e.tensor_scalar(out=BT["sw" + par][:, :, 64:128], in0=bre3, scalar1=msk,
                                                       scalar2=None, op0=ALU.mult), reads=[d_b, c.d_const], writes=[d_BT])
            d_C = Dep()
            kb.op("pool", lambda e: e.memset(Cpad[:], 0.0), writes=[d_C])
            kb.op("dve", lambda e: e.tensor_scalar(
                out=Cpad[:].rearrange("p a (g x) -> p a g x", x=144)[:, :, :, 0:16], in0=CN[:], scalar1=c.sgnC,
                scalar2=None, op0=ALU.mult), reads=[d_in, d_C, c.d_const], writes=[d_C])
            kb.barrier()
        dQv = tsb("dQv", [128, 4])
        d_dq = Dep()
        kb.dma("sp", dQv[:], c.dQ[L], writes=[d_dq])
        uT = tsb("uTs", [128, T], BF16)
        d_u = Dep()
        XA = [tsb(f"XA{i}", [128, T]) for i in range(2)]
        XS = [tsb(f"XS{i}", [128, T]) for i in range(2)]
        d_X = [Dep(), Dep()]
        T1 = tsb("T1s", [128, T])
        d_T1, d_T2 = Dep(), Dep()
        d_XS = [Dep(), Dep()]
        dummy = tsb("dummy", [128, 2])
        Hb = [tsb("Hb0", [128, T], BF16)] * 2
        d_Hb = [Dep()] * 2
        Yacc = tsb("Yacc", [128, T])
        d_Y = Dep()
        px = [tps(f"px{i}", [128, 512]) for i in range(2)]
        pxs = [tps(f"pxs{i}", [128, 512]) for i in range(2)]
        d_px = [Dep(), Dep()]
        d_pxs = [Dep(), Dep()]
        py = [tps(f"py{i}", [128, 512]) for i in range(2)]
        d_py = [Dep(), Dep()]
        uTv = c.uT_d.rearrange("(k p) t -> p k t", p=128)
        nck = int(os.environ.get("NCK", 4))
        ngd = int(os.environ.get("NGD", 16))
        it = 0
        for ck in range(nck):
            kb.dma("sp", uT[:], uTv[:, ck, :], reads=[c.d_uT_d], writes=[d_u])
            kb.op("dve", lambda e: e.tensor_scalar(out=Yacc[:], in0=uT[:], scalar1=dQv[:, ck:ck + 1], scalar2=None,
                                                   op0=ALU.mult), reads=[d_u, d_dq], writes=[d_Y])
            for gd in range(ngd):
                dr, g8 = gd // 8, gd % 8
                dc = dr * 4 + ck
                col = dr * 32 + ck * 8 + g8
                par = str(g8)
                pr = slice(0, 128)
                cur = 0
                for si, (t0, n) in enumerate(segs):
                    pb = it % 2
                    it += 1
                    if dr == 0:
                        o0 = t0
                    else:
                        o0 = t0 - NCTX if t0 >= NCTX else S
                    for (ps, dps, kind, dst) in ((px[pb], d_px[pb], "st", XA[cur]), (pxs[pb], d_pxs[pb], "sw", XS[cur])):
                        kb.op("pe", lambda e: e.matmul(ps[:, 0:n], lhsT=BT[kind + par][pr, dc, :], rhs=uT[pr, t0:t0 + n],
                                                       start=True, stop=True), reads=[d_BT, d_u], writes=[dps])
                        kb.op("act", lambda e: e.activation(out=dst[:, o0:o0 + n], in_=ps[:, 0:n], func=AF.Copy),
                              reads=[dps], writes=[d_X[cur]])
                for k in range(NK):
                    s = 1 << k
                    nxt = 1 - cur
                    ar, ai, an = P2r[:, k, col:col + 1], P2i[:, k, col:col + 1], P2n[:, k, col:col + 1]
                    if dr == 0:
                        lo, hi, keep = slice(0, T - s), slice(s, T), slice(0, s)
                    else:
                        lo, hi, keep = slice(s, T), slice(0, T - s), slice(T - s, T)
                    kb.op("dve", lambda e: e.scalar_tensor_tensor(out=XA[nxt][:, hi], in0=XA[cur][:, lo], scalar=ar,
                                                                  in1=XA[cur][:, hi], op0=ALU.mult, op1=ALU.add),
                          reads=[d_X[cur], d_P2], writes=[d_X[nxt]])
                    kb.op("dve", lambda e: e.scalar_tensor_tensor(out=XA[nxt][:, hi], in0=XS[cur][:, lo], scalar=ai,
                                                                  in1=XA[nxt][:, hi], op0=ALU.mult, op1=ALU.add),
                          reads=[d_X[cur], d_P2], writes=[d_X[nxt]])
                    kb.op("act", lambda e: e.activation(out=T1[:, hi], in_=XS[cur][:, lo], func=AF.Copy, scale=ar),
                          reads=[d_X[cur], d_P2], writes=[d_T1])
                    kb.op("act", lambda e: e.activation(out=XS[nxt][:, hi], in_=XA[cur][:, lo], func=AF.Copy, scale=an),
                          reads=[d_X[cur], d_P2], writes=[d_XS[nxt]])
                    kb.op("pool", lambda e: e.tensor_tensor(out=T1[:, hi], in0=T1[:, hi], in1=XS[nxt][:, hi], op=ALU.add),
                          reads=[d_XS[nxt]], writes=[d_T1])
                    kb.op("pool", lambda e: e.tensor_tensor(out=XS[nxt][:, hi], in0=XS[cur][:, hi], in1=T1[:, hi], op=ALU.add),
                          reads=[d_T1, d_X[cur]], writes=[d_XS[nxt]])
                    kb.op("act", lambda e: e.activation(out=XA[nxt][:, keep], in_=XA[cur][:, keep], func=AF.Copy),
                          reads=[d_X[cur]], writes=[d_X[nxt]])
                    kb.op("pool", lambda e: e.tensor_copy(out=XS[nxt][:, keep], in_=XS[cur][:, keep]),
                          reads=[d_X[cur]], writes=[d_XS[nxt]])
                    kb.op("pool", lambda e: e.memset(dummy[:], 0.0), reads=[d_XS[nxt]], writes=[d_X[nxt]])
                    cur = nxt
                hb = gd % 2
                kb.op("act", lambda e: e.activation(out=Hb[hb][:], in_=XA[cur][:], func=AF.Copy),
                      reads=[d_X[cur]], writes=[d_Hb[hb]])
                for si, (t0, n) in enumerate(segs):
                    o0 = t0 if dr == 0 else (t0 - NCTX if t0 >= NCTX else S)
                    pb = it % 2
                    it += 1
                    kb.op("pe", lambda e: e.matmul(py[pb][:, 0:n], lhsT=Cpad[:, dc, g8 * 128:(g8 + 1) * 128],
                                                   rhs=Hb[hb][:, o0:o0 + n], start=True, stop=True),
                          reads=[d_C, d_Hb[hb]], writes=[d_py[pb]])
                    kb.op("dve", lambda e: e.tensor_tensor(out=Yacc[:, t0:t0 + n], in0=py[pb][:, 0:n],
                                                           in1=Yacc[:, t0:t0 + n], op=ALU.add),
                          reads=[d_py[pb]], writes=[d_Y])
            g1 = XA[0]
            kb.op("dve", lambda e: e.tensor_tensor(out=g1[:], in0=Yacc[:], in1=Yacc[:], op=ALU.mult),
                  reads=[d_Y, d_X[0], d_X[1]], writes=[d_X[0]])
            kb.op("dve", lambda e: e.tensor_scalar(out=g1[:], in0=g1[:], scalar1=0.044715, scalar2=1.0, op0=ALU.mult,
                                                   op1=ALU.add), reads=[d_X[0]], writes=[d_X[0]])
            kb.op("dve", lambda e: e.tensor_tensor(out=g1[:], in0=g1[:], in1=Yacc[:], op=ALU.mult),
                  reads=[d_X[0], d_Y], writes=[d_X[0]])
            kb.op("act", lambda e: e.activation(out=g1[:], in_=g1[:], func=AF.Sigmoid, scale=1.5957691216057308),
                  reads=[d_X[0]], writes=[d_X[0]])
            kb.op("dve", lambda e: e.tensor_tensor(out=z[:], in0=g1[:], in1=Yacc[:], op=ALU.mult),
                  reads=[d_X[0], d_Y], writes=[d_z])
            kb.dma("sp", c.zT_d[ck * 128:(ck + 1) * 128, :], z[:], reads=[d_z], writes=[c.d_zT_d])
        if "pS1" in dbg:
            o = nc.dram_tensor("dbg_z", [512, T], BF16, kind="ExternalOutput").ap()
            tk = kb.dma("sp", o, c.zT_d[:, :], reads=[c.d_zT_d])
            kb.wait("sp", tk)
            kb.barrier()
            return
        kb.barrier()
    phaseS2(c, L, dbg, segs)


def phaseS2(c, L, dbg, segs):
    nc, kb = c.nc, c.kb
    with ExitStack() as ps_:
        def tsb(name, shape, dt=F32):
            return ps_.enter_context(nc.sbuf_tensor(f"{name}_{L}", list(shape), dt))

        def tps(name, shape, dt=F32):
            return ps_.enter_context(nc.psum_tensor(f"{name}_{L}", list(shape), dt))
        wglu = tsb("wglu", [128, 4, 512], BF16)
        d_w = Dep()
        wsrc = c.w_glu[L].rearrange("(kt p) n -> p kt n", p=128)
        for kt in range(4):
            kb.dma("pool", wglu[:, kt, :], wsrc[:, kt, :], writes=[d_w])
        bgl, gs = tsb("bgl", [128, 4]), tsb("gs", [128, 4])
        kb.dma("sp", bgl[:], c.bgluT[L], writes=[d_w])
        kb.dma("sp", gs[:], c.ongT[L], writes=[d_w])
        zt = [tsb(f"zt{i}", [128, 4, 512], BF16) for i in range(2)]
        d_zt = [Dep(), Dep()]
        pg = [tps(f"pg{i}", [128, 512]) for i in range(2)]
        d_pg = [Dep(), Dep()]
        pss = tps("pss", [128, 512])
        d_pss = Dep()
        sg = [tsb(f"sg{i}", [128, 512]) for i in range(2)]
        d_sg = [Dep(), Dep()]
        o = tsb("og", [128, 4, 512])
        d_o = Dep()
        sq = tsb("sqg", [128, 4, 512])
        d_sq = Dep()
        rs = tsb("rsg", [128, 512])
        d_rs = Dep()
        sb_ = [tsb(f"sbg{i}", [128, 4, 512], BF16) for i in range(2)]
        d_sb = [Dep(), Dep()]
        zv = c.zT_d.rearrange("(k p) t -> p k t", p=128)
        sv = c.sT_d.rearrange("(k p) t -> p k t", p=128)
        it = 0
        for si, (t0, n) in enumerate(segs):
            b = si % 2
            kb.dma("sp", zt[b][:, :, 0:n], zv[:, :, t0:t0 + n], reads=[c.d_zT_d], writes=[d_zt[b]])
            for m in range(4):
                pb = it % 2
                it += 1
                for k in range(4):
                    kb.op("pe", lambda e: e.matmul(pg[pb][:, 0:n], lhsT=wglu[:, k, m * 128:(m + 1) * 128], rhs=zt[b][:, k, 0:n],
                                                   start=(k == 0), stop=(k == 3)), reads=[d_w, d_zt[b]], writes=[d_pg[pb]])
                kb.op("act", lambda e: e.activation(out=sg[pb][:, 0:n], in_=pg[pb][:, 0:n], func=AF.Sigmoid,
                                                    bias=bgl[:, m:m + 1]), reads=[d_pg[pb], d_w], writes=[d_sg[pb]])
                kb.op("dve", lambda e: e.tensor_tensor(out=o[:, m, 0:n], in0=sg[pb][:, 0:n], in1=zt[b][:, m, 0:n], op=ALU.mult),
                      reads=[d_sg[pb], d_zt[b]], writes=[d_o])
                kb.op("pool", lambda e: e.tensor_tensor(out=sq[:, m, 0:n], in0=o[:, m, 0:n], in1=o[:, m, 0:n], op=ALU.mult),
                      reads=[d_o], writes=[d_sq])
            for m in range(4):
                kb.op("pe", lambda e: e.matmul(pss[:, 0:n], lhsT=c.onesf[:], rhs=sq[:, m, 0:n], start=(m == 0), stop=(m == 3)),
                      reads=[d_sq, c.d_const], writes=[d_pss])
            kb.op("act", lambda e: e.activation(out=rs[:, 0:n], in_=pss[:, 0:n], func=AF.Identity, scale=1.0 / 512.0,
                                                bias=c.cst[:, 1:2]), reads=[d_pss, c.d_const], writes=[d_rs])
            kb.op("act", lambda e: e.activation(out=rs[:, 0:n], in_=rs[:, 0:n], func=AF.Sqrt), reads=[d_rs], writes=[d_rs])
            kb.op("dve", lambda e: e.reciprocal(out=rs[:, 0:n], in_=rs[:, 0:n]), reads=[d_rs], writes=[d_rs])
            for m in range(4):
                kb.op("dve", lambda e: e.scalar_tensor_tensor(out=sb_[b][:, m, 0:n], in0=o[:, m, 0:n], scalar=gs[:, m:m + 1],
                                                              in1=rs[:, 0:n], op0=ALU.mult, op1=ALU.mult),
                      reads=[d_o, d_rs, d_w], writes=[d_sb[b]])
            kb.dma("sp", sv[:, :, t0:t0 + n], sb_[b][:, :, 0:n], reads=[d_sb[b]], writes=[c.d_sT_d])
        kb.barrier()


def kernel(**inputs):
    nc, c = build(DEPTH)
    sh = prep_shared(inputs)
    in_maps = []
    for i in range(8):
        m = dict(sh)
        m.update(prep_core(inputs, i % 4))
        in_maps.append(m)
    res = run_bass_kernel_spmd(nc, in_maps, core_ids=list(range(8)))
    out = np.stack([np.asarray(res.results[b]["out"]) for b in range(4)], axis=0)
    return np.ascontiguousarray(out, dtype=np.float32)
```

```python
import numpy as np
from contextlib import ExitStack
import concourse.bass as bass
import concourse.mybir as mybir
from concourse.bass_utils import run_bass_kernel_spmd

F32 = mybir.dt.float32
BF16 = mybir.dt.bfloat16
AF = mybir.ActivationFunctionType
ALU = mybir.AluOpType
AX = mybir.AxisListType

D = 1024
S = 4096
NCTX = 256
T = S + NCTX
NT = T // 128
DEPTH = 4
DFF = 2816
EPS = 1e-6
HEAD_PERM = [0, 4, 1, 5, 2, 6, 3, 7]


class Dep:
    __slots__ = ("w", "r")

    def __init__(self):
        self.w = None
        self.r = {}


class KB:
    def __init__(self, nc, es, ndma=24):
        self.nc = nc
        self.es = es
        self.E = {"pe": nc.tensor, "act": nc.scalar, "dve": nc.vector, "pool": nc.gpsimd, "sp": nc.sync}
        self.sems = []
        self.esem = {}
        self.cnt = {}
        self.seen = {e: {} for e in self.E}
        self.pe_sems = set()
        for e in self.E:
            self._newsem(e)
        self.dq = []
        for i in range(ndma):
            self.sems.append(es.enter_context(nc.semaphore(f"dq{i}")))
            self.dq.append([len(self.sems) - 1, 0])
        self.dn = 0
        self.nins = 0

    def _newsem(self, e):
        self.sems.append(self.es.enter_context(self.nc.semaphore(f"e{e}{len(self.sems)}")))
        self.esem[e] = len(self.sems) - 1
        if e == "pe":
            self.pe_sems.add(len(self.sems) - 1)
        self.cnt[e] = 0

    def wait(self, e, tk):
        si, v = tk
        if self.seen[e].get(si, 0) >= v:
            return
        if e == "pe" and si in self.pe_sems:
            return
        self.E[e].wait_ge(self.sems[si], v)
        self.seen[e][si] = v
        self.nins += 1

    def _deps(self, e, reads, writes):
        for d in reads:
            if d.w is not None:
                self.wait(e, d.w)
        for d in writes:
            if d.w is not None:
                self.wait(e, d.w)
            for si, v in d.r.items():
                self.wait(e, (si, v))

    def _mark(self, tk, reads, writes):
        for d in reads:
            if d.r.get(tk[0], 0) < tk[1]:
                d.r[tk[0]] = tk[1]
        for d in writes:
            d.w = tk
            d.r = {}

    def op(self, e, fn, reads=(), writes=()):
        self._deps(e, reads, writes)
        if self.cnt[e] >= 30000:
            self._newsem(e)
        inst = fn(self.E[e])
        self.cnt[e] += 1
        inst.then_inc(self.sems[self.esem[e]], 1)
        tk = (self.esem[e], self.cnt[e])
        self._mark(tk, reads, writes)
        self.nins += 1
        return tk

    def barrier(self):
        for e in self.E:
            for f in self.E:
                if f != e and self.cnt[f] > 0:
                    self.wait(e, (self.esem[f], self.cnt[f]))
            for si, v in self.dq:
                if v > 0:
                    self.wait(e, (si, v))

    def dma(self, q, out, in_, reads=(), writes=()):
        k = self.dn
        self.dn = (self.dn + 1) % len(self.dq)
        si, v = self.dq[k]
        if v > 0:
            self.wait(q, (si, v))
        self._deps(q, reads, writes)
        inst = self.E[q].dma_start(out=out, in_=in_)
        v += 16
        self.dq[k][1] = v
        inst.then_inc(self.sems[si], 16)
        tk = (si, v)
        self._mark(tk, reads, writes)
        self.nins += 1
        return tk


class Ctx:
    pass


def build(depth=DEPTH, dbg=None):
    dbg = dbg or set()
    nc = bass.Bass("TRN2", target_bir_lowering=False)
    es = ExitStack()
    c = Ctx()
    c.nc = nc
    c.es = es

    def din(name, shape, dt=F32):
        return nc.dram_tensor(name, list(shape), dt, kind="ExternalInput").ap()

    def dout(name, shape, dt=F32):
        return nc.dram_tensor(name, list(shape), dt, kind="ExternalOutput").ap()

    def dscr(name, shape, dt=F32):
        return nc.dram_tensor(name, list(shape), dt, kind="Internal").ap()

    c.xin = din("xin", [T, D])
    c.cT = din("cT", [128, 8, 2])
    c.w_ada = din("w_ada", [DEPTH, D, 6 * D])
    c.b_ada = din("b_ada", [DEPTH, 6 * D])
    c.n1g = din("n1g", [DEPTH, D])
    c.n2g = din("n2g", [DEPTH, D])
    c.w_in = din("w_in", [DEPTH, D, 1280])
    c.qg = din("qg", [DEPTH, 64])
    c.kg = din("kg", [DEPTH, 64])
    c.cos_t = din("cos_t", [T, 64])
    c.sin_t = din("sin_t", [T, 64])
    c.ident = din("ident", [128, 128])
    c.xres = dscr("xres", [T, D])
    c.modv = dscr("modv", [2, 6 * D])
    c.uT_d = dscr("uT_d", [512, T], BF16)
    c.aT_d = dscr("aT_d", [512, T], BF16)
    c.ong = din("ong", [DEPTH, D])
    c.sT_d = dscr("sT_d", [512, T], BF16)
    c.zT_d = dscr("zT_d", [512, T], BF16)
    c.lamN_re = din("lamN_re", [DEPTH, 128, 32])
    c.lamN_im = din("lamN_im", [DEPTH, 128, 32])
    c.lsN = din("lsN", [DEPTH, 128, 32])
    c.lamQ_re = din("lamQ_re", [DEPTH, 128, 512])
    c.lamQ_im = din("lamQ_im", [DEPTH, 128, 512])
    c.lsQ = din("lsQ", [DEPTH, 128, 512])
    c.BQ_re = din("BQ_re", [DEPTH, 128, 512])
    c.BQ_im = din("BQ_im", [DEPTH, 128, 512])
    c.CNr = din("CNr", [DEPTH, 128, 8, 4, 16])
    c.CNi = din("CNi", [DEPTH, 128, 8, 4, 16])
    c.dQ = din("dQ", [DEPTH, 128, 4])
    c.w_glu = din("w_glu", [DEPTH, 512, 512])
    c.bgluT = din("bgluT", [DEPTH, 128, 4])
    c.ongT = din("ongT", [DEPTH, 128, 4])
    c.consts = din("consts", [128, 16])
    c.h2T_d = dscr("h2T_d", [D, T], BF16)
    c.w_out = din("w_out", [DEPTH, D, D])
    c.w_up = din("w_up", [DEPTH, D, 2 * DFF])
    c.w_down = din("w_down", [DEPTH, DFF, D])
    c.cwT = din("cwT", [DEPTH, 128, 44, 3])
    c.cbT = din("cbT", [DEPTH, 128, 44])
    c.out = dout("out", [S, D])
    c.dbg = {}

    with es:
        kb = KB(nc, es)
        c.kb = kb

        def sb(name, shape, dt=F32):
            return es.enter_context(nc.sbuf_tensor(name, list(shape), dt))

        c.sb = sb
        c.identf = sb("identf", [128, 128], F32)
        c.identb = sb("identb", [128, 128], BF16)
        c.d_ident = Dep()
        kb.dma("sp", c.identf[:], c.ident[:], writes=[c.d_ident])
        kb.op("dve", lambda e: e.tensor_copy(out=c.identb[:], in_=c.identf[:]), reads=[c.d_ident], writes=[c.d_ident])
        c.cst = sb("cst", [128, 16], F32)
        c.onesf = sb("onesf", [128, 128], F32)
        c.d_const = Dep()
        kb.dma("sp", c.cst[:], c.consts[:], writes=[c.d_const])
        kb.op("pool", lambda e: e.memset(c.onesf[:], 1.0), writes=[c.d_const])
        c.halfpi, c.mE, c.mO, c.sgnC = c.cst[:, 0:1], c.cst[:, 2:3], c.cst[:, 3:4], c.cst[:, 4:5]
        c.sT = sb("sT", [128, 8, 2], F32)
        c.d_sT = Dep()
        kb.dma("sp", c.sT[:], c.cT[:], writes=[c.d_sT])
        kb.op("act", lambda e: e.activation(out=c.sT[:], in_=c.sT[:], func=AF.Silu), reads=[c.d_sT], writes=[c.d_sT])
        c.d_xres = [Dep() for _ in range(NT)]
        for i in range(NT):
            kb.dma("sp", c.xres[i * 128:(i + 1) * 128, :], c.xin[i * 128:(i + 1) * 128, :], writes=[c.d_xres[i]])

        for L in range(depth):
            layer(c, L, dbg)

        tks = []
        for i in range(2, NT):
            tks.append(kb.dma("sp", c.out[(i - 2) * 128:(i - 1) * 128, :], c.xres[i * 128:(i + 1) * 128, :],
                              reads=[c.d_xres[i]]))
        for tk in tks:
            kb.wait("sp", tk)
        for e in ("pe", "act", "dve", "pool"):
            if kb.cnt[e] > 0:
                kb.wait("sp", (kb.esem[e], kb.cnt[e]))
    return nc, c


def layer(c, L, dbg):
    nc, kb = c.nc, c.kb
    c.d_qT, c.d_kT, c.d_V, c.d_uT_d = Dep(), Dep(), Dep(), Dep()
    c.d_aT_d, c.d_sT_d, c.d_h2T_d = Dep(), Dep(), Dep()
    phase0(c, L, None, None)
    if "p0" in dbg:
        o = nc.dram_tensor("dbg_modv", [2, 6 * D], F32, kind="ExternalOutput").ap()
        tk = kb.dma("sp", o, c.modv[:, :], reads=[c.d_modv])
        kb.wait("sp", tk)
        return
    with ExitStack() as ls:
        def lsb(name, shape, dt=F32):
            return ls.enter_context(nc.sbuf_tensor(f"{name}_{L}", list(shape), dt))
        c.qT = lsb("qT", [128, 4, T], BF16)
        c.kT = lsb("kT", [128, T], BF16)
        c.Vaug = lsb("Vaug", [128, NT, 2, 66], BF16)
        kb.op("pool", lambda e: e.memset(c.Vaug[:, :, :, 64:65], 1.0), writes=[c.d_V])
        phaseA(c, L, dbg)
        if "pA1" in dbg:
            return
        if "pA" in dbg:
            dbg_dump(c, "qT", c.qT, [c.d_qT], BF16)
            dbg_dump(c, "kT", c.kT, [c.d_kT], BF16)
            dbg_dump(c, "Vaug", c.Vaug, [c.d_V], BF16)
            return
        phaseB(c, L, dbg)
        if "pB" in dbg:
            o = nc.dram_tensor("dbg_aT", [512, T], BF16, kind="ExternalOutput").ap()
            tk = kb.dma("sp", o, c.aT_d[:, :], reads=[c.d_aT_d])
            kb.wait("sp", tk)
            return
    if "noS" in dbg:
        with ExitStack() as zs:
            z = zs.enter_context(nc.sbuf_tensor(f"zt_{L}", [128, 4, T], BF16))
            dz = Dep()
            kb.op("pool", lambda e: e.memset(z[:], 0.0), writes=[dz])
            kb.dma("sp", c.sT_d.rearrange("(k p) t -> p k t", p=128), z[:], reads=[dz], writes=[c.d_sT_d])
            kb.barrier()
    else:
        phaseS(c, L, dbg)
        if "pS1" in dbg:
            return
        if "pS" in dbg:
            o = nc.dram_tensor("dbg_sT", [512, T], BF16, kind="ExternalOutput").ap()
            tk = kb.dma("sp", o, c.sT_d[:, :], reads=[c.d_sT_d])
            kb.wait("sp", tk)
            return
    phaseC1(c, L, dbg)
    phaseC2(c, L, dbg)


def dbg_dump(c, name, tile, deps, dt=F32):
    o = c.nc.dram_tensor("dbg_" + name, list(tile.shape), dt, kind="ExternalOutput").ap()
    tk = c.kb.dma("sp", o, tile[:], reads=deps)
    c.kb.wait("sp", tk)


def phase0(c, L, lsb, lps):
    nc, kb = c.nc, c.kb
    with ExitStack() as ps_:
        def tsb(name, shape, dt=F32):
            return ps_.enter_context(nc.sbuf_tensor(f"{name}_{L}", list(shape), dt))
        wa = [tsb(f"wa{i}", [128, 8, 512]) for i in range(2)]
        d_wa = [Dep(), Dep()]
        mod = tsb("mod", [2, 6 * D])
        d_mod = Dep()
        bada = tsb("bada", [2, 6 * D])
        d_bada = Dep()
        ng = tsb("ng", [2, 2, D])
        d_ng = Dep()
        vec = tsb("vec", [2, 6 * D])
        d_vec = Dep()
        ps = ps_.enter_context(nc.psum_tensor(f"p0ps_{L}", [128, 512], F32))
        d_ps = Dep()
        kb.dma("sp", bada[:], c.b_ada[L:L + 1, :].to_broadcast([2, 6 * D]), writes=[d_bada])
        kb.dma("sp", ng[:, 0, :], c.n1g[L:L + 1, :].to_broadcast([2, D]), writes=[d_ng])
        kb.dma("sp", ng[:, 1, :], c.n2g[L:L + 1, :].to_broadcast([2, D]), writes=[d_ng])
        wsrc = c.w_ada[L].rearrange("(kt p) n -> p kt n", p=128)
        for j in range(12):
            b = j % 2
            kb.dma("sp", wa[b][:], wsrc[:, :, j * 512:(j + 1) * 512], writes=[d_wa[b]])
            for kt in range(8):
                kb.op("pe", lambda e: e.matmul(ps[0:2, :], lhsT=c.sT[:, kt, :], rhs=wa[b][:, kt, :],
                                               start=(kt == 0), stop=(kt == 7)),
                      reads=[d_wa[b], c.d_sT], writes=[d_ps])
            kb.op("dve", lambda e: e.tensor_tensor(out=mod[:, j * 512:(j + 1) * 512], in0=ps[0:2, :],
                                                   in1=bada[:, j * 512:(j + 1) * 512], op=ALU.add),
                  reads=[d_ps, d_bada], writes=[d_mod])
        def sl(k):
            return slice(k * D, (k + 1) * D)
        kb.op("dve", lambda e: e.scalar_tensor_tensor(out=vec[:, sl(0)], in0=mod[:, sl(1)], scalar=1.0, in1=ng[:, 0, :],
                                                      op0=ALU.add, op1=ALU.mult), reads=[d_mod, d_ng], writes=[d_vec])
        kb.op("dve", lambda e: e.tensor_copy(out=vec[:, sl(1)], in_=mod[:, sl(0)]), reads=[d_mod], writes=[d_vec])
        kb.op("dve", lambda e: e.tensor_copy(out=vec[:, sl(2)], in_=mod[:, sl(2)]), reads=[d_mod], writes=[d_vec])
        kb.op("dve", lambda e: e.scalar_tensor_tensor(out=vec[:, sl(3)], in0=mod[:, sl(4)], scalar=1.0, in1=ng[:, 1, :],
                                                      op0=ALU.add, op1=ALU.mult), reads=[d_mod, d_ng], writes=[d_vec])
        kb.op("dve", lambda e: e.tensor_copy(out=vec[:, sl(4)], in_=mod[:, sl(3)]), reads=[d_mod], writes=[d_vec])
        kb.op("dve", lambda e: e.tensor_copy(out=vec[:, sl(5)], in_=mod[:, sl(5)]), reads=[d_mod], writes=[d_vec])
        if not hasattr(c, "d_modv"):
            c.d_modv = Dep()
        kb.dma("sp", c.modv[:, :], vec[:], reads=[d_vec], writes=[c.d_modv])
        kb.barrier()


def load_bc(c, tile, row, k, dep):
    c.kb.dma("sp", tile[:], c.modv[row:row + 1, k * D:(k + 1) * D].to_broadcast([128, D]),
             reads=[c.d_modv], writes=[dep])


def rstd_from_ss(c, ss, n, d_ss, shape_cols=1):
    kb = c.kb
    kb.op("dve", lambda e: e.tensor_scalar(out=ss, in0=ss, scalar1=1.0 / n, scalar2=EPS, op0=ALU.mult, op1=ALU.add),
          reads=[d_ss], writes=[d_ss])
    kb.op("act", lambda e: e.activation(out=ss, in_=ss, func=AF.Sqrt), reads=[d_ss], writes=[d_ss])
    kb.op("dve", lambda e: e.reciprocal(out=ss, in_=ss), reads=[d_ss], writes=[d_ss])


def phaseA(c, L, dbg):
    nc, kb = c.nc, c.kb
    with ExitStack() as ps_:
        def tsb(name, shape, dt=F32):
            return ps_.enter_context(nc.sbuf_tensor(f"{name}_{L}", list(shape), dt))

        def tps(name, shape, dt=F32):
            return ps_.enter_context(nc.psum_tensor(f"{name}_{L}", list(shape), dt))
        win = tsb("win", [128, 8, 1280], BF16)
        d_win = Dep()
        wsrc = c.w_in[L].rearrange("(kt p) n -> p kt n", p=128)
        for kt in range(8):
            kb.dma("pool", win[:, kt, :], wsrc[:, kt, :], writes=[d_win])
        bc = {}
        d_bc = Dep()
        for nm, row, k in (("A1l", 0, 0), ("S1l", 0, 1), ("A1c", 1, 0), ("S1c", 1, 1)):
            bc[nm] = tsb(nm, [128, D])
            load_bc(c, bc[nm], row, k, d_bc)
        gq = tsb("gq", [128, 64])
        gk = tsb("gk", [128, 64])
        d_g = Dep()
        kb.dma("sp", gq[:], c.qg[L:L + 1, :].to_broadcast([128, 64]), writes=[d_g])
        kb.dma("sp", gk[:], c.kg[L:L + 1, :].to_broadcast([128, 64]), writes=[d_g])
        kb.op("dve", lambda e: e.tensor_scalar(out=gq[:], in0=gq[:], scalar1=0.125, scalar2=None, op0=ALU.mult),
              reads=[d_g], writes=[d_g])

        NB = 2
        xt = [tsb(f"xt{i}", [128, D]) for i in range(NB)]
        d_xt = [Dep() for _ in range(NB)]
        cs = [tsb(f"cs{i}", [128, 2, 64]) for i in range(NB)]
        d_cs = [Dep() for _ in range(NB)]
        junk = tsb("junk", [128, D])
        d_junk = Dep()
        ss = tsb("ss", [128, 1])
        d_ss = Dep()
        hf = tsb("hf", [128, D])
        d_hf = Dep()
        hb = tsb("hb", [128, D], BF16)
        d_hb = Dep()
        hT = tsb("hT", [128, 8, 128], BF16)
        d_hT = Dep()
        tp = tps("tp", [128, 8, 128], BF16)
        d_tp = Dep()
        pp = [tps(f"pp{i}", [128, 512]) for i in range(3)]
        d_pp = [Dep() for _ in range(3)]
        qf = tsb("qf", [128, 640])
        d_qf = Dep()
        sq = tsb("sq", [128, 640])
        d_sq = Dep()
        ssq = tsb("ssq", [128, 10])
        d_ssq = Dep()
        ra = tsb("ra", [128, 640])
        d_ra = Dep()
        rb = tsb("rb", [128, 640])
        d_rb = Dep()
        qb = tsb("qb", [128, 640], BF16)
        d_qb = Dep()
        ub = tsb("ub", [128, 512], BF16)
        d_ub = Dep()
        tp2 = tps("tp2", [128, 8, 128], BF16)
        d_tp2 = Dep()
        uTs = [tsb(f"uTs{i}", [128, 4, 128], BF16) for i in range(2)]
        d_uTs = [Dep(), Dep()]

        import os
        CUT = int(os.environ.get("CUT", "99"))
        for i in range(int(os.environ.get("NTA", NT))):
            b = i % NB
            isctx = i < 2
            A1 = bc["A1c"] if isctx else bc["A1l"]
            S1 = bc["S1c"] if isctx else bc["S1l"]
            rows = slice(i * 128, (i + 1) * 128)
            kb.dma("sp", xt[b][:], c.xres[rows, :], reads=[c.d_xres[i]], writes=[d_xt[b]])
            kb.dma("sp", cs[b][:, 0, :], c.cos_t[rows, :], writes=[d_cs[b]])
            kb.dma("sp", cs[b][:, 1, :], c.sin_t[rows, :], writes=[d_cs[b]])
            if CUT < 1:
                continue
            kb.op("act", lambda e: e.activation(out=junk[:], in_=xt[b][:], func=AF.Square, accum_out=ss[:]),
                  reads=[d_xt[b]], writes=[d_junk, d_ss])
            rstd_from_ss(c, ss[:], D, d_ss)
            kb.op("dve", lambda e: e.scalar_tensor_tensor(out=hf[:], in0=xt[b][:], scalar=ss[:], in1=A1[:],
                                                          op0=ALU.mult, op1=ALU.mult),
                  reads=[d_xt[b], d_ss, d_bc], writes=[d_hf])
            kb.op("dve", lambda e: e.tensor_tensor(out=hb[:], in0=hf[:], in1=S1[:], op=ALU.add),
                  reads=[d_hf, d_bc], writes=[d_hb])
            if "pA1" in dbg and i == 0:
                dbg_dump(c, "ss", ss, [d_ss]); dbg_dump(c, "hf", hf, [d_hf]); dbg_dump(c, "hb", hb, [d_hb], BF16)
                dbg_dump(c, "xt", xt[b], [d_xt[b]]); dbg_dump(c, "A1", A1, [d_bc])
            if CUT < 2:
                continue
            for kt in range(8):
                kb.op("pe", lambda e: e.transpose(tp[:, kt, :], hb[:, kt * 128:(kt + 1) * 128], c.identb[:]),
                      reads=[d_hb, c.d_ident], writes=[d_tp])
            kb.op("act", lambda e: e.activation(out=hT[:], in_=tp[:], func=AF.Copy), reads=[d_tp], writes=[d_hT])
            SUB = int(os.environ.get("SUB", "9"))
            if SUB < 1:
                continue
            for j, (c0, c1) in enumerate(((0, 512), (512, 1024), (1024, 1280))):
                for kt in range(8):
                    kb.op("pe", lambda e: e.matmul(pp[j][:, 0:c1 - c0], lhsT=hT[:, kt, :], rhs=win[:, kt, c0:c1],
                                                   start=(kt == 0), stop=(kt == 7)),
                          reads=[d_hT, d_win], writes=[d_pp[j]])
            if SUB < 2:
                continue
            kb.op("act", lambda e: e.activation(out=qf[:, 0:512], in_=pp[0][:, :], func=AF.Copy),
                  reads=[d_pp[0]], writes=[d_qf])
            kb.op("act", lambda e: e.activation(out=qf[:, 512:640], in_=pp[1][:, 0:128], func=AF.Copy),
                  reads=[d_pp[1]], writes=[d_qf])
            if SUB < 3:
                continue
            if os.environ.get("NOV") is None:
                kb.op("act", lambda e: e.activation(out=c.Vaug[:, i, :, 0:64],
                                                    in_=pp[1][:, 128:256].rearrange("p (h d) -> p h d", h=2), func=AF.Copy),
                      reads=[d_pp[1]], writes=[c.d_V])
            if os.environ.get("NOU") is not None:
                continue
            kb.op("act", lambda e: e.activation(out=ub[:, 0:256], in_=pp[1][:, 256:512], func=AF.Copy), reads=[d_pp[1]], writes=[d_ub])
            kb.op("act", lambda e: e.activation(out=ub[:, 256:512], in_=pp[2][:, 0:256], func=AF.Copy), reads=[d_pp[2]], writes=[d_ub])
            if CUT < 3:
                continue
            kb.op("dve", lambda e: e.tensor_tensor(out=sq[:], in0=qf[:], in1=qf[:], op=ALU.mult), reads=[d_qf], writes=[d_sq])
            kb.op("dve", lambda e: e.tensor_reduce(out=ssq[:], in_=sq[:].rearrange("p (h d) -> p h d", d=64), axis=AX.X,
                                                   op=ALU.add), reads=[d_sq], writes=[d_ssq])
            rstd_from_ss(c, ssq[:], 64, d_ssq)
            q3 = qf[:].rearrange("p (h d) -> p h d", d=64)
            ra3 = ra[:].rearrange("p (h d) -> p h d", d=64)
            kb.op("dve", lambda e: e.tensor_tensor(out=ra3, in0=q3, in1=ssq[:].unsqueeze(2).to_broadcast([128, 10, 64]),
                                                   op=ALU.mult), reads=[d_qf, d_ssq], writes=[d_ra])
            kb.op("dve", lambda e: e.tensor_tensor(out=ra3[:, 0:8, :], in0=ra3[:, 0:8, :],
                                                   in1=gq[:].unsqueeze(1).to_broadcast([128, 8, 64]), op=ALU.mult),
                  reads=[d_ra, d_g], writes=[d_ra])
            kb.op("dve", lambda e: e.tensor_tensor(out=ra3[:, 8:10, :], in0=ra3[:, 8:10, :],
                                                   in1=gk[:].unsqueeze(1).to_broadcast([128, 2, 64]), op=ALU.mult),
                  reads=[d_ra, d_g], writes=[d_ra])
            if CUT < 4:
                continue
            cosb = cs[b][:, 0, :].unsqueeze(1).to_broadcast([128, 10, 64])
            t6 = ra[:].rearrange("p (h a x f) -> p h a x f", h=10, a=2, x=2)
            r6 = rb[:].rearrange("p (h a x f) -> p h a x f", h=10, a=2, x=2)
            sin4 = cs[b][:, 1, :].rearrange("p (a x f) -> p a x f", a=2, x=2)
            for half in range(2):
                kb.op("pool", lambda e: e.tensor_tensor(
                    out=r6[:, :, :, half, :], in0=t6[:, :, :, 1 - half, :],
                    in1=sin4[:, :, half, :].unsqueeze(1).to_broadcast([128, 10, 2, 16]), op=ALU.mult),
                    reads=[d_ra, d_cs[b]], writes=[d_rb])
            kb.op("dve", lambda e: e.tensor_tensor(out=ra3, in0=ra3, in1=cosb, op=ALU.mult),
                  reads=[d_ra, d_cs[b]], writes=[d_ra])
            kb.op("dve", lambda e: e.tensor_tensor(out=qb[:], in0=ra[:], in1=rb[:], op=ALU.add),
                  reads=[d_ra, d_rb], writes=[d_qb])
            if CUT < 5:
                continue
            for j in range(5):
                kb.op("pe", lambda e: e.transpose(tp2[:, j, :], qb[:, j * 128:(j + 1) * 128], c.identb[:]),
                      reads=[d_qb, c.d_ident], writes=[d_tp2])
            kb.op("act", lambda e: e.activation(out=c.qT[:, :, rows], in_=tp2[:, 0:4, :], func=AF.Copy),
                  reads=[d_tp2], writes=[c.d_qT])
            kb.op("act", lambda e: e.activation(out=c.kT[:, rows], in_=tp2[:, 4, :], func=AF.Copy),
                  reads=[d_tp2], writes=[c.d_kT])
            for j in range(4):
                kb.op("pe", lambda e: e.transpose(tp[:, j, :], ub[:, j * 128:(j + 1) * 128], c.identb[:]),
                      reads=[d_ub, c.d_ident], writes=[d_tp])
            ut = uTs[i % 2]
            kb.op("dve", lambda e: e.tensor_copy(out=ut[:], in_=tp[:, 0:4, :]), reads=[d_tp], writes=[d_uTs[i % 2]])
            kb.dma("sp", c.uT_d.rearrange("(k p) t -> p k t", p=128)[:, :, rows], ut[:], reads=[d_uTs[i % 2]],
                   writes=[c.d_uT_d])
        kb.barrier()


def rope_tables():
    rows = S // 64
    r, col = np.meshgrid(np.arange(rows), np.arange(64), indexing="ij")
    pos = np.stack([r.reshape(-1), col.reshape(-1)], axis=-1).astype(np.float32)
    freqs = (np.float32(10000.0) ** (-np.arange(16, dtype=np.float32) / np.float32(16))).astype(np.float32)
    ang = (pos[:, :, None] * freqs).astype(np.float32)
    cs, sn = np.cos(ang).astype(np.float32), np.sin(ang).astype(np.float32)
    cos64 = np.ones((T, 2, 2, 16), np.float32)
    sin64 = np.zeros((T, 2, 2, 16), np.float32)
    cos64[NCTX:, :, 0, :] = cs
    cos64[NCTX:, :, 1, :] = cs
    sin64[NCTX:, :, 0, :] = -sn
    sin64[NCTX:, :, 1, :] = sn
    return cos64.reshape(T, 64), sin64.reshape(T, 64)


def prep_shared(inp):
    f = lambda a: np.ascontiguousarray(np.asarray(a, dtype=np.float32))
    sh = {}
    sh["w_ada"] = f(inp["w_ada"])
    sh["b_ada"] = f(inp["b_ada"])
    sh["n1g"] = f(inp["norm1_g"])
    sh["n2g"] = f(inp["norm2_g"])
    w_in = f(inp["w_in"]).copy()
    qcols = np.concatenate([np.arange(h * 64, (h + 1) * 64) for h in HEAD_PERM])
    w_in[:, :, 0:512] = w_in[:, :, qcols]
    sh["w_in"] = w_in
    ong = f(inp["out_norm_g"]).copy()
    ong[:, 0:512] = ong[:, qcols]
    sh["ong"] = ong
    w_out = f(inp["w_out"]).copy()
    w_out[:, 0:512, :] = w_out[:, qcols, :]
    sh["w_out"] = w_out
    sh["w_up"] = f(inp["w_up"])
    sh["w_down"] = f(inp["w_down"])
    cw = f(inp["conv_w"])
    sh["cwT"] = np.ascontiguousarray(cw.reshape(DEPTH, 3, 44, 128).transpose(0, 3, 2, 1))
    sh["cbT"] = np.ascontiguousarray(f(inp["conv_b"]).reshape(DEPTH, 44, 128).transpose(0, 2, 1))
    lre, lim, lst = f(inp["ssm_lambda_re"]), f(inp["ssm_lambda_im"]), f(inp["ssm_log_step"])
    def layN(a):
        t = a.reshape(DEPTH, 2, 16, 2, 64).transpose(0, 3, 4, 1, 2)
        return np.ascontiguousarray(t.reshape(DEPTH, 128, 32))
    sh["lamN_re"], sh["lamN_im"] = layN(lre), layN(lim)
    sh["lsN"] = layN(np.broadcast_to(lst[..., None], (DEPTH, 2, 32, 64)))
    def layQ(a):
        t = a.reshape(DEPTH, 2, 4, 8, 1, 64)
        t = np.broadcast_to(t, (DEPTH, 2, 4, 8, 16, 64)).transpose(0, 3, 4, 1, 2, 5)
        return np.ascontiguousarray(t.reshape(DEPTH, 128, 512))
    sh["lamQ_re"], sh["lamQ_im"] = layQ(lre), layQ(lim)
    sh["lsQ"] = layQ(np.broadcast_to(lst[..., None], (DEPTH, 2, 32, 64)))
    def layB(a):
        t = a.reshape(DEPTH, 2, 4, 8, 64, 16).transpose(0, 3, 5, 1, 2, 4)
        return np.ascontiguousarray(t.reshape(DEPTH, 128, 512))
    sh["BQ_re"], sh["BQ_im"] = layB(f(inp["ssm_b_re"])), layB(f(inp["ssm_b_im"]))
    def layC(a):
        t = a.reshape(DEPTH, 2, 4, 4, 2, 16, 64).transpose(0, 4, 6, 1, 2, 3, 5)
        return np.ascontiguousarray(t.reshape(DEPTH, 128, 8, 4, 16))
    sh["CNr"], sh["CNi"] = layC(f(inp["ssm_c_re"])), layC(f(inp["ssm_c_im"]))
    sh["dQ"] = np.ascontiguousarray(f(inp["ssm_d"]).reshape(DEPTH, 4, 128).transpose(0, 2, 1))
    sh["w_glu"] = f(inp["w_glu"])
    sh["bgluT"] = np.ascontiguousarray(f(inp["b_glu"]).reshape(DEPTH, 4, 128).transpose(0, 2, 1))
    sh["ongT"] = np.ascontiguousarray(f(inp["out_norm_g"])[:, 512:].reshape(DEPTH, 4, 128).transpose(0, 2, 1))
    cst = np.zeros((128, 16), np.float32)
    for g8_ in range(8):
        cst[:, 8 + g8_] = (np.arange(128) // 16 == g8_)
    cst[:, 0] = np.pi / 2
    cst[:, 1] = EPS
    p = np.arange(128)
    cst[:, 2] = ((p // 16) % 2 == 0)
    cst[:, 3] = ((p // 16) % 2 == 1)
    cst[:, 4] = np.where(p < 64, 1.0, -1.0)
    sh["consts"] = cst
    sh["qg"] = f(inp["q_norm_g"])
    sh["kg"] = f(inp["k_norm_g"])
    cos64, sin64 = rope_tables()
    sh["cos_t"] = cos64
    sh["sin_t"] = sin64
    sh["ident"] = np.eye(128, dtype=np.float32)
    return sh


def prep_core(inp, b):
    m = {}
    m["xin"] = np.ascontiguousarray(np.concatenate([np.asarray(inp["ctx"][b]), np.asarray(inp["x"][b])], axis=0),
                                    dtype=np.float32)
    cv = np.stack([np.asarray(inp["c"][b]), np.asarray(inp["c_ctx"])], axis=0).astype(np.float32)
    m["cT"] = np.ascontiguousarray(cv.reshape(2, 8, 128).transpose(2, 1, 0))
    return m


def phaseB(c, L, dbg):
    import os
    nc, kb = c.nc, c.kb
    with ExitStack() as ps_:
        def tsb(name, shape, dt=F32):
            return ps_.enter_context(nc.sbuf_tensor(f"{name}_{L}", list(shape), dt))

        def tps(name, shape, dt=F32):
            return ps_.enter_context(nc.psum_tensor(f"{name}_{L}", list(shape), dt))
        st = [tps(f"st{i}", [128, 512]) for i in range(2)]
        d_st = [Dep(), Dep()]
        oacc = [tps(f"oa{i}", [128, 512]) for i in range(4)]
        d_oa = [Dep() for _ in range(4)]
        tpb = tps("tpb", [128, 8, 128], BF16)
        d_tpb = Dep()
        pt = [tsb(f"pt{i}", [128, 512], BF16) for i in range(3)]
        d_pt = [Dep() for _ in range(3)]
        osb = [tsb(f"osb{i}", [128, 66]) for i in range(2)]
        d_osb = [Dep(), Dep()]
        rl = [tsb(f"rl{i}", [128, 1]) for i in range(2)]
        d_rl = [Dep(), Dep()]
        attn = tsb("attn", [128, 4, 512])
        d_attn = [Dep() for _ in range(4)]
        ga = tsb("ga", [128, 512])
        d_ga = Dep()
        kb.dma("sp", ga[:], c.ong[L:L + 1, 0:512].to_broadcast([128, 512]), writes=[d_ga])
        junk = tsb("junkb", [128, 512])
        d_junk = Dep()
        ss = tsb("ssb", [128, 1])
        d_ss = Dep()
        ab = tsb("ab", [128, 512], BF16)
        d_ab = Dep()
        aTs = [tsb(f"aTs{i}", [128, 4, 128], BF16) for i in range(2)]
        d_aTs = [Dep(), Dep()]
        aTv = c.aT_d.rearrange("(k p) t -> p k t", p=128)

        qblocks = [(0, 256, [0, 1])] + [(256 + 512 * j, 512, list(range(NT))) for j in range(8)]
        nqb = int(os.environ.get("NQB", len(qblocks)))
        ev = 0
        for (q0, nq, kts) in qblocks[:nqb]:
            nsub = nq // 128
            for hp in range(8):
                cidx, half = hp // 2, hp % 2
                pr = slice(64 * half, 64 * half + 64)

                def qk(n):
                    kt = kts[n]
                    kb.op("pe", lambda e: e.matmul(st[n % 2][:, 0:nq], lhsT=c.kT[pr, kt * 128:(kt + 1) * 128],
                                                   rhs=c.qT[pr, cidx, q0:q0 + nq], start=True, stop=True),
                          reads=[c.d_kT, c.d_qT], writes=[d_st[n % 2]])
                qk(0)
                for n, kt in enumerate(kts):
                    if n + 1 < len(kts):
                        qk(n + 1)
                    pb = pt[n % 3]
                    kb.op("act", lambda e: e.activation(out=pb[:, 0:nq], in_=st[n % 2][:, 0:nq], func=AF.Exp),
                          reads=[d_st[n % 2]], writes=[d_pt[n % 3]])
                    for s in range(nsub):
                        kb.op("pe", lambda e: e.matmul(oacc[s][:, 0:65], lhsT=pb[:, s * 128:(s + 1) * 128],
                                                       rhs=c.Vaug[:, kt, half, 0:65], start=(n == 0),
                                                       stop=(n == len(kts) - 1)),
                              reads=[d_pt[n % 3], c.d_V], writes=[d_oa[s]])
                for s in range(nsub):
                    o = osb[ev % 2]
                    r = rl[ev % 2]
                    kb.op("act", lambda e: e.activation(out=o[:, 0:65], in_=oacc[s][:, 0:65], func=AF.Copy),
                          reads=[d_oa[s]], writes=[d_osb[ev % 2]])
                    kb.op("dve", lambda e: e.reciprocal(out=r[:], in_=o[:, 64:65]), reads=[d_osb[ev % 2]],
                          writes=[d_rl[ev % 2]])
                    kb.op("dve", lambda e: e.tensor_scalar(out=attn[:, s, hp * 64:(hp + 1) * 64], in0=o[:, 0:64],
                                                           scalar1=r[:], scalar2=None, op0=ALU.mult),
                          reads=[d_osb[ev % 2], d_rl[ev % 2]], writes=[d_attn[s]])
                    ev += 1
            for s in range(nsub):
                kb.op("act", lambda e: e.activation(out=junk[:], in_=attn[:, s, :], func=AF.Square, accum_out=ss[:]),
                      reads=[d_attn[s]], writes=[d_junk, d_ss])
                rstd_from_ss(c, ss[:], 512, d_ss)
                kb.op("dve", lambda e: e.scalar_tensor_tensor(out=ab[:], in0=attn[:, s, :], scalar=ss[:], in1=ga[:],
                                                              op0=ALU.mult, op1=ALU.mult),
                      reads=[d_attn[s], d_ss, d_ga], writes=[d_ab])
                for j in range(4):
                    kb.op("pe", lambda e: e.transpose(tpb[:, j, :], ab[:, j * 128:(j + 1) * 128], c.identb[:]),
                          reads=[d_ab, c.d_ident], writes=[d_tpb])
                at = aTs[s % 2]
                kb.op("act", lambda e: e.activation(out=at[:], in_=tpb[:, 0:4, :], func=AF.Copy),
                      reads=[d_tpb], writes=[d_aTs[s % 2]])
                t0 = q0 + s * 128
                kb.dma("sp", aTv[:, :, t0:t0 + 128], at[:], reads=[d_aTs[s % 2]], writes=[c.d_aT_d])
        kb.barrier()


def phaseC1(c, L, dbg):
    nc, kb = c.nc, c.kb
    with ExitStack() as ps_:
        def tsb(name, shape, dt=F32):
            return ps_.enter_context(nc.sbuf_tensor(f"{name}_{L}", list(shape), dt))

        def tps(name, shape, dt=F32):
            return ps_.enter_context(nc.psum_tensor(f"{name}_{L}", list(shape), dt))
        wout = tsb("wout", [128, 8, D], BF16)
        d_w = Dep()
        wsrc = c.w_out[L].rearrange("(kt p) n -> p kt n", p=128)
        for kt in range(8):
            kb.dma("pool", wout[:, kt, :], wsrc[:, kt, :], writes=[d_w])
        bc = {}
        d_bc = Dep()
        for nm, row, k in (("G1l", 0, 2), ("A2l", 0, 3), ("S2l", 0, 4), ("G1c", 1, 2), ("A2c", 1, 3), ("S2c", 1, 4)):
            bc[nm] = tsb(nm, [128, D])
            load_bc(c, bc[nm], row, k, d_bc)
        NB = 2
        xt = [tsb(f"xc{i}", [128, D]) for i in range(NB)]
        d_xt = [Dep() for _ in range(NB)]
        asT = [tsb(f"asT{i}", [128, 8, 128], BF16) for i in range(NB)]
        d_as = [Dep() for _ in range(NB)]
        pp = [tps(f"pc{i}", [128, 512]) for i in range(2)]
        d_pp = [Dep(), Dep()]
        tp = tps("tpc", [128, 8, 128], BF16)
        d_tp = Dep()
        tmp = tsb("tmpc", [128, D])
        d_tmp = Dep()
        x1 = [tsb(f"x1{i}", [128, D]) for i in range(NB)]
        d_x1 = [Dep() for _ in range(NB)]
        junk = tsb("junkc", [128, D])
        d_junk = Dep()
        ss = tsb("ssc", [128, 1])
        d_ss = Dep()
        hf = tsb("hfc", [128, D])
        d_hf = Dep()
        hb = tsb("hbc", [128, D], BF16)
        d_hb = Dep()
        hT = [tsb(f"hTc{i}", [128, 8, 128], BF16) for i in range(NB)]
        d_hT = [Dep() for _ in range(NB)]
        aTv = c.aT_d.rearrange("(k p) t -> p k t", p=128)
        sTv = c.sT_d.rearrange("(k p) t -> p k t", p=128)
        hTv = c.h2T_d.rearrange("(k p) t -> p k t", p=128)
        for i in range(NT):
            b = i % NB
            sfx = "c" if i < 2 else "l"
            rows = slice(i * 128, (i + 1) * 128)
            kb.dma("sp", xt[b][:], c.xres[rows, :], reads=[c.d_xres[i]], writes=[d_xt[b]])
            kb.dma("sp", asT[b][:, 0:4, :], aTv[:, :, rows], reads=[c.d_aT_d], writes=[d_as[b]])
            kb.dma("sp", asT[b][:, 4:8, :], sTv[:, :, rows], reads=[c.d_sT_d], writes=[d_as[b]])
            for j in range(2):
                for kt in range(8):
                    kb.op("pe", lambda e: e.matmul(pp[j][:, :], lhsT=asT[b][:, kt, :], rhs=wout[:, kt, j * 512:(j + 1) * 512],
                                                   start=(kt == 0), stop=(kt == 7)),
                          reads=[d_as[b], d_w], writes=[d_pp[j]])
                kb.op("dve", lambda e: e.tensor_tensor(out=tmp[:, j * 512:(j + 1) * 512], in0=pp[j][:, :],
                                                       in1=bc["G1" + sfx][:, j * 512:(j + 1) * 512], op=ALU.mult),
                      reads=[d_pp[j], d_bc], writes=[d_tmp])
            kb.op("dve", lambda e: e.tensor_tensor(out=x1[b][:], in0=xt[b][:], in1=tmp[:], op=ALU.add),
                  reads=[d_xt[b], d_tmp], writes=[d_x1[b]])
            kb.dma("sp", c.xres[rows, :], x1[b][:], reads=[d_x1[b]], writes=[c.d_xres[i]])
            kb.op("act", lambda e: e.activation(out=junk[:], in_=x1[b][:], func=AF.Square, accum_out=ss[:]),
                  reads=[d_x1[b]], writes=[d_junk, d_ss])
            rstd_from_ss(c, ss[:], D, d_ss)
            kb.op("dve", lambda e: e.scalar_tensor_tensor(out=hf[:], in0=x1[b][:], scalar=ss[:], in1=bc["A2" + sfx][:],
                                                          op0=ALU.mult, op1=ALU.mult),
                  reads=[d_x1[b], d_ss, d_bc], writes=[d_hf])
            kb.op("dve", lambda e: e.tensor_tensor(out=hb[:], in0=hf[:], in1=bc["S2" + sfx][:], op=ALU.add),
                  reads=[d_hf, d_bc], writes=[d_hb])
            for kt in range(8):
                kb.op("pe", lambda e: e.transpose(tp[:, kt, :], hb[:, kt * 128:(kt + 1) * 128], c.identb[:]),
                      reads=[d_hb, c.d_ident], writes=[d_tp])
            kb.op("act", lambda e: e.activation(out=hT[b][:], in_=tp[:], func=AF.Copy), reads=[d_tp], writes=[d_hT[b]])
            kb.dma("sp", hTv[:, :, rows], hT[b][:], reads=[d_hT[b]], writes=[c.d_h2T_d])
        kb.barrier()


def phaseC2(c, L, dbg):
    import os
    nc, kb = c.nc, c.kb
    with ExitStack() as ps_:
        def tsb(name, shape, dt=F32):
            return ps_.enter_context(nc.sbuf_tensor(f"{name}_{L}", list(shape), dt))

        def tps(name, shape, dt=F32):
            return ps_.enter_context(nc.psum_tensor(f"{name}_{L}", list(shape), dt))
        wup = tsb("wup", [128, 8, 2 * DFF], BF16)
        wdn = tsb("wdn", [128, 22, D], BF16)
        d_w = Dep()
        usrc = c.w_up[L].rearrange("(kt p) n -> p kt n", p=128)
        for kt in range(8):
            for q in range(4):
                kb.dma("pool", wup[:, kt, q * 1408:(q + 1) * 1408], usrc[:, kt, q * 1408:(q + 1) * 1408], writes=[d_w])
        dsrc = c.w_down[L].rearrange("(kt p) n -> p kt n", p=128)
        for kt in range(22):
            kb.dma("pool", wdn[:, kt, :], dsrc[:, kt, :], writes=[d_w])
        cw = tsb("cw", [128, 44, 3])
        cb = tsb("cb", [128, 44])
        d_cw = Dep()
        kb.dma("sp", cw[:], c.cwT[L], writes=[d_cw])
        kb.dma("sp", cb[:], c.cbT[L], writes=[d_cw])
        G2 = {}
        d_bc = Dep()
        for nm, row in (("l", 0), ("c", 1)):
            G2[nm] = tsb("G2" + nm, [128, D])
            load_bc(c, G2[nm], row, 5, d_bc)
        NB = 2
        h2 = [tsb(f"h2t{i}", [128, 8, 258], BF16) for i in range(NB)]
        d_h2 = [Dep() for _ in range(NB)]
        psA = [tps(f"psA{i}", [128, 512]) for i in range(2)]
        psG = [tps(f"psG{i}", [128, 512]) for i in range(2)]
        d_pA = [Dep(), Dep()]
        d_pG = [Dep(), Dep()]
        pso = [tps(f"pso{i}", [128, 512]) for i in range(2)]
        d_po = [Dep(), Dep()]
        ca = [tsb(f"ca{i}", [128, 256]) for i in range(2)]
        cg = [tsb(f"cg{i}", [128, 256]) for i in range(2)]
        d_ca = [Dep(), Dep()]
        d_cg = [Dep(), Dep()]
        hid = [tsb(f"hid{i}", [128, 22, 256], BF16) for i in range(NB)]
        d_hid = [Dep() for _ in range(NB)]
        xt = [tsb(f"xf{i}", [128, D]) for i in range(NB)]
        d_xt = [Dep() for _ in range(NB)]
        tmp = tsb("tmpf", [128, D])
        d_tmp = Dep()
        hTv = c.h2T_d.rearrange("(k p) t -> p k t", p=128)
        tiles = [(0, 0, NCTX)] + [(NCTX + 256 * j, NCTX, T) for j in range(S // 256)]
        ntile = int(os.environ.get("NFF", len(tiles)))
        it = 0
        for ti, (t0, s0, s1) in enumerate(tiles[:ntile]):
            b = ti % NB
            lo = t0 - 1
            hi = t0 + 257
            c0 = 0
            if lo < s0:
                kb.op("pool", lambda e: e.memset(h2[b][:, :, 0:1], 0.0), writes=[d_h2[b]])
                lo, c0 = s0, 1
            c1 = 258
            if hi > s1:
                kb.op("pool", lambda e: e.memset(h2[b][:, :, 257:258], 0.0), writes=[d_h2[b]])
                hi, c1 = s1, 257
            kb.dma("sp", h2[b][:, :, c0:c1], hTv[:, :, lo:hi], reads=[c.d_h2T_d], writes=[d_h2[b]])
            for m in range(22):
                pb = it % 2
                it += 1
                for (ps, dps, off) in ((psA[pb], d_pA[pb], 0), (psG[pb], d_pG[pb], DFF)):
                    for kt in range(8):
                        kb.op("pe", lambda e: e.matmul(ps[:, 0:258], lhsT=wup[:, kt, off + m * 128:off + (m + 1) * 128],
                                                       rhs=h2[b][:, kt, :], start=(kt == 0), stop=(kt == 7)),
                              reads=[d_w, d_h2[b]], writes=[dps])
                for (ps, dps, dst, ddst, ch) in ((psA[pb], d_pA[pb], ca[pb], d_ca[pb], m),
                                                 (psG[pb], d_pG[pb], cg[pb], d_cg[pb], 22 + m)):
                    kb.op("act", lambda e: e.activation(out=dst[:], in_=ps[:, 1:257], func=AF.Identity,
                                                        scale=cw[:, ch, 1:2], bias=cb[:, ch:ch + 1]),
                          reads=[dps, d_cw], writes=[ddst])
                    kb.op("dve", lambda e: e.scalar_tensor_tensor(out=dst[:], in0=ps[:, 0:256], scalar=cw[:, ch, 0:1],
                                                                  in1=dst[:], op0=ALU.mult, op1=ALU.add),
                          reads=[dps, d_cw], writes=[ddst])
                    kb.op("dve", lambda e: e.scalar_tensor_tensor(out=dst[:], in0=ps[:, 2:258], scalar=cw[:, ch, 2:3],
                                                                  in1=dst[:], op0=ALU.mult, op1=ALU.add),
                          reads=[dps, d_cw], writes=[ddst])
                kb.op("act", lambda e: e.activation(out=cg[pb][:], in_=cg[pb][:], func=AF.Silu),
                      reads=[d_cg[pb]], writes=[d_cg[pb]])
                kb.op("pool", lambda e: e.tensor_tensor(out=hid[b][:, m, :], in0=cg[pb][:], in1=ca[pb][:], op=ALU.mult),
                      reads=[d_cg[pb], d_ca[pb]], writes=[d_hid[b]])
            sfx = "c" if t0 < NCTX else "l"
            for s in range(2):
                i = (t0 + s * 128) // 128
                rows = slice(i * 128, (i + 1) * 128)
                xb = (2 * ti + s) % NB
                kb.dma("sp", xt[xb][:], c.xres[rows, :], reads=[c.d_xres[i]], writes=[d_xt[xb]])
                for j in range(2):
                    for m in range(22):
                        kb.op("pe", lambda e: e.matmul(pso[j][:, :], lhsT=hid[b][:, m, s * 128:(s + 1) * 128],
                                                       rhs=wdn[:, m, j * 512:(j + 1) * 512], start=(m == 0), stop=(m == 21)),
                              reads=[d_hid[b], d_w], writes=[d_po[j]])
                    kb.op("dve", lambda e: e.tensor_tensor(out=tmp[:, j * 512:(j + 1) * 512], in0=pso[j][:, :],
                                                           in1=G2[sfx][:, j * 512:(j + 1) * 512], op=ALU.mult),
                          reads=[d_po[j], d_bc], writes=[d_tmp])
                kb.op("dve", lambda e: e.tensor_tensor(out=xt[xb][:], in0=xt[xb][:], in1=tmp[:], op=ALU.add),
                      reads=[d_xt[xb], d_tmp], writes=[d_xt[xb]])
                kb.dma("sp", c.xres[rows, :], xt[xb][:], reads=[d_xt[xb]], writes=[c.d_xres[i]])
        kb.barrier()


def zoh(c, kb, mk, lr, li, ls, F, tag):
    d = Dep()
    names = ["dt", "t", "mag", "a16", "s", "cc", "c2", "s2", "sc", "are", "aim", "den", "am1", "fre", "fim", "u1", "u2"]
    t = {n: mk(f"z{tag}_{n}", [128, F]) for n in names}

    def tt(out, a, b, op):
        kb.op("dve", lambda e: e.tensor_tensor(out=t[out][:], in0=t[a][:] if isinstance(a, str) else a,
                                               in1=t[b][:] if isinstance(b, str) else b, op=op), reads=[d], writes=[d])
    kb.op("act", lambda e: e.activation(out=t["dt"][:], in_=ls, func=AF.Exp), reads=[d], writes=[d])
    tt("t", lr, "dt", ALU.mult)
    kb.op("act", lambda e: e.activation(out=t["mag"][:], in_=t["t"][:], func=AF.Exp), reads=[d], writes=[d])
    kb.op("dve", lambda e: e.scalar_tensor_tensor(out=t["a16"][:], in0=li, scalar=1.0 / 16.0, in1=t["dt"][:],
                                                  op0=ALU.mult, op1=ALU.mult), reads=[d], writes=[d])
    kb.op("act", lambda e: e.activation(out=t["s"][:], in_=t["a16"][:], func=AF.Sin), reads=[d], writes=[d])
    kb.op("act", lambda e: e.activation(out=t["cc"][:], in_=t["a16"][:], func=AF.Sin, bias=c.halfpi),
          reads=[d, c.d_const], writes=[d])
    for _ in range(4):
        tt("c2", "cc", "cc", ALU.mult)
        tt("s2", "s", "s", ALU.mult)
        tt("sc", "s", "cc", ALU.mult)
        tt("cc", "c2", "s2", ALU.subtract)
        tt("s", "sc", "sc", ALU.add)
    tt("are", "mag", "cc", ALU.mult)
    tt("aim", "mag", "s", ALU.mult)
    tt("u1", lr, lr, ALU.mult)
    tt("u2", li, li, ALU.mult)
    tt("den", "u1", "u2", ALU.add)
    kb.op("dve", lambda e: e.reciprocal(out=t["den"][:], in_=t["den"][:]), reads=[d], writes=[d])
    kb.op("dve", lambda e: e.tensor_scalar(out=t["am1"][:], in0=t["are"][:], scalar1=-1.0, scalar2=None, op0=ALU.add),
          reads=[d], writes=[d])
    tt("u1", "am1", lr, ALU.mult)
    tt("u2", "aim", li, ALU.mult)
    tt("u1", "u1", "u2", ALU.add)
    tt("fre", "u1", "den", ALU.mult)
    tt("u1", "aim", lr, ALU.mult)
    tt("u2", "am1", li, ALU.mult)
    tt("u1", "u1", "u2", ALU.subtract)
    tt("fim", "u1", "den", ALU.mult)
    return t["are"], t["aim"], t["fre"], t["fim"], d


def phaseS(c, L, dbg):
    import os
    nc, kb = c.nc, c.kb
    NK = 13
    segs = [(0, NCTX)] + [(NCTX + 512 * j, 512) for j in range(S // 512)]
    with ExitStack() as ps_:
        def tsb(name, shape, dt=F32):
            return ps_.enter_context(nc.sbuf_tensor(f"{name}_{L}", list(shape), dt))

        def tps(name, shape, dt=F32):
            return ps_.enter_context(nc.psum_tensor(f"{name}_{L}", list(shape), dt))
        z = tsb("zs", [128, T], BF16)
        d_z = Dep()
        c.d_zT_d = Dep()
        P2r, P2i, P2n = tsb("P2r", [128, NK, 32]), tsb("P2i", [128, NK, 32]), tsb("P2n", [128, NK, 32])
        BT = {}
        for j_ in range(4):
            for kind_ in ("r", "i"):
                BT[kind_ + str(j_)] = tsb("BT" + kind_ + str(j_), [128, 8, 128], BF16)
        CpR = tsb("CpR", [128, 8, 640], BF16)
        CpI = tsb("CpI", [128, 8, 640], BF16)
        with ExitStack() as pp_:
            def psb(name, shape, dt=F32):
                return pp_.enter_context(nc.sbuf_tensor(f"{name}_{L}", list(shape), dt))
            d_in = Dep()
            lrN, liN, lsN = psb("lrN", [128, 32]), psb("liN", [128, 32]), psb("lsN", [128, 32])
            lrQ, liQ, lsQ = psb("lrQ", [128, 512]), psb("liQ", [128, 512]), psb("lsQ", [128, 512])
            for tl, src in ((lrN, c.lamN_re), (liN, c.lamN_im), (lsN, c.lsN)):
                kb.dma("sp", tl[:], src[L], writes=[d_in])
            for tl, src in ((lrQ, c.lamQ_re), (liQ, c.lamQ_im), (lsQ, c.lsQ)):
                kb.dma("sp", tl[:], src[L], writes=[d_in])
            BQr, BQi = psb("BQr", [128, 512]), psb("BQi", [128, 512])
            kb.dma("sp", BQr[:], c.BQ_re[L], writes=[d_in])
            kb.dma("sp", BQi[:], c.BQ_im[L], writes=[d_in])
            CNr, CNi = psb("CNr", [128, 8, 4, 16]), psb("CNi", [128, 8, 4, 16])
            kb.dma("sp", CNr[:], c.CNr[L], writes=[d_in])
            kb.dma("sp", CNi[:], c.CNi[L], writes=[d_in])
            kb.barrier()
            areN, aimN, _, _, dN = zoh(c, kb, psb, lrN[:], liN[:], lsN[:], 32, "N")
            _, _, freQ, fimQ, dQ = zoh(c, kb, psb, lrQ[:], liQ[:], lsQ[:], 512, "Q")
            d_P2 = Dep()
            t1, t2 = psb("pw1", [128, 32]), psb("pw2", [128, 32])
            kb.op("dve", lambda e: e.tensor_copy(out=P2r[:, 0, :], in_=areN[:]), reads=[dN], writes=[d_P2])
            kb.op("dve", lambda e: e.tensor_copy(out=P2i[:, 0, :], in_=aimN[:]), reads=[dN], writes=[d_P2])
            for k in range(1, NK):
                kb.op("dve", lambda e: e.tensor_tensor(out=t1[:], in0=P2r[:, k - 1, :], in1=P2r[:, k - 1, :], op=ALU.mult),
                      reads=[d_P2], writes=[d_P2])
                kb.op("dve", lambda e: e.tensor_tensor(out=t2[:], in0=P2i[:, k - 1, :], in1=P2i[:, k - 1, :], op=ALU.mult),
                      reads=[d_P2], writes=[d_P2])
                kb.op("dve", lambda e: e.tensor_tensor(out=P2r[:, k, :], in0=t1[:], in1=t2[:], op=ALU.subtract),
                      reads=[d_P2], writes=[d_P2])
                kb.op("dve", lambda e: e.tensor_tensor(out=t1[:], in0=P2r[:, k - 1, :], in1=P2i[:, k - 1, :], op=ALU.mult),
                      reads=[d_P2], writes=[d_P2])
                kb.op("dve", lambda e: e.tensor_tensor(out=P2i[:, k, :], in0=t1[:], in1=t1[:], op=ALU.add),
                      reads=[d_P2], writes=[d_P2])
            kb.op("dve", lambda e: e.tensor_scalar(out=P2n[:], in0=P2i[:], scalar1=-1.0, scalar2=None, op0=ALU.mult),
                  reads=[d_P2], writes=[d_P2])
            bre, bim, u1, u2 = psb("bre", [128, 512]), psb("bim", [128, 512]), psb("bu1", [128, 512]), psb("bu2", [128, 512])
            d_b = Dep()

            def tq(out, a, b, op):
                kb.op("dve", lambda e: e.tensor_tensor(out=out[:], in0=a[:], in1=b[:], op=op), reads=[dQ, d_in, d_b],
                      writes=[d_b])
            tq(u1, freQ, BQr, ALU.mult)
            tq(u2, fimQ, BQi, ALU.mult)
            tq(bre, u1, u2, ALU.subtract)
            tq(u1, freQ, BQi, ALU.mult)
            tq(u2, fimQ, BQr, ALU.mult)
            tq(bim, u1, u2, ALU.add)
            d_BT = Dep()
            bre3 = bre[:].rearrange("p (a n) -> p a n", a=8)
            bim3 = bim[:].rearrange("p (a n) -> p a n", a=8)
            for j in range(4):
                mA = c.cst[:, 8 + 2 * j:9 + 2 * j]
                mB = c.cst[:, 9 + 2 * j:10 + 2 * j]
                for kind, src in (("r", bre3), ("i", bim3)):
                    kb.op("dve", lambda e: e.tensor_scalar(out=BT[kind + str(j)][:, :, 0:64], in0=src, scalar1=mA, scalar2=None,
                                                           op0=ALU.mult), reads=[d_b, c.d_const], writes=[d_BT])
                    kb.op("dve", lambda e: e.tensor_scalar(out=BT[kind + str(j)][:, :, 64:128], in0=src, scalar1=mB,
                                                           scalar2=None, op0=ALU.mult), reads=[d_b, c.d_const], writes=[d_BT])
            d_C = Dep()
            kb.op("pool", lambda e: e.memset(CpR[:], 0.0), writes=[d_C])
            kb.op("pool", lambda e: e.memset(CpI[:], 0.0), writes=[d_C])
            for (Cp, CNx, sc) in ((CpR, CNr, 1.0), (CpI, CNi, -1.0)):
                v = Cp[:].rearrange("p a (j x) -> p a j x", x=160)
                kb.op("dve", lambda e: e.tensor_scalar(out=v[0:64, :, :, 0:16], in0=CNx[0:64], scalar1=sc, scalar2=None,
                                                       op0=ALU.mult), reads=[d_in, d_C], writes=[d_C])
                kb.op("dve", lambda e: e.tensor_scalar(out=v[64:128, :, :, 16:32], in0=CNx[64:128], scalar1=sc, scalar2=None,
                                                       op0=ALU.mult), reads=[d_in, d_C], writes=[d_C])
            kb.barrier()
        dQv = tsb("dQv", [128, 4])
        d_dq = Dep()
        kb.dma("sp", dQv[:], c.dQ[L], writes=[d_dq])
        uT = tsb("uTs", [128, T], BF16)
        d_u = Dep()
        RA = [tsb(f"RA{i}", [128, T]) for i in range(2)]
        IA = [tsb(f"IA{i}", [128, T]) for i in range(2)]
        d_R = [Dep(), Dep()]
        d_I = [Dep(), Dep()]
        HbR, HbI = tsb("HbR", [128, T], BF16), tsb("HbI", [128, T], BF16)
        d_Hb = Dep()
        Yacc = tsb("Yacc", [128, T])
        d_Y = Dep()
        px = [tps(f"px{i}", [128, 512]) for i in range(2)]
        pxs = [tps(f"pxs{i}", [128, 512]) for i in range(2)]
        d_px = [Dep(), Dep()]
        d_pxs = [Dep(), Dep()]
        py = [tps(f"py{i}", [128, 512]) for i in range(2)]
        d_py = [Dep(), Dep()]
        uTv = c.uT_d.rearrange("(k p) t -> p k t", p=128)
        nck = int(os.environ.get("NCK", 4))
        ngd = int(os.environ.get("NGD", 8))
        it = 0
        for ck in range(nck):
            kb.dma("sp", uT[:], uTv[:, ck, :], reads=[c.d_uT_d], writes=[d_u])
            kb.op("dve", lambda e: e.tensor_scalar(out=Yacc[:], in0=uT[:], scalar1=dQv[:, ck:ck + 1], scalar2=None,
                                                   op0=ALU.mult), reads=[d_u, d_dq], writes=[d_Y])
            for gd in range(ngd):
                dr, j = gd // 4, gd % 4
                dc = dr * 4 + ck
                col = dr * 16 + ck * 4 + j
                cur = 0
                for si, (t0, n) in enumerate(segs):
                    pb = it % 2
                    it += 1
                    o0 = t0 if dr == 0 else (t0 - NCTX if t0 >= NCTX else S)
                    for (ps, dps, kind, dst, dd) in ((px[pb], d_px[pb], "r", RA[cur], d_R[cur]),
                                                     (pxs[pb], d_pxs[pb], "i", IA[cur], d_I[cur])):
                        kb.op("pe", lambda e: e.matmul(ps[:, 0:n], lhsT=BT[kind + str(j)][:, dc, :], rhs=uT[:, t0:t0 + n],
                                                       start=True, stop=True), reads=[d_BT, d_u], writes=[dps])
                        kb.op("act", lambda e: e.activation(out=dst[:, o0:o0 + n], in_=ps[:, 0:n], func=AF.Copy),
                              reads=[dps], writes=[dd])
                for k in range(NK):
                    s = 1 << k
                    nxt = 1 - cur
                    ar, ai, an = P2r[:, k, col:col + 1], P2i[:, k, col:col + 1], P2n[:, k, col:col + 1]
                    if dr == 0:
                        lo, hi, keep = slice(0, T - s), slice(s, T), slice(0, s)
                    else:
                        lo, hi, keep = slice(s, T), slice(0, T - s), slice(T - s, T)
                    kb.op("dve", lambda e: e.scalar_tensor_tensor(out=RA[nxt][:, hi], in0=RA[cur][:, lo], scalar=ar,
                                                                  in1=RA[cur][:, hi], op0=ALU.mult, op1=ALU.add),
                          reads=[d_R[cur], d_P2], writes=[d_R[nxt]])
                    kb.op("dve", lambda e: e.scalar_tensor_tensor(out=IA[nxt][:, hi], in0=IA[cur][:, lo], scalar=ar,
                                                                  in1=IA[cur][:, hi], op0=ALU.mult, op1=ALU.add),
                          reads=[d_I[cur], d_P2], writes=[d_I[nxt]])
                    kb.op("dve", lambda e: e.scalar_tensor_tensor(out=RA[nxt][:, hi], in0=IA[cur][:, lo], scalar=an,
                                                                  in1=RA[nxt][:, hi], op0=ALU.mult, op1=ALU.add),
                          reads=[d_I[cur], d_P2], writes=[d_R[nxt]])
                    kb.op("dve", lambda e: e.scalar_tensor_tensor(out=IA[nxt][:, hi], in0=RA[cur][:, lo], scalar=ai,
                                                                  in1=IA[nxt][:, hi], op0=ALU.mult, op1=ALU.add),
                          reads=[d_R[cur], d_P2], writes=[d_I[nxt]])
                    kb.op("act", lambda e: e.activation(out=RA[nxt][:, keep], in_=RA[cur][:, keep], func=AF.Copy),
                          reads=[d_R[cur]], writes=[d_R[nxt]])
                    kb.op("pool", lambda e: e.tensor_copy(out=IA[nxt][:, keep], in_=IA[cur][:, keep]),
                          reads=[d_I[cur]], writes=[d_I[nxt]])
                    cur = nxt
                kb.op("act", lambda e: e.activation(out=HbR[:], in_=RA[cur][:], func=AF.Copy),
                      reads=[d_R[cur]], writes=[d_Hb])
                kb.op("act", lambda e: e.activation(out=HbI[:], in_=IA[cur][:], func=AF.Copy),
                      reads=[d_I[cur]], writes=[d_Hb])
                for si, (t0, n) in enumerate(segs):
                    o0 = t0 if dr == 0 else (t0 - NCTX if t0 >= NCTX else S)
                    pb = it % 2
                    it += 1
                    kb.op("pe", lambda e: e.matmul(py[pb][:, 0:n], lhsT=CpR[:, dc, j * 128:(j + 1) * 128],
                                                   rhs=HbR[:, o0:o0 + n], start=True, stop=False),
                          reads=[d_C, d_Hb], writes=[d_py[pb]])
                    kb.op("pe", lambda e: e.matmul(py[pb][:, 0:n], lhsT=CpI[:, dc, j * 128:(j + 1) * 128],
                                                   rhs=HbI[:, o0:o0 + n], start=False, stop=True),
                          reads=[d_C, d_Hb], writes=[d_py[pb]])
                    kb.op("dve", lambda e: e.tensor_tensor(out=Yacc[:, t0:t0 + n], in0=py[pb][:, 0:n],
                                                           in1=Yacc[:, t0:t0 + n], op=ALU.add),
                          reads=[d_py[pb]], writes=[d_Y])
            g1 = RA[0]
            kb.op("dve", lambda e: e.tensor_tensor(out=g1[:], in0=Yacc[:], in1=Yacc[:], op=ALU.mult),
                  reads=[d_Y, d_R[0], d_R[1], d_I[0], d_I[1]], writes=[d_R[0]])
            kb.op("dve", lambda e: e.tensor_scalar(out=g1[:], in0=g1[:], scalar1=0.044715, scalar2=1.0, op0=ALU.mult,
                                                   op1=ALU.add), reads=[d_R[0]], writes=[d_R[0]])
            kb.op("dve", lambda e: e.tensor_tensor(out=g1[:], in0=g1[:], in1=Yacc[:], op=ALU.mult),
                  reads=[d_R[0], d_Y], writes=[d_R[0]])
            kb.op("act", lambda e: e.activation(out=g1[:], in_=g1[:], func=AF.Sigmoid, scale=1.5957691216057308),
                  reads=[d_R[0]], writes=[d_R[0]])
            kb.op("dve", lambda e: e.tensor_tensor(out=z[:], in0=g1[:], in1=Yacc[:], op=ALU.mult),
                  reads=[d_R[0], d_Y], writes=[d_z])
            kb.dma("sp", c.zT_d[ck * 128:(ck + 1) * 128, :], z[:], reads=[d_z], writes=[c.d_zT_d])
        if "pS1" in dbg:
            o = nc.dram_tensor("dbg_z", [512, T], BF16, kind="ExternalOutput").ap()
            tk = kb.dma("sp", o, c.zT_d[:, :], reads=[c.d_zT_d])
            kb.wait("sp", tk)
            kb.barrier()
            return
        kb.barrier()
    phaseS2(c, L, dbg, segs)


def phaseS2(c, L, dbg, segs):
    nc, kb = c.nc, c.kb
    with ExitStack() as ps_:
        def tsb(name, shape, dt=F32):
            return ps_.enter_context(nc.sbuf_tensor(f"{name}_{L}", list(shape), dt))

        def tps(name, shape, dt=F32):
            return ps_.enter_context(nc.psum_tensor(f"{name}_{L}", list(shape), dt))
        wglu = tsb("wglu", [128, 4, 512], BF16)
        d_w = Dep()
        wsrc = c.w_glu[L].rearrange("(kt p) n -> p kt n", p=128)
        for kt in range(4):
            kb.dma("pool", wglu[:, kt, :], wsrc[:, kt, :], writes=[d_w])
        bgl, gs = tsb("bgl", [128, 4]), tsb("gs", [128, 4])
        kb.dma("sp", bgl[:], c.bgluT[L], writes=[d_w])
        kb.dma("sp", gs[:], c.ongT[L], writes=[d_w])
        zt = [tsb(f"zt{i}", [128, 4, 512], BF16) for i in range(2)]
        d_zt = [Dep(), Dep()]
        pg = [tps(f"pg{i}", [128, 512]) for i in range(2)]
        d_pg = [Dep(), Dep()]
        pss = tps("pss", [128, 512])
        d_pss = Dep()
        sg = [tsb(f"sg{i}", [128, 512]) for i in range(2)]
        d_sg = [Dep(), Dep()]
        o = tsb("og", [128, 4, 512])
        d_o = Dep()
        sq = tsb("sqg", [128, 4, 512])
        d_sq = Dep()
        rs = tsb("rsg", [128, 512])
        d_rs = Dep()
        sb_ = [tsb(f"sbg{i}", [128, 4, 512], BF16) for i in range(2)]
        d_sb = [Dep(), Dep()]
        zv = c.zT_d.rearrange("(k p) t -> p k t", p=128)
        sv = c.sT_d.rearrange("(k p) t -> p k t", p=128)
        it = 0
        for si, (t0, n) in enumerate(segs):
            b = si % 2
            kb.dma("sp", zt[b][:, :, 0:n], zv[:, :, t0:t0 + n], reads=[c.d_zT_d], writes=[d_zt[b]])
            for m in range(4):
                pb = it % 2
                it += 1
                for k in range(4):
                    kb.op("pe", lambda e: e.matmul(pg[pb][:, 0:n], lhsT=wglu[:, k, m * 128:(m + 1) * 128], rhs=zt[b][:, k, 0:n],
                                                   start=(k == 0), stop=(k == 3)), reads=[d_w, d_zt[b]], writes=[d_pg[pb]])
                kb.op("act", lambda e: e.activation(out=sg[pb][:, 0:n], in_=pg[pb][:, 0:n], func=AF.Sigmoid,
                                                    bias=bgl[:, m:m + 1]), reads=[d_pg[pb], d_w], writes=[d_sg[pb]])
                kb.op("dve", lambda e: e.tensor_tensor(out=o[:, m, 0:n], in0=sg[pb][:, 0:n], in1=zt[b][:, m, 0:n], op=ALU.mult),
                      reads=[d_sg[pb], d_zt[b]], writes=[d_o])
                kb.op("pool", lambda e: e.tensor_tensor(out=sq[:, m, 0:n], in0=o[:, m, 0:n], in1=o[:, m, 0:n], op=ALU.mult),
                      reads=[d_o], writes=[d_sq])
            for m in range(4):
                kb.op("pe", lambda e: e.matmul(pss[:, 0:n], lhsT=c.onesf[:], rhs=sq[:, m, 0:n], start=(m == 0), stop=(m == 3)),
                      reads=[d_sq, c.d_const], writes=[d_pss])
            kb.op("act", lambda e: e.activation(out=rs[:, 0:n], in_=pss[:, 0:n], func=AF.Identity, scale=1.0 / 512.0,
                                                bias=c.cst[:, 1:2]), reads=[d_pss, c.d_const], writes=[d_rs])
            kb.op("act", lambda e: e.activation(out=rs[:, 0:n], in_=rs[:, 0:n], func=AF.Sqrt), reads=[d_rs], writes=[d_rs])
            kb.op("dve", lambda e: e.reciprocal(out=rs[:, 0:n], in_=rs[:, 0:n]), reads=[d_rs], writes=[d_rs])
            for m in range(4):
                kb.op("dve", lambda e: e.scalar_tensor_tensor(out=sb_[b][:, m, 0:n], in0=o[:, m, 0:n], scalar=gs[:, m:m + 1],
                                                              in1=rs[:, 0:n], op0=ALU.mult, op1=ALU.mult),
                      reads=[d_o, d_rs, d_w], writes=[d_sb[b]])
            kb.dma("sp", sv[:, :, t0:t0 + n], sb_[b][:, :, 0:n], reads=[d_sb[b]], writes=[c.d_sT_d])
        kb.barrier()


def kernel(**inputs):
    nc, c = build(DEPTH)
    sh = prep_shared(inputs)
    in_maps = []
    for i in range(8):
        m = dict(sh)
        m.update(prep_core(inputs, i % 4))
        in_maps.append(m)
    res = run_bass_kernel_spmd(nc, in_maps, core_ids=list(range(8)))
    out = np.stack([np.asarray(res.results[b]["out"]) for b in range(4)], axis=0)
    return np.ascontiguousarray(out, dtype=np.float32)
```

```python
import numpy as np
from contextlib import ExitStack
import concourse.bass as bass
import concourse.mybir as mybir
from concourse.bass_utils import run_bass_kernel_spmd

F32 = mybir.dt.float32
BF16 = mybir.dt.bfloat16
AF = mybir.ActivationFunctionType
ALU = mybir.AluOpType
AX = mybir.AxisListType

D = 1024
S = 4096
NCTX = 256
T = S + NCTX
NT = T // 128
DEPTH = 4
DFF = 2816
EPS = 1e-6
HEAD_PERM = [0, 4, 1, 5, 2, 6, 3, 7]


class Dep:
    __slots__ = ("w", "r")

    def __init__(self):
        self.w = None
        self.r = {}


class KB:
    def __init__(self, nc, es, ndma=24):
        self.nc = nc
        self.es = es
        self.E = {"pe": nc.tensor, "act": nc.scalar, "dve": nc.vector, "pool": nc.gpsimd, "sp": nc.sync}
        self.sems = []
        self.esem = {}
        self.cnt = {}
        self.seen = {e: {} for e in self.E}
        self.pe_sems = set()
        for e in self.E:
            self._newsem(e)
        self.dq = []
        for i in range(ndma):
            self.sems.append(es.enter_context(nc.semaphore(f"dq{i}")))
            self.dq.append([len(self.sems) - 1, 0])
        self.dn = 0
        self.nins = 0

    def _newsem(self, e):
        self.sems.append(self.es.enter_context(self.nc.semaphore(f"e{e}{len(self.sems)}")))
        self.esem[e] = len(self.sems) - 1
        if e == "pe":
            self.pe_sems.add(len(self.sems) - 1)
        self.cnt[e] = 0

    def wait(self, e, tk):
        si, v = tk
        if self.seen[e].get(si, 0) >= v:
            return
        if e == "pe" and si in self.pe_sems:
            return
        self.E[e].wait_ge(self.sems[si], v)
        self.seen[e][si] = v
        self.nins += 1

    def _deps(self, e, reads, writes):
        for d in reads:
            if d.w is not None:
                self.wait(e, d.w)
        for d in writes:
            if d.w is not None:
                self.wait(e, d.w)
            for si, v in d.r.items():
                self.wait(e, (si, v))

    def _mark(self, tk, reads, writes):
        for d in reads:
            if d.r.get(tk[0], 0) < tk[1]:
                d.r[tk[0]] = tk[1]
        for d in writes:
            d.w = tk
            d.r = {}

    def op(self, e, fn, reads=(), writes=()):
        self._deps(e, reads, writes)
        if self.cnt[e] >= 30000:
            self._newsem(e)
        inst = fn(self.E[e])
        self.cnt[e] += 1
        inst.then_inc(self.sems[self.esem[e]], 1)
        tk = (self.esem[e], self.cnt[e])
        self._mark(tk, reads, writes)
        self.nins += 1
        return tk

    def barrier(self):
        for e in self.E:
            for f in self.E:
                if f != e and self.cnt[f] > 0:
                    self.wait(e, (self.esem[f], self.cnt[f]))
            for si, v in self.dq:
                if v > 0:
                    self.wait(e, (si, v))

    def dma(self, q, out, in_, reads=(), writes=()):
        k = self.dn
        self.dn = (self.dn + 1) % len(self.dq)
        si, v = self.dq[k]
        if v > 0:
            self.wait(q, (si, v))
        self._deps(q, reads, writes)
        inst = self.E[q].dma_start(out=out, in_=in_)
        v += 16
        self.dq[k][1] = v
        inst.then_inc(self.sems[si], 16)
        tk = (si, v)
        self._mark(tk, reads, writes)
        self.nins += 1
        return tk


class Ctx:
    pass


def build(depth=DEPTH, dbg=None):
    dbg = dbg or set()
    nc = bass.Bass("TRN2", target_bir_lowering=False)
    es = ExitStack()
    c = Ctx()
    c.nc = nc
    c.es = es

    def din(name, shape, dt=F32):
        return nc.dram_tensor(name, list(shape), dt, kind="ExternalInput").ap()

    def dout(name, shape, dt=F32):
        return nc.dram_tensor(name, list(shape), dt, kind="ExternalOutput").ap()

    def dscr(name, shape, dt=F32):
        return nc.dram_tensor(name, list(shape), dt, kind="Internal").ap()

    c.xin = din("xin", [T, D])
    c.cT = din("cT", [128, 8, 2])
    c.w_ada = din("w_ada", [DEPTH, D, 6 * D])
    c.b_ada = din("b_ada", [DEPTH, 6 * D])
    c.n1g = din("n1g", [DEPTH, D])
    c.n2g = din("n2g", [DEPTH, D])
    c.w_in = din("w_in", [DEPTH, D, 1280])
    c.qg = din("qg", [DEPTH, 64])
    c.kg = din("kg", [DEPTH, 64])
    c.cos_t = din("cos_t", [T, 64])
    c.sin_t = din("sin_t", [T, 64])
    c.ident = din("ident", [128, 128])
    c.xres = dscr("xres", [T, D])
    c.modv = dscr("modv", [2, 6 * D])
    c.uT_d = dscr("uT_d", [512, T], BF16)
    c.aT_d = dscr("aT_d", [512, T], BF16)
    c.ong = din("ong", [DEPTH, D])
    c.sT_d = dscr("sT_d", [512, T], BF16)
    c.zT_d = dscr("zT_d", [512, T], BF16)
    c.lamN_re = din("lamN_re", [DEPTH, 128, 32])
    c.lamN_im = din("lamN_im", [DEPTH, 128, 32])
    c.lsN = din("lsN", [DEPTH, 128, 32])
    c.lamQ_re = din("lamQ_re", [DEPTH, 128, 512])
    c.lamQ_im = din("lamQ_im", [DEPTH, 128, 512])
    c.lsQ = din("lsQ", [DEPTH, 128, 512])
    c.BQ_re = din("BQ_re", [DEPTH, 128, 512])
    c.BQ_im = din("BQ_im", [DEPTH, 128, 512])
    c.CNr = din("CNr", [DEPTH, 128, 8, 4, 16])
    c.CNi = din("CNi", [DEPTH, 128, 8, 4, 16])
    c.dQ = din("dQ", [DEPTH, 128, 4])
    c.w_glu = din("w_glu", [DEPTH, 512, 512])
    c.bgluT = din("bgluT", [DEPTH, 128, 4])
    c.ongT = din("ongT", [DEPTH, 128, 4])
    c.consts = din("consts", [128, 16])
    c.h2T_d = dscr("h2T_d", [D, T], BF16)
    c.w_out = din("w_out", [DEPTH, D, D])
    c.w_up = din("w_up", [DEPTH, D, 2 * DFF])
    c.w_down = din("w_down", [DEPTH, DFF, D])
    c.cwT = din("cwT", [DEPTH, 128, 44, 3])
    c.cbT = din("cbT", [DEPTH, 128, 44])
    c.out = dout("out", [S, D])
    c.dbg = {}

    with es:
        kb = KB(nc, es)
        c.kb = kb

        def sb(name, shape, dt=F32):
            return es.enter_context(nc.sbuf_tensor(name, list(shape), dt))

        c.sb = sb
        c.identf = sb("identf", [128, 128], F32)
        c.identb = sb("identb", [128, 128], BF16)
        c.d_ident = Dep()
        kb.dma("sp", c.identf[:], c.ident[:], writes=[c.d_ident])
        kb.op("dve", lambda e: e.tensor_copy(out=c.identb[:], in_=c.identf[:]), reads=[c.d_ident], writes=[c.d_ident])
        c.cst = sb("cst", [128, 16], F32)
        c.onesf = sb("onesf", [128, 128], F32)
        c.d_const = Dep()
        kb.dma("sp", c.cst[:], c.consts[:], writes=[c.d_const])
        kb.op("pool", lambda e: e.memset(c.onesf[:], 1.0), writes=[c.d_const])
        c.halfpi, c.mE, c.mO, c.sgnC = c.cst[:, 0:1], c.cst[:, 2:3], c.cst[:, 3:4], c.cst[:, 4:5]
        c.sT = sb("sT", [128, 8, 2], F32)
        c.d_sT = Dep()
        kb.dma("sp", c.sT[:], c.cT[:], writes=[c.d_sT])
        kb.op("act", lambda e: e.activation(out=c.sT[:], in_=c.sT[:], func=AF.Silu), reads=[c.d_sT], writes=[c.d_sT])
        c.d_xres = [Dep() for _ in range(NT)]
        for i in range(NT):
            kb.dma("sp", c.xres[i * 128:(i + 1) * 128, :], c.xin[i * 128:(i + 1) * 128, :], writes=[c.d_xres[i]])

        for L in range(depth):
            layer(c, L, dbg)

        tks = []
        for i in range(2, NT):
            tks.append(kb.dma("sp", c.out[(i - 2) * 128:(i - 1) * 128, :], c.xres[i * 128:(i + 1) * 128, :],
                              reads=[c.d_xres[i]]))
        for tk in tks:
            kb.wait("sp", tk)
        for e in ("pe", "act", "dve", "pool"):
            if kb.cnt[e] > 0:
                kb.wait("sp", (kb.esem[e], kb.cnt[e]))
    return nc, c


def layer(c, L, dbg):
    nc, kb = c.nc, c.kb
    c.d_qT, c.d_kT, c.d_V, c.d_uT_d = Dep(), Dep(), Dep(), Dep()
    c.d_aT_d, c.d_sT_d, c.d_h2T_d = Dep(), Dep(), Dep()
    phase0(c, L, None, None)
    if "p0" in dbg:
        o = nc.dram_tensor("dbg_modv", [2, 6 * D], F32, kind="ExternalOutput").ap()
        tk = kb.dma("sp", o, c.modv[:, :], reads=[c.d_modv])
        kb.wait("sp", tk)
        return
    with ExitStack() as ls:
        def lsb(name, shape, dt=F32):
            return ls.enter_context(nc.sbuf_tensor(f"{name}_{L}", list(shape), dt))
        c.qT = lsb("qT", [128, 4, T], BF16)
        c.kT = lsb("kT", [128, T], BF16)
        c.Vaug = lsb("Vaug", [128, NT, 2, 66], BF16)
        kb.op("pool", lambda e: e.memset(c.Vaug[:, :, :, 64:65], 1.0), writes=[c.d_V])
        phaseA(c, L, dbg)
        if "pA1" in dbg:
            return
        if "pA" in dbg:
            dbg_dump(c, "qT", c.qT, [c.d_qT], BF16)
            dbg_dump(c, "kT", c.kT, [c.d_kT], BF16)
            dbg_dump(c, "Vaug", c.Vaug, [c.d_V], BF16)
            return
        phaseB(c, L, dbg)
        if "pB" in dbg:
            o = nc.dram_tensor("dbg_aT", [512, T], BF16, kind="ExternalOutput").ap()
            tk = kb.dma("sp", o, c.aT_d[:, :], reads=[c.d_aT_d])
            kb.wait("sp", tk)
            return
    if "noS" in dbg:
        with ExitStack() as zs:
            z = zs.enter_context(nc.sbuf_tensor(f"zt_{L}", [128, 4, T], BF16))
            dz = Dep()
            kb.op("pool", lambda e: e.memset(z[:], 0.0), writes=[dz])
            kb.dma("sp", c.sT_d.rearrange("(k p) t -> p k t", p=128), z[:], reads=[dz], writes=[c.d_sT_d])
            kb.barrier()
    else:
        phaseS(c, L, dbg)
        if "pS1" in dbg:
            return
        if "pS" in dbg:
            o = nc.dram_tensor("dbg_sT", [512, T], BF16, kind="ExternalOutput").ap()
            tk = kb.dma("sp", o, c.sT_d[:, :], reads=[c.d_sT_d])
            kb.wait("sp", tk)
            return
    phaseC1(c, L, dbg)
    phaseC2(c, L, dbg)


def dbg_dump(c, name, tile, deps, dt=F32):
    o = c.nc.dram_tensor("dbg_" + name, list(tile.shape), dt, kind="ExternalOutput").ap()
    tk = c.kb.dma("sp", o, tile[:], reads=deps)
    c.kb.wait("sp", tk)


def phase0(c, L, lsb, lps):
    nc, kb = c.nc, c.kb
    with ExitStack() as ps_:
        def tsb(name, shape, dt=F32):
            return ps_.enter_context(nc.sbuf_tensor(f"{name}_{L}", list(shape), dt))
        wa = [tsb(f"wa{i}", [128, 8, 512]) for i in range(2)]
        d_wa = [Dep(), Dep()]
        mod = tsb("mod", [2, 6 * D])
        d_mod = Dep()
        bada = tsb("bada", [2, 6 * D])
        d_bada = Dep()
        ng = tsb("ng", [2, 2, D])
        d_ng = Dep()
        vec = tsb("vec", [2, 6 * D])
        d_vec = Dep()
        ps = ps_.enter_context(nc.psum_tensor(f"p0ps_{L}", [128, 512], F32))
        d_ps = Dep()
        kb.dma("sp", bada[:], c.b_ada[L:L + 1, :].to_broadcast([2, 6 * D]), writes=[d_bada])
        kb.dma("sp", ng[:, 0, :], c.n1g[L:L + 1, :].to_broadcast([2, D]), writes=[d_ng])
        kb.dma("sp", ng[:, 1, :], c.n2g[L:L + 1, :].to_broadcast([2, D]), writes=[d_ng])
        wsrc = c.w_ada[L].rearrange("(kt p) n -> p kt n", p=128)
        for j in range(12):
            b = j % 2
            kb.dma("sp", wa[b][:], wsrc[:, :, j * 512:(j + 1) * 512], writes=[d_wa[b]])
            for kt in range(8):
                kb.op("pe", lambda e: e.matmul(ps[0:2, :], lhsT=c.sT[:, kt, :], rhs=wa[b][:, kt, :],
                                               start=(kt == 0), stop=(kt == 7)),
                      reads=[d_wa[b], c.d_sT], writes=[d_ps])
            kb.op("dve", lambda e: e.tensor_tensor(out=mod[:, j * 512:(j + 1) * 512], in0=ps[0:2, :],
                                                   in1=bada[:, j * 512:(j + 1) * 512], op=ALU.add),
                  reads=[d_ps, d_bada], writes=[d_mod])
        def sl(k):
            return slice(k * D, (k + 1) * D)
        kb.op("dve", lambda e: e.scalar_tensor_tensor(out=vec[:, sl(0)], in0=mod[:, sl(1)], scalar=1.0, in1=ng[:, 0, :],
                                                      op0=ALU.add, op1=ALU.mult), reads=[d_mod, d_ng], writes=[d_vec])
        kb.op("dve", lambda e: e.tensor_copy(out=vec[:, sl(1)], in_=mod[:, sl(0)]), reads=[d_mod], writes=[d_vec])
        kb.op("dve", lambda e: e.tensor_copy(out=vec[:, sl(2)], in_=mod[:, sl(2)]), reads=[d_mod], writes=[d_vec])
        kb.op("dve", lambda e: e.scalar_tensor_tensor(out=vec[:, sl(3)], in0=mod[:, sl(4)], scalar=1.0, in1=ng[:, 1, :],
                                                      op0=ALU.add, op1=ALU.mult), reads=[d_mod, d_ng], writes=[d_vec])
        kb.op("dve", lambda e: e.tensor_copy(out=vec[:, sl(4)], in_=mod[:, sl(3)]), reads=[d_mod], writes=[d_vec])
        kb.op("dve", lambda e: e.tensor_copy(out=vec[:, sl(5)], in_=mod[:, sl(5)]), reads=[d_mod], writes=[d_vec])
        if not hasattr(c, "d_modv"):
            c.d_modv = Dep()
        kb.dma("sp", c.modv[:, :], vec[:], reads=[d_vec], writes=[c.d_modv])
        kb.barrier()


def load_bc(c, tile, row, k, dep):
    c.kb.dma("sp", tile[:], c.modv[row:row + 1, k * D:(k + 1) * D].to_broadcast([128, D]),
             reads=[c.d_modv], writes=[dep])


def rstd_from_ss(c, ss, n, d_ss, shape_cols=1):
    kb = c.kb
    kb.op("dve", lambda e: e.tensor_scalar(out=ss, in0=ss, scalar1=1.0 / n, scalar2=EPS, op0=ALU.mult, op1=ALU.add),
          reads=[d_ss], writes=[d_ss])
    kb.op("act", lambda e: e.activation(out=ss, in_=ss, func=AF.Sqrt), reads=[d_ss], writes=[d_ss])
    kb.op("dve", lambda e: e.reciprocal(out=ss, in_=ss), reads=[d_ss], writes=[d_ss])


def phaseA(c, L, dbg):
    nc, kb = c.nc, c.kb
    with ExitStack() as ps_:
        def tsb(name, shape, dt=F32):
            return ps_.enter_context(nc.sbuf_tensor(f"{name}_{L}", list(shape), dt))

        def tps(name, shape, dt=F32):
            return ps_.enter_context(nc.psum_tensor(f"{name}_{L}", list(shape), dt))
        win = tsb("win", [128, 8, 1280], BF16)
        d_win = Dep()
        wsrc = c.w_in[L].rearrange("(kt p) n -> p kt n", p=128)
        for kt in range(8):
            kb.dma("pool", win[:, kt, :], wsrc[:, kt, :], writes=[d_win])
        bc = {}
        d_bc = Dep()
        for nm, row, k in (("A1l", 0, 0), ("S1l", 0, 1), ("A1c", 1, 0), ("S1c", 1, 1)):
            bc[nm] = tsb(nm, [128, D])
            load_bc(c, bc[nm], row, k, d_bc)
        gq = tsb("gq", [128, 64])
        gk = tsb("gk", [128, 64])
        d_g = Dep()
        kb.dma("sp", gq[:], c.qg[L:L + 1, :].to_broadcast([128, 64]), writes=[d_g])
        kb.dma("sp", gk[:], c.kg[L:L + 1, :].to_broadcast([128, 64]), writes=[d_g])
        kb.op("dve", lambda e: e.tensor_scalar(out=gq[:], in0=gq[:], scalar1=0.125, scalar2=None, op0=ALU.mult),
              reads=[d_g], writes=[d_g])

        NB = 2
        xt = [tsb(f"xt{i}", [128, D]) for i in range(NB)]
        d_xt = [Dep() for _ in range(NB)]
        cs = [tsb(f"cs{i}", [128, 2, 64]) for i in range(NB)]
        d_cs = [Dep() for _ in range(NB)]
        junk = tsb("junk", [128, D])
        d_junk = Dep()
        ss = tsb("ss", [128, 1])
        d_ss = Dep()
        hf = tsb("hf", [128, D])
        d_hf = Dep()
        hb = tsb("hb", [128, D], BF16)
        d_hb = Dep()
        hT = tsb("hT", [128, 8, 128], BF16)
        d_hT = Dep()
        tp = tps("tp", [128, 8, 128], BF16)
        d_tp = Dep()
        pp = [tps(f"pp{i}", [128, 512]) for i in range(3)]
        d_pp = [Dep() for _ in range(3)]
        qf = tsb("qf", [128, 640])
        d_qf = Dep()
        sq = tsb("sq", [128, 640])
        d_sq = Dep()
        ssq = tsb("ssq", [128, 10])
        d_ssq = Dep()
        ra = tsb("ra", [128, 640])
        d_ra = Dep()
        rb = tsb("rb", [128, 640])
        d_rb = Dep()
        qb = tsb("qb", [128, 640], BF16)
        d_qb = Dep()
        ub = tsb("ub", [128, 512], BF16)
        d_ub = Dep()
        tp2 = tps("tp2", [128, 8, 128], BF16)
        d_tp2 = Dep()
        uTs = [tsb(f"uTs{i}", [128, 4, 128], BF16) for i in range(2)]
        d_uTs = [Dep(), Dep()]

        import os
        CUT = int(os.environ.get("CUT", "99"))
        for i in range(int(os.environ.get("NTA", NT))):
            b = i % NB
            isctx = i < 2
            A1 = bc["A1c"] if isctx else bc["A1l"]
            S1 = bc["S1c"] if isctx else bc["S1l"]
            rows = slice(i * 128, (i + 1) * 128)
            kb.dma("sp", xt[b][:], c.xres[rows, :], reads=[c.d_xres[i]], writes=[d_xt[b]])
            kb.dma("sp", cs[b][:, 0, :], c.cos_t[rows, :], writes=[d_cs[b]])
            kb.dma("sp", cs[b][:, 1, :], c.sin_t[rows, :], writes=[d_cs[b]])
            if CUT < 1:
                continue
            kb.op("act", lambda e: e.activation(out=junk[:], in_=xt[b][:], func=AF.Square, accum_out=ss[:]),
                  reads=[d_xt[b]], writes=[d_junk, d_ss])
            rstd_from_ss(c, ss[:], D, d_ss)
            kb.op("dve", lambda e: e.scalar_tensor_tensor(out=hf[:], in0=xt[b][:], scalar=ss[:], in1=A1[:],
                                                          op0=ALU.mult, op1=ALU.mult),
                  reads=[d_xt[b], d_ss, d_bc], writes=[d_hf])
            kb.op("dve", lambda e: e.tensor_tensor(out=hb[:], in0=hf[:], in1=S1[:], op=ALU.add),
                  reads=[d_hf, d_bc], writes=[d_hb])
            if "pA1" in dbg and i == 0:
                dbg_dump(c, "ss", ss, [d_ss]); dbg_dump(c, "hf", hf, [d_hf]); dbg_dump(c, "hb", hb, [d_hb], BF16)
                dbg_dump(c, "xt", xt[b], [d_xt[b]]); dbg_dump(c, "A1", A1, [d_bc])
            if CUT < 2:
                continue
            for kt in range(8):
                kb.op("pe", lambda e: e.transpose(tp[:, kt, :], hb[:, kt * 128:(kt + 1) * 128], c.identb[:]),
                      reads=[d_hb, c.d_ident], writes=[d_tp])
            kb.op("act", lambda e: e.activation(out=hT[:], in_=tp[:], func=AF.Copy), reads=[d_tp], writes=[d_hT])
            SUB = int(os.environ.get("SUB", "9"))
            if SUB < 1:
                continue
            for j, (c0, c1) in enumerate(((0, 512), (512, 1024), (1024, 1280))):
                for kt in range(8):
                    kb.op("pe", lambda e: e.matmul(pp[j][:, 0:c1 - c0], lhsT=hT[:, kt, :], rhs=win[:, kt, c0:c1],
                                                   start=(kt == 0), stop=(kt == 7)),
                          reads=[d_hT, d_win], writes=[d_pp[j]])
            if SUB < 2:
                continue
            kb.op("act", lambda e: e.activation(out=qf[:, 0:512], in_=pp[0][:, :], func=AF.Copy),
                  reads=[d_pp[0]], writes=[d_qf])
            kb.op("act", lambda e: e.activation(out=qf[:, 512:640], in_=pp[1][:, 0:128], func=AF.Copy),
                  reads=[d_pp[1]], writes=[d_qf])
            if SUB < 3:
                continue
            if os.environ.get("NOV") is None:
                kb.op("act", lambda e: e.activation(out=c.Vaug[:, i, :, 0:64],
                                                    in_=pp[1][:, 128:256].rearrange("p (h d) -> p h d", h=2), func=AF.Copy),
                      reads=[d_pp[1]], writes=[c.d_V])
            if os.environ.get("NOU") is not None:
                continue
            kb.op("act", lambda e: e.activation(out=ub[:, 0:256], in_=pp[1][:, 256:512], func=AF.Copy), reads=[d_pp[1]], writes=[d_ub])
            kb.op("act", lambda e: e.activation(out=ub[:, 256:512], in_=pp[2][:, 0:256], func=AF.Copy), reads=[d_pp[2]], writes=[d_ub])
            if CUT < 3:
                continue
            kb.op("dve", lambda e: e.tensor_tensor(out=sq[:], in0=qf[:], in1=qf[:], op=ALU.mult), reads=[d_qf], writes=[d_sq])
            kb.op("dve", lambda e: e.tensor_reduce(out=ssq[:], in_=sq[:].rearrange("p (h d) -> p h d", d=64), axis=AX.X,
                                                   op=ALU.add), reads=[d_sq], writes=[d_ssq])
            rstd_from_ss(c, ssq[:], 64, d_ssq)
            q3 = qf[:].rearrange("p (h d) -> p h d", d=64)
            ra3 = ra[:].rearrange("p (h d) -> p h d", d=64)
            kb.op("dve", lambda e: e.tensor_tensor(out=ra3, in0=q3, in1=ssq[:].unsqueeze(2).to_broadcast([128, 10, 64]),
                                                   op=ALU.mult), reads=[d_qf, d_ssq], writes=[d_ra])
            kb.op("dve", lambda e: e.tensor_tensor(out=ra3[:, 0:8, :], in0=ra3[:, 0:8, :],
                                                   in1=gq[:].unsqueeze(1).to_broadcast([128, 8, 64]), op=ALU.mult),
                  reads=[d_ra, d_g], writes=[d_ra])
            kb.op("dve", lambda e: e.tensor_tensor(out=ra3[:, 8:10, :], in0=ra3[:, 8:10, :],
                                                   in1=gk[:].unsqueeze(1).to_broadcast([128, 2, 64]), op=ALU.mult),
                  reads=[d_ra, d_g], writes=[d_ra])
            if CUT < 4:
                continue
            cosb = cs[b][:, 0, :].unsqueeze(1).to_broadcast([128, 10, 64])
            t6 = ra[:].rearrange("p (h a x f) -> p h a x f", h=10, a=2, x=2)
            r6 = rb[:].rearrange("p (h a x f) -> p h a x f", h=10, a=2, x=2)
            sin4 = cs[b][:, 1, :].rearrange("p (a x f) -> p a x f", a=2, x=2)
            for half in range(2):
                kb.op("pool", lambda e: e.tensor_tensor(
                    out=r6[:, :, :, half, :], in0=t6[:, :, :, 1 - half, :],
                    in1=sin4[:, :, half, :].unsqueeze(1).to_broadcast([128, 10, 2, 16]), op=ALU.mult),
                    reads=[d_ra, d_cs[b]], writes=[d_rb])
            kb.op("dve", lambda e: e.tensor_tensor(out=ra3, in0=ra3, in1=cosb, op=ALU.mult),
                  reads=[d_ra, d_cs[b]], writes=[d_ra])
            kb.op("dve", lambda e: e.tensor_tensor(out=qb[:], in0=ra[:], in1=rb[:], op=ALU.add),
                  reads=[d_ra, d_rb], writes=[d_qb])
            if CUT < 5:
                continue
            for j in range(5):
                kb.op("pe", lambda e: e.transpose(tp2[:, j, :], qb[:, j * 128:(j + 1) * 128], c.identb[:]),
                      reads=[d_qb, c.d_ident], writes=[d_tp2])
            kb.op("act", lambda e: e.activation(out=c.qT[:, :, rows], in_=tp2[:, 0:4, :], func=AF.Copy),
                  reads=[d_tp2], writes=[c.d_qT])
            kb.op("act", lambda e: e.activation(out=c.kT[:, rows], in_=tp2[:, 4, :], func=AF.Copy),
                  reads=[d_tp2], writes=[c.d_kT])
            for j in range(4):
                kb.op("pe", lambda e: e.transpose(tp[:, j, :], ub[:, j * 128:(j + 1) * 128], c.identb[:]),
                      reads=[d_ub, c.d_ident], writes=[d_tp])
            ut = uTs[i % 2]
            kb.op("dve", lambda e: e.tensor_copy(out=ut[:], in_=tp[:, 0:4, :]), reads=[d_tp], writes=[d_uTs[i % 2]])
            kb.dma("sp", c.uT_d.rearrange("(k p) t -> p k t", p=128)[:, :, rows], ut[:], reads=[d_uTs[i % 2]],
                   writes=[c.d_uT_d])
        kb.barrier()


def rope_tables():
    rows = S // 64
    r, col = np.meshgrid(np.arange(rows), np.arange(64), indexing="ij")
    pos = np.stack([r.reshape(-1), col.reshape(-1)], axis=-1).astype(np.float32)
    freqs = (np.float32(10000.0) ** (-np.arange(16, dtype=np.float32) / np.float32(16))).astype(np.float32)
    ang = (pos[:, :, None] * freqs).astype(np.float32)
    cs, sn = np.cos(ang).astype(np.float32), np.sin(ang).astype(np.float32)
    cos64 = np.ones((T, 2, 2, 16), np.float32)
    sin64 = np.zeros((T, 2, 2, 16), np.float32)
    cos64[NCTX:, :, 0, :] = cs
    cos64[NCTX:, :, 1, :] = cs
    sin64[NCTX:, :, 0, :] = -sn
    sin64[NCTX:, :, 1, :] = sn
    return cos64.reshape(T, 64), sin64.reshape(T, 64)


def prep_shared(inp):
    f = lambda a: np.ascontiguousarray(np.asarray(a, dtype=np.float32))
    sh = {}
    sh["w_ada"] = f(inp["w_ada"])
    sh["b_ada"] = f(inp["b_ada"])
    sh["n1g"] = f(inp["norm1_g"])
    sh["n2g"] = f(inp["norm2_g"])
    w_in = f(inp["w_in"]).copy()
    qcols = np.concatenate([np.arange(h * 64, (h + 1) * 64) for h in HEAD_PERM])
    w_in[:, :, 0:512] = w_in[:, :, qcols]
    sh["w_in"] = w_in
    ong = f(inp["out_norm_g"]).copy()
    ong[:, 0:512] = ong[:, qcols]
    sh["ong"] = ong
    w_out = f(inp["w_out"]).copy()
    w_out[:, 0:512, :] = w_out[:, qcols, :]
    sh["w_out"] = w_out
    sh["w_up"] = f(inp["w_up"])
    sh["w_down"] = f(inp["w_down"])
    cw = f(inp["conv_w"])
    sh["cwT"] = np.ascontiguousarray(cw.reshape(DEPTH, 3, 44, 128).transpose(0, 3, 2, 1))
    sh["cbT"] = np.ascontiguousarray(f(inp["conv_b"]).reshape(DEPTH, 44, 128).transpose(0, 2, 1))
    lre, lim, lst = f(inp["ssm_lambda_re"]), f(inp["ssm_lambda_im"]), f(inp["ssm_log_step"])
    def layN(a):
        t = a.reshape(DEPTH, 2, 16, 2, 64).transpose(0, 3, 4, 1, 2)
        return np.ascontiguousarray(t.reshape(DEPTH, 128, 32))
    sh["lamN_re"], sh["lamN_im"] = layN(lre), layN(lim)
    sh["lsN"] = layN(np.broadcast_to(lst[..., None], (DEPTH, 2, 32, 64)))
    def layQ(a):
        t = a.reshape(DEPTH, 2, 4, 8, 1, 64)
        t = np.broadcast_to(t, (DEPTH, 2, 4, 8, 16, 64)).transpose(0, 3, 4, 1, 2, 5)
        return np.ascontiguousarray(t.reshape(DEPTH, 128, 512))
    sh["lamQ_re"], sh["lamQ_im"] = layQ(lre), layQ(lim)
    sh["lsQ"] = layQ(np.broadcast_to(lst[..., None], (DEPTH, 2, 32, 64)))
    def layB(a):
        t = a.reshape(DEPTH, 2, 4, 8, 64, 16).transpose(0, 3, 5, 1, 2, 4)
        return np.ascontiguousarray(t.reshape(DEPTH, 128, 512))
    sh["BQ_re"], sh["BQ_im"] = layB(f(inp["ssm_b_re"])), layB(f(inp["ssm_b_im"]))
    def layC(a):
        t = a.reshape(DEPTH, 2, 4, 4, 2, 16, 64).transpose(0, 4, 6, 1, 2, 3, 5)
        return np.ascontiguousarray(t.reshape(DEPTH, 128, 8, 4, 16))
    sh["CNr"], sh["CNi"] = layC(f(inp["ssm_c_re"])), layC(f(inp["ssm_c_im"]))
    sh["dQ"] = np.ascontiguousarray(f(inp["ssm_d"]).reshape(DEPTH, 4, 128).transpose(0, 2, 1))
    sh["w_glu"] = f(inp["w_glu"])
    sh["bgluT"] = np.ascontiguousarray(f(inp["b_glu"]).reshape(DEPTH, 4, 128).transpose(0, 2, 1))
    sh["ongT"] = np.ascontiguousarray(f(inp["out_norm_g"])[:, 512:].reshape(DEPTH, 4, 128).transpose(0, 2, 1))
    cst = np.zeros((128, 16), np.float32)
    for g8_ in range(8):
        cst[:, 8 + g8_] = (np.arange(128) // 16 == g8_)
    cst[:, 0] = np.pi / 2
    cst[:, 1] = EPS
    p = np.arange(128)
    cst[:, 2] = ((p // 16) % 2 == 0)
    cst[:, 3] = ((p // 16) % 2 == 1)
    cst[:, 4] = np.where(p < 64, 1.0, -1.0)
    sh["consts"] = cst
    sh["qg"] = f(inp["q_norm_g"])
    sh["kg"] = f(inp["k_norm_g"])
    cos64, sin64 = rope_tables()
    sh["cos_t"] = cos64
    sh["sin_t"] = sin64
    sh["ident"] = np.eye(128, dtype=np.float32)
    return sh


def prep_core(inp, b):
    m = {}
    m["xin"] = np.ascontiguousarray(np.concatenate([np.asarray(inp["ctx"][b]), np.asarray(inp["x"][b])], axis=0),
                                    dtype=np.float32)
    cv = np.stack([np.asarray(inp["c"][b]), np.asarray(inp["c_ctx"])], axis=0).astype(np.float32)
    m["cT"] = np.ascontiguousarray(cv.reshape(2, 8, 128).transpose(2, 1, 0))
    return m


def phaseB(c, L, dbg):
    import os
    nc, kb = c.nc, c.kb
    with ExitStack() as ps_:
        def tsb(name, shape, dt=F32):
            return ps_.enter_context(nc.sbuf_tensor(f"{name}_{L}", list(shape), dt))

        def tps(name, shape, dt=F32):
            return ps_.enter_context(nc.psum_tensor(f"{name}_{L}", list(shape), dt))
        st = [tps(f"st{i}", [128, 512]) for i in range(2)]
        d_st = [Dep(), Dep()]
        oacc = [tps(f"oa{i}", [128, 512]) for i in range(4)]
        d_oa = [Dep() for _ in range(4)]
        tpb = tps("tpb", [128, 8, 128], BF16)
        d_tpb = Dep()
        pt = [tsb(f"pt{i}", [128, 512], BF16) for i in range(3)]
        d_pt = [Dep() for _ in range(3)]
        osb = [tsb(f"osb{i}", [128, 66]) for i in range(2)]
        d_osb = [Dep(), Dep()]
        rl = [tsb(f"rl{i}", [128, 1]) for i in range(2)]
        d_rl = [Dep(), Dep()]
        attn = tsb("attn", [128, 4, 512])
        d_attn = [Dep() for _ in range(4)]
        ga = tsb("ga", [128, 512])
        d_ga = Dep()
        kb.dma("sp", ga[:], c.ong[L:L + 1, 0:512].to_broadcast([128, 512]), writes=[d_ga])
        junk = tsb("junkb", [128, 512])
        d_junk = Dep()
        ss = tsb("ssb", [128, 1])
        d_ss = Dep()
        ab = tsb("ab", [128, 512], BF16)
        d_ab = Dep()
        aTs = [tsb(f"aTs{i}", [128, 4, 128], BF16) for i in range(2)]
        d_aTs = [Dep(), Dep()]
        aTv = c.aT_d.rearrange("(k p) t -> p k t", p=128)

        qblocks = [(0, 256, [0, 1])] + [(256 + 512 * j, 512, list(range(NT))) for j in range(8)]
        nqb = int(os.environ.get("NQB", len(qblocks)))
        ev = 0
        for (q0, nq, kts) in qblocks[:nqb]:
            nsub = nq // 128
            for hp in range(8):
                cidx, half = hp // 2, hp % 2
                pr = slice(64 * half, 64 * half + 64)

                def qk(n):
                    kt = kts[n]
                    kb.op("pe", lambda e: e.matmul(st[n % 2][:, 0:nq], lhsT=c.kT[pr, kt * 128:(kt + 1) * 128],
                                                   rhs=c.qT[pr, cidx, q0:q0 + nq], start=True, stop=True),
                          reads=[c.d_kT, c.d_qT], writes=[d_st[n % 2]])
                qk(0)
                for n, kt in enumerate(kts):
                    if n + 1 < len(kts):
                        qk(n + 1)
                    pb = pt[n % 3]
                    kb.op("act", lambda e: e.activation(out=pb[:, 0:nq], in_=st[n % 2][:, 0:nq], func=AF.Exp),
                          reads=[d_st[n % 2]], writes=[d_pt[n % 3]])
                    for s in range(nsub):
                        kb.op("pe", lambda e: e.matmul(oacc[s][:, 0:65], lhsT=pb[:, s * 128:(s + 1) * 128],
                                                       rhs=c.Vaug[:, kt, half, 0:65], start=(n == 0),
                                                       stop=(n == len(kts) - 1)),
                              reads=[d_pt[n % 3], c.d_V], writes=[d_oa[s]])
                for s in range(nsub):
                    o = osb[ev % 2]
                    r = rl[ev % 2]
                    kb.op("act", lambda e: e.activation(out=o[:, 0:65], in_=oacc[s][:, 0:65], func=AF.Copy),
                          reads=[d_oa[s]], writes=[d_osb[ev % 2]])
                    kb.op("dve", lambda e: e.reciprocal(out=r[:], in_=o[:, 64:65]), reads=[d_osb[ev % 2]],
                          writes=[d_rl[ev % 2]])
                    kb.op("dve", lambda e: e.tensor_scalar(out=attn[:, s, hp * 64:(hp + 1) * 64], in0=o[:, 0:64],
                                                           scalar1=r[:], scalar2=None, op0=ALU.mult),
                          reads=[d_osb[ev % 2], d_rl[ev % 2]], writes=[d_attn[s]])
                    ev += 1
            for s in range(nsub):
                kb.op("act", lambda e: e.activation(out=junk[:], in_=attn[:, s, :], func=AF.Square, accum_out=ss[:]),
                      reads=[d_attn[s]], writes=[d_junk, d_ss])
                rstd_from_ss(c, ss[:], 512, d_ss)
                kb.op("dve", lambda e: e.scalar_tensor_tensor(out=ab[:], in0=attn[:, s, :], scalar=ss[:], in1=ga[:],
                                                              op0=ALU.mult, op1=ALU.mult),
                      reads=[d_attn[s], d_ss, d_ga], writes=[d_ab])
                for j in range(4):
                    kb.op("pe", lambda e: e.transpose(tpb[:, j, :], ab[:, j * 128:(j + 1) * 128], c.identb[:]),
                          reads=[d_ab, c.d_ident], writes=[d_tpb])
                at = aTs[s % 2]
                kb.op("act", lambda e: e.activation(out=at[:], in_=tpb[:, 0:4, :], func=AF.Copy),
                      reads=[d_tpb], writes=[d_aTs[s % 2]])
                t0 = q0 + s * 128
                kb.dma("sp", aTv[:, :, t0:t0 + 128], at[:], reads=[d_aTs[s % 2]], writes=[c.d_aT_d])
        kb.barrier()


def phaseC1(c, L, dbg):
    nc, kb = c.nc, c.kb
    with ExitStack() as ps_:
        def tsb(name, shape, dt=F32):
            return ps_.enter_context(nc.sbuf_tensor(f"{name}_{L}", list(shape), dt))

        def tps(name, shape, dt=F32):
            return ps_.enter_context(nc.psum_tensor(f"{name}_{L}", list(shape), dt))
        wout = tsb("wout", [128, 8, D], BF16)
        d_w = Dep()
        wsrc = c.w_out[L].rearrange("(kt p) n -> p kt n", p=128)
        for kt in range(8):
            kb.dma("pool", wout[:, kt, :], wsrc[:, kt, :], writes=[d_w])
        bc = {}
        d_bc = Dep()
        for nm, row, k in (("G1l", 0, 2), ("A2l", 0, 3), ("S2l", 0, 4), ("G1c", 1, 2), ("A2c", 1, 3), ("S2c", 1, 4)):
            bc[nm] = tsb(nm, [128, D])
            load_bc(c, bc[nm], row, k, d_bc)
        NB = 2
        xt = [tsb(f"xc{i}", [128, D]) for i in range(NB)]
        d_xt = [Dep() for _ in range(NB)]
        asT = [tsb(f"asT{i}", [128, 8, 128], BF16) for i in range(NB)]
        d_as = [Dep() for _ in range(NB)]
        pp = [tps(f"pc{i}", [128, 512]) for i in range(2)]
        d_pp = [Dep(), Dep()]
        tp = tps("tpc", [128, 8, 128], BF16)
        d_tp = Dep()
        tmp = tsb("tmpc", [128, D])
        d_tmp = Dep()
        x1 = [tsb(f"x1{i}", [128, D]) for i in range(NB)]
        d_x1 = [Dep() for _ in range(NB)]
        junk = tsb("junkc", [128, D])
        d_junk = Dep()
        ss = tsb("ssc", [128, 1])
        d_ss = Dep()
        hf = tsb("hfc", [128, D])
        d_hf = Dep()
        hb = tsb("hbc", [128, D], BF16)
        d_hb = Dep()
        hT = [tsb(f"hTc{i}", [128, 8, 128], BF16) for i in range(NB)]
        d_hT = [Dep() for _ in range(NB)]
        aTv = c.aT_d.rearrange("(k p) t -> p k t", p=128)
        sTv = c.sT_d.rearrange("(k p) t -> p k t", p=128)
        hTv = c.h2T_d.rearrange("(k p) t -> p k t", p=128)
        for i in range(NT):
            b = i % NB
            sfx = "c" if i < 2 else "l"
            rows = slice(i * 128, (i + 1) * 128)
            kb.dma("sp", xt[b][:], c.xres[rows, :], reads=[c.d_xres[i]], writes=[d_xt[b]])
            kb.dma("sp", asT[b][:, 0:4, :], aTv[:, :, rows], reads=[c.d_aT_d], writes=[d_as[b]])
            kb.dma("sp", asT[b][:, 4:8, :], sTv[:, :, rows], reads=[c.d_sT_d], writes=[d_as[b]])
            for j in range(2):
                for kt in range(8):
                    kb.op("pe", lambda e: e.matmul(pp[j][:, :], lhsT=asT[b][:, kt, :], rhs=wout[:, kt, j * 512:(j + 1) * 512],
                                                   start=(kt == 0), stop=(kt == 7)),
                          reads=[d_as[b], d_w], writes=[d_pp[j]])
                kb.op("dve", lambda e: e.tensor_tensor(out=tmp[:, j * 512:(j + 1) * 512], in0=pp[j][:, :],
                                                       in1=bc["G1" + sfx][:, j * 512:(j + 1) * 512], op=ALU.mult),
                      reads=[d_pp[j], d_bc], writes=[d_tmp])
            kb.op("dve", lambda e: e.tensor_tensor(out=x1[b][:], in0=xt[b][:], in1=tmp[:], op=ALU.add),
                  reads=[d_xt[b], d_tmp], writes=[d_x1[b]])
            kb.dma("sp", c.xres[rows, :], x1[b][:], reads=[d_x1[b]], writes=[c.d_xres[i]])
            kb.op("act", lambda e: e.activation(out=junk[:], in_=x1[b][:], func=AF.Square, accum_out=ss[:]),
                  reads=[d_x1[b]], writes=[d_junk, d_ss])
            rstd_from_ss(c, ss[:], D, d_ss)
            kb.op("dve", lambda e: e.scalar_tensor_tensor(out=hf[:], in0=x1[b][:], scalar=ss[:], in1=bc["A2" + sfx][:],
                                                          op0=ALU.mult, op1=ALU.mult),
                  reads=[d_x1[b], d_ss, d_bc], writes=[d_hf])
            kb.op("dve", lambda e: e.tensor_tensor(out=hb[:], in0=hf[:], in1=bc["S2" + sfx][:], op=ALU.add),
                  reads=[d_hf, d_bc], writes=[d_hb])
            for kt in range(8):
                kb.op("pe", lambda e: e.transpose(tp[:, kt, :], hb[:, kt * 128:(kt + 1) * 128], c.identb[:]),
                      reads=[d_hb, c.d_ident], writes=[d_tp])
            kb.op("act", lambda e: e.activation(out=hT[b][:], in_=tp[:], func=AF.Copy), reads=[d_tp], writes=[d_hT[b]])
            kb.dma("sp", hTv[:, :, rows], hT[b][:], reads=[d_hT[b]], writes=[c.d_h2T_d])
        kb.barrier()


def phaseC2(c, L, dbg):
    import os
    nc, kb = c.nc, c.kb
    with ExitStack() as ps_:
        def tsb(name, shape, dt=F32):
            return ps_.enter_context(nc.sbuf_tensor(f"{name}_{L}", list(shape), dt))

        def tps(name, shape, dt=F32):
            return ps_.enter_context(nc.psum_tensor(f"{name}_{L}", list(shape), dt))
        wup = tsb("wup", [128, 8, 2 * DFF], BF16)
        wdn = tsb("wdn", [128, 22, D], BF16)
        d_w = Dep()
        usrc = c.w_up[L].rearrange("(kt p) n -> p kt n", p=128)
        for kt in range(8):
            for q in range(4):
                kb.dma("pool", wup[:, kt, q * 1408:(q + 1) * 1408], usrc[:, kt, q * 1408:(q + 1) * 1408], writes=[d_w])
        dsrc = c.w_down[L].rearrange("(kt p) n -> p kt n", p=128)
        for kt in range(22):
            kb.dma("pool", wdn[:, kt, :], dsrc[:, kt, :], writes=[d_w])
        cw = tsb("cw", [128, 44, 3])
        cb = tsb("cb", [128, 44])
        d_cw = Dep()
        kb.dma("sp", cw[:], c.cwT[L], writes=[d_cw])
        kb.dma("sp", cb[:], c.cbT[L], writes=[d_cw])
        G2 = {}
        d_bc = Dep()
        for nm, row in (("l", 0), ("c", 1)):
            G2[nm] = tsb("G2" + nm, [128, D])
            load_bc(c, G2[nm], row, 5, d_bc)
        NB = 2
        h2 = [tsb(f"h2t{i}", [128, 8, 258], BF16) for i in range(NB)]
        d_h2 = [Dep() for _ in range(NB)]
        psA = [tps(f"psA{i}", [128, 512]) for i in range(3)]
        psG = [tps(f"psG{i}", [128, 512]) for i in range(3)]
        d_pA = [Dep() for _ in range(3)]
        d_pG = [Dep() for _ in range(3)]
        pso = [tps(f"pso{i}", [128, 512]) for i in range(2)]
        d_po = [Dep(), Dep()]
        ca = [tsb(f"ca{i}", [128, 256]) for i in range(3)]
        cg = [tsb(f"cg{i}", [128, 256]) for i in range(3)]
        d_ca = [Dep() for _ in range(3)]
        d_cg = [Dep() for _ in range(3)]
        hid = [tsb(f"hid{i}", [128, 22, 256], BF16) for i in range(NB)]
        d_hid = [Dep() for _ in range(NB)]
        xt = [tsb(f"xf{i}", [128, D]) for i in range(NB)]
        d_xt = [Dep() for _ in range(NB)]
        tmp = tsb("tmpf", [128, D])
        d_tmp = Dep()
        hTv = c.h2T_d.rearrange("(k p) t -> p k t", p=128)
        tiles = [(0, 0, NCTX)] + [(NCTX + 256 * j, NCTX, T) for j in range(S // 256)]
        ntile = int(os.environ.get("NFF", len(tiles)))
        it = 0
        for ti, (t0, s0, s1) in enumerate(tiles[:ntile]):
            b = ti % NB
            lo = t0 - 1
            hi = t0 + 257
            c0 = 0
            if lo < s0:
                kb.op("pool", lambda e: e.memset(h2[b][:, :, 0:1], 0.0), writes=[d_h2[b]])
                lo, c0 = s0, 1
            c1 = 258
            if hi > s1:
                kb.op("pool", lambda e: e.memset(h2[b][:, :, 257:258], 0.0), writes=[d_h2[b]])
                hi, c1 = s1, 257
            kb.dma("sp", h2[b][:, :, c0:c1], hTv[:, :, lo:hi], reads=[c.d_h2T_d], writes=[d_h2[b]])
            for m in range(22):
                pb = it % 3
                it += 1
                for (ps, dps, off) in ((psA[pb], d_pA[pb], 0), (psG[pb], d_pG[pb], DFF)):
                    for kt in range(8):
                        kb.op("pe", lambda e: e.matmul(ps[:, 0:258], lhsT=wup[:, kt, off + m * 128:off + (m + 1) * 128],
                                                       rhs=h2[b][:, kt, :], start=(kt == 0), stop=(kt == 7)),
                              reads=[d_w, d_h2[b]], writes=[dps])
                grp = ((psA[pb], d_pA[pb], ca[pb], d_ca[pb], m), (psG[pb], d_pG[pb], cg[pb], d_cg[pb], 22 + m))
                for (ps, dps, dst, ddst, ch) in grp:
                    kb.op("act", lambda e: e.activation(out=dst[:], in_=ps[:, 1:257], func=AF.Identity,
                                                        scale=cw[:, ch, 1:2], bias=cb[:, ch:ch + 1]),
                          reads=[dps, d_cw], writes=[ddst])
                for tap, (a0, a1) in ((0, (0, 256)), (2, (2, 258))):
                    for (ps, dps, dst, ddst, ch) in grp:
                        kb.op("dve", lambda e: e.scalar_tensor_tensor(out=dst[:], in0=ps[:, a0:a1], scalar=cw[:, ch, tap:tap + 1],
                                                                      in1=dst[:], op0=ALU.mult, op1=ALU.add),
                              reads=[dps, d_cw], writes=[ddst])
                kb.op("act", lambda e: e.activation(out=cg[pb][:], in_=cg[pb][:], func=AF.Silu),
                      reads=[d_cg[pb]], writes=[d_cg[pb]])
                kb.op("pool", lambda e: e.tensor_tensor(out=hid[b][:, m, :], in0=cg[pb][:], in1=ca[pb][:], op=ALU.mult),
                      reads=[d_cg[pb], d_ca[pb]], writes=[d_hid[b]])
            sfx = "c" if t0 < NCTX else "l"
            for s in range(2):
                i = (t0 + s * 128) // 128
                rows = slice(i * 128, (i + 1) * 128)
                xb = (2 * ti + s) % NB
                kb.dma("sp", xt[xb][:], c.xres[rows, :], reads=[c.d_xres[i]], writes=[d_xt[xb]])
                for j in range(2):
                    for m in range(22):
                        kb.op("pe", lambda e: e.matmul(pso[j][:, :], lhsT=hid[b][:, m, s * 128:(s + 1) * 128],
                                                       rhs=wdn[:, m, j * 512:(j + 1) * 512], start=(m == 0), stop=(m == 21)),
                              reads=[d_hid[b], d_w], writes=[d_po[j]])
                    kb.op("dve", lambda e: e.tensor_tensor(out=tmp[:, j * 512:(j + 1) * 512], in0=pso[j][:, :],
                                                           in1=G2[sfx][:, j * 512:(j + 1) * 512], op=ALU.mult),
                          reads=[d_po[j], d_bc], writes=[d_tmp])
                kb.op("dve", lambda e: e.tensor_tensor(out=xt[xb][:], in0=xt[xb][:], in1=tmp[:], op=ALU.add),
                      reads=[d_xt[xb], d_tmp], writes=[d_xt[xb]])
                kb.dma("sp", c.xres[rows, :], xt[xb][:], reads=[d_xt[xb]], writes=[c.d_xres[i]])
        kb.barrier()


def zoh(c, kb, mk, lr, li, ls, F, tag):
    d = Dep()
    names = ["dt", "t", "mag", "a16", "s", "cc", "c2", "s2", "sc", "are", "aim", "den", "am1", "fre", "fim", "u1", "u2"]
    t = {n: mk(f"z{tag}_{n}", [128, F]) for n in names}

    def tt(out, a, b, op):
        kb.op("dve", lambda e: e.tensor_tensor(out=t[out][:], in0=t[a][:] if isinstance(a, str) else a,
                                               in1=t[b][:] if isinstance(b, str) else b, op=op), reads=[d], writes=[d])
    kb.op("act", lambda e: e.activation(out=t["dt"][:], in_=ls, func=AF.Exp), reads=[d], writes=[d])
    tt("t", lr, "dt", ALU.mult)
    kb.op("act", lambda e: e.activation(out=t["mag"][:], in_=t["t"][:], func=AF.Exp), reads=[d], writes=[d])
    kb.op("dve", lambda e: e.scalar_tensor_tensor(out=t["a16"][:], in0=li, scalar=1.0 / 16.0, in1=t["dt"][:],
                                                  op0=ALU.mult, op1=ALU.mult), reads=[d], writes=[d])
    kb.op("act", lambda e: e.activation(out=t["s"][:], in_=t["a16"][:], func=AF.Sin), reads=[d], writes=[d])
    kb.op("act", lambda e: e.activation(out=t["cc"][:], in_=t["a16"][:], func=AF.Sin, bias=c.halfpi),
          reads=[d, c.d_const], writes=[d])
    for _ in range(4):
        tt("c2", "cc", "cc", ALU.mult)
        tt("s2", "s", "s", ALU.mult)
        tt("sc", "s", "cc", ALU.mult)
        tt("cc", "c2", "s2", ALU.subtract)
        tt("s", "sc", "sc", ALU.add)
    tt("are", "mag", "cc", ALU.mult)
    tt("aim", "mag", "s", ALU.mult)
    tt("u1", lr, lr, ALU.mult)
    tt("u2", li, li, ALU.mult)
    tt("den", "u1", "u2", ALU.add)
    kb.op("dve", lambda e: e.reciprocal(out=t["den"][:], in_=t["den"][:]), reads=[d], writes=[d])
    kb.op("dve", lambda e: e.tensor_scalar(out=t["am1"][:], in0=t["are"][:], scalar1=-1.0, scalar2=None, op0=ALU.add),
          reads=[d], writes=[d])
    tt("u1", "am1", lr, ALU.mult)
    tt("u2", "aim", li, ALU.mult)
    tt("u1", "u1", "u2", ALU.add)
    tt("fre", "u1", "den", ALU.mult)
    tt("u1", "aim", lr, ALU.mult)
    tt("u2", "am1", li, ALU.mult)
    tt("u1", "u1", "u2", ALU.subtract)
    tt("fim", "u1", "den", ALU.mult)
    return t["are"], t["aim"], t["fre"], t["fim"], d


def phaseS(c, L, dbg):
    import os
    nc, kb = c.nc, c.kb
    NK = 13
    segs = [(0, NCTX)] + [(NCTX + 512 * j, 512) for j in range(S // 512)]
    with ExitStack() as ps_:
        def tsb(name, shape, dt=F32):
            return ps_.enter_context(nc.sbuf_tensor(f"{name}_{L}", list(shape), dt))

        def tps(name, shape, dt=F32):
            return ps_.enter_context(nc.psum_tensor(f"{name}_{L}", list(shape), dt))
        z = tsb("zs", [128, T], BF16)
        d_z = Dep()
        c.d_zT_d = Dep()
        P2r, P2i, P2n = tsb("P2r", [128, NK, 32]), tsb("P2i", [128, NK, 32]), tsb("P2n", [128, NK, 32])
        BT = {}
        for j_ in range(4):
            for kind_ in ("r", "i"):
                BT[kind_ + str(j_)] = tsb("BT" + kind_ + str(j_), [128, 8, 128], BF16)
        CpR = tsb("CpR", [128, 8, 640], BF16)
        CpI = tsb("CpI", [128, 8, 640], BF16)
        with ExitStack() as pp_:
            def psb(name, shape, dt=F32):
                return pp_.enter_context(nc.sbuf_tensor(f"{name}_{L}", list(shape), dt))
            d_in = Dep()
            lrN, liN, lsN = psb("lrN", [128, 32]), psb("liN", [128, 32]), psb("lsN", [128, 32])
            lrQ, liQ, lsQ = psb("lrQ", [128, 512]), psb("liQ", [128, 512]), psb("lsQ", [128, 512])
            for tl, src in ((lrN, c.lamN_re), (liN, c.lamN_im), (lsN, c.lsN)):
                kb.dma("sp", tl[:], src[L], writes=[d_in])
            for tl, src in ((lrQ, c.lamQ_re), (liQ, c.lamQ_im), (lsQ, c.lsQ)):
                kb.dma("sp", tl[:], src[L], writes=[d_in])
            BQr, BQi = psb("BQr", [128, 512]), psb("BQi", [128, 512])
            kb.dma("sp", BQr[:], c.BQ_re[L], writes=[d_in])
            kb.dma("sp", BQi[:], c.BQ_im[L], writes=[d_in])
            CNr, CNi = psb("CNr", [128, 8, 4, 16]), psb("CNi", [128, 8, 4, 16])
            kb.dma("sp", CNr[:], c.CNr[L], writes=[d_in])
            kb.dma("sp", CNi[:], c.CNi[L], writes=[d_in])
            kb.barrier()
            areN, aimN, _, _, dN = zoh(c, kb, psb, lrN[:], liN[:], lsN[:], 32, "N")
            _, _, freQ, fimQ, dQ = zoh(c, kb, psb, lrQ[:], liQ[:], lsQ[:], 512, "Q")
            d_P2 = Dep()
            t1, t2 = psb("pw1", [128, 32]), psb("pw2", [128, 32])
            kb.op("dve", lambda e: e.tensor_copy(out=P2r[:, 0, :], in_=areN[:]), reads=[dN], writes=[d_P2])
            kb.op("dve", lambda e: e.tensor_copy(out=P2i[:, 0, :], in_=aimN[:]), reads=[dN], writes=[d_P2])
            for k in range(1, NK):
                kb.op("dve", lambda e: e.tensor_tensor(out=t1[:], in0=P2r[:, k - 1, :], in1=P2r[:, k - 1, :], op=ALU.mult),
                      reads=[d_P2], writes=[d_P2])
                kb.op("dve", lambda e: e.tensor_tensor(out=t2[:], in0=P2i[:, k - 1, :], in1=P2i[:, k - 1, :], op=ALU.mult),
                      reads=[d_P2], writes=[d_P2])
                kb.op("dve", lambda e: e.tensor_tensor(out=P2r[:, k, :], in0=t1[:], in1=t2[:], op=ALU.subtract),
                      reads=[d_P2], writes=[d_P2])
                kb.op("dve", lambda e: e.tensor_tensor(out=t1[:], in0=P2r[:, k - 1, :], in1=P2i[:, k - 1, :], op=ALU.mult),
                      reads=[d_P2], writes=[d_P2])
                kb.op("dve", lambda e: e.tensor_tensor(out=P2i[:, k, :], in0=t1[:], in1=t1[:], op=ALU.add),
                      reads=[d_P2], writes=[d_P2])
            kb.op("dve", lambda e: e.tensor_scalar(out=P2n[:], in0=P2i[:], scalar1=-1.0, scalar2=None, op0=ALU.mult),
                  reads=[d_P2], writes=[d_P2])
            bre, bim, u1, u2 = psb("bre", [128, 512]), psb("bim", [128, 512]), psb("bu1", [128, 512]), psb("bu2", [128, 512])
            d_b = Dep()

            def tq(out, a, b, op):
                kb.op("dve", lambda e: e.tensor_tensor(out=out[:], in0=a[:], in1=b[:], op=op), reads=[dQ, d_in, d_b],
                      writes=[d_b])
            tq(u1, freQ, BQr, ALU.mult)
            tq(u2, fimQ, BQi, ALU.mult)
            tq(bre, u1, u2, ALU.subtract)
            tq(u1, freQ, BQi, ALU.mult)
            tq(u2, fimQ, BQr, ALU.mult)
            tq(bim, u1, u2, ALU.add)
            d_BT = Dep()
            bre3 = bre[:].rearrange("p (a n) -> p a n", a=8)
            bim3 = bim[:].rearrange("p (a n) -> p a n", a=8)
            for j in range(4):
                mA = c.cst[:, 8 + 2 * j:9 + 2 * j]
                mB = c.cst[:, 9 + 2 * j:10 + 2 * j]
                for kind, src in (("r", bre3), ("i", bim3)):
                    kb.op("dve", lambda e: e.tensor_scalar(out=BT[kind + str(j)][:, :, 0:64], in0=src, scalar1=mA, scalar2=None,
                                                           op0=ALU.mult), reads=[d_b, c.d_const], writes=[d_BT])
                    kb.op("dve", lambda e: e.tensor_scalar(out=BT[kind + str(j)][:, :, 64:128], in0=src, scalar1=mB,
                                                           scalar2=None, op0=ALU.mult), reads=[d_b, c.d_const], writes=[d_BT])
            d_C = Dep()
            kb.op("pool", lambda e: e.memset(CpR[:], 0.0), writes=[d_C])
            kb.op("pool", lambda e: e.memset(CpI[:], 0.0), writes=[d_C])
            for (Cp, CNx, sc) in ((CpR, CNr, 1.0), (CpI, CNi, -1.0)):
                v = Cp[:].rearrange("p a (j x) -> p a j x", x=160)
                kb.op("dve", lambda e: e.tensor_scalar(out=v[0:64, :, :, 0:16], in0=CNx[0:64], scalar1=sc, scalar2=None,
                                                       op0=ALU.mult), reads=[d_in, d_C], writes=[d_C])
                kb.op("dve", lambda e: e.tensor_scalar(out=v[64:128, :, :, 16:32], in0=CNx[64:128], scalar1=sc, scalar2=None,
                                                       op0=ALU.mult), reads=[d_in, d_C], writes=[d_C])
            kb.barrier()
        dQv = tsb("dQv", [128, 4])
        d_dq = Dep()
        kb.dma("sp", dQv[:], c.dQ[L], writes=[d_dq])
        uT = tsb("uTs", [128, T], BF16)
        d_u = Dep()
        RA = [tsb(f"RA{i}", [128, T]) for i in range(2)]
        IA = [tsb(f"IA{i}", [128, T]) for i in range(2)]
        d_R = [[Dep(), Dep(), Dep()] for _ in range(2)]
        d_I = [[Dep(), Dep(), Dep()] for _ in range(2)]
        FR = float(os.environ.get("FR", "1.0"))
        NTP = int((T - 1) * (1.0 - FR)) + 8
        TP = [tsb(f"TP{i}", [128, NTP]) for i in range(4)]
        d_TP = [Dep() for _ in range(4)]
        HbR, HbI = tsb("HbR", [128, T], BF16), tsb("HbI", [128, T], BF16)
        d_Hb = Dep()
        Yacc = tsb("Yacc", [128, T])
        d_Y = Dep()
        px = [tps(f"px{i}", [128, 512]) for i in range(2)]
        pxs = [tps(f"pxs{i}", [128, 512]) for i in range(2)]
        d_px = [Dep(), Dep()]
        d_pxs = [Dep(), Dep()]
        py = [tps(f"py{i}", [128, 512]) for i in range(2)]
        d_py = [Dep(), Dep()]
        uTv = c.uT_d.rearrange("(k p) t -> p k t", p=128)
        nck = int(os.environ.get("NCK", 4))
        ngd = int(os.environ.get("NGD", 8))
        it = 0
        for ck in range(nck):
            kb.dma("sp", uT[:], uTv[:, ck, :], reads=[c.d_uT_d], writes=[d_u])
            kb.op("dve", lambda e: e.tensor_scalar(out=Yacc[:], in0=uT[:], scalar1=dQv[:, ck:ck + 1], scalar2=None,
                                                   op0=ALU.mult), reads=[d_u, d_dq], writes=[d_Y])
            for gd in range(ngd):
                dr, j = gd // 4, gd % 4
                dc = dr * 4 + ck
                col = dr * 16 + ck * 4 + j
                cur = 0
                for si, (t0, n) in enumerate(segs):
                    pb = it % 2
                    it += 1
                    o0 = t0 if dr == 0 else (t0 - NCTX if t0 >= NCTX else S)
                    for (ps, dps, kind, dst, dd) in ((px[pb], d_px[pb], "r", RA[cur], d_R[cur][0]),
                                                     (pxs[pb], d_pxs[pb], "i", IA[cur], d_I[cur][0])):
                        kb.op("pe", lambda e: e.matmul(ps[:, 0:n], lhsT=BT[kind + str(j)][:, dc, :], rhs=uT[:, t0:t0 + n],
                                                       start=True, stop=True), reads=[d_BT, d_u], writes=[dps])
                        kb.op("act", lambda e: e.activation(out=dst[:, o0:o0 + n], in_=ps[:, 0:n], func=AF.Copy),
                              reads=[dps] + d_R[cur][1:] + d_I[cur][1:], writes=[dd])
                for k in range(NK):
                    s = 1 << k
                    nxt = 1 - cur
                    ar, ai, an = P2r[:, k, col:col + 1], P2i[:, k, col:col + 1], P2n[:, k, col:col + 1]
                    if dr == 0:
                        lo, hi, keep = slice(0, T - s), slice(s, T), slice(0, s)
                    else:
                        lo, hi, keep = slice(s, T), slice(0, T - s), slice(T - s, T)
                    n_hi = T - s
                    n1 = (int(n_hi * FR) // 2) * 2
                    h0 = hi.start
                    l0 = lo.start
                    hi1, lo1 = slice(h0, h0 + n1), slice(l0, l0 + n1)
                    hi2, lo2 = slice(h0 + n1, h0 + n_hi), slice(l0 + n1, l0 + n_hi)
                    n2 = n_hi - n1
                    rdR = [d_R[cur][0], d_R[cur][1], d_R[cur][2]]
                    rdI = [d_I[cur][0], d_I[cur][1], d_I[cur][2]]
                    kb.op("dve", lambda e: e.scalar_tensor_tensor(out=RA[nxt][:, hi1], in0=RA[cur][:, lo1], scalar=ar,
                                                                  in1=RA[cur][:, hi1], op0=ALU.mult, op1=ALU.add),
                          reads=rdR + [d_P2], writes=[d_R[nxt][0]])
                    kb.op("dve", lambda e: e.scalar_tensor_tensor(out=IA[nxt][:, hi1], in0=IA[cur][:, lo1], scalar=ar,
                                                                  in1=IA[cur][:, hi1], op0=ALU.mult, op1=ALU.add),
                          reads=rdI + [d_P2], writes=[d_I[nxt][0]])
                    kb.op("dve", lambda e: e.scalar_tensor_tensor(out=RA[nxt][:, hi1], in0=IA[cur][:, lo1], scalar=an,
                                                                  in1=RA[nxt][:, hi1], op0=ALU.mult, op1=ALU.add),
                          reads=rdI + [d_P2], writes=[d_R[nxt][0]])
                    kb.op("dve", lambda e: e.scalar_tensor_tensor(out=IA[nxt][:, hi1], in0=RA[cur][:, lo1], scalar=ai,
                                                                  in1=IA[nxt][:, hi1], op0=ALU.mult, op1=ALU.add),
                          reads=rdR + [d_P2], writes=[d_I[nxt][0]])
                    if n2 > 0:
                        for q, (src, rd, sc) in enumerate(((RA[cur], rdR, ar), (IA[cur], rdI, ar), (IA[cur], rdI, an),
                                                           (RA[cur], rdR, ai))):
                            kb.op("act", lambda e: e.activation(out=TP[q][:, 0:n2], in_=src[:, lo2], func=AF.Copy, scale=sc),
                                  reads=rd + [d_P2], writes=[d_TP[q]])
                        kb.op("pool", lambda e: e.tensor_tensor(out=RA[nxt][:, hi2], in0=TP[0][:, 0:n2], in1=RA[cur][:, hi2],
                                                                op=ALU.add), reads=rdR + [d_TP[0]], writes=[d_R[nxt][1]])
                        kb.op("pool", lambda e: e.tensor_tensor(out=IA[nxt][:, hi2], in0=TP[1][:, 0:n2], in1=IA[cur][:, hi2],
                                                                op=ALU.add), reads=rdI + [d_TP[1]], writes=[d_I[nxt][1]])
                        kb.op("pool", lambda e: e.tensor_tensor(out=RA[nxt][:, hi2], in0=TP[2][:, 0:n2], in1=RA[nxt][:, hi2],
                                                                op=ALU.add), reads=[d_TP[2]], writes=[d_R[nxt][1]])
                        kb.op("pool", lambda e: e.tensor_tensor(out=IA[nxt][:, hi2], in0=TP[3][:, 0:n2], in1=IA[nxt][:, hi2],
                                                                op=ALU.add), reads=[d_TP[3]], writes=[d_I[nxt][1]])
                    kb.op("act", lambda e: e.activation(out=RA[nxt][:, keep], in_=RA[cur][:, keep], func=AF.Copy),
                          reads=rdR, writes=[d_R[nxt][2]])
                    kb.op("act", lambda e: e.activation(out=IA[nxt][:, keep], in_=IA[cur][:, keep], func=AF.Copy),
                          reads=rdI, writes=[d_I[nxt][2]])
                    cur = nxt
                kb.op("act", lambda e: e.activation(out=HbR[:], in_=RA[cur][:], func=AF.Copy),
                      reads=d_R[cur], writes=[d_Hb])
                kb.op("act", lambda e: e.activation(out=HbI[:], in_=IA[cur][:], func=AF.Copy),
                      reads=d_I[cur], writes=[d_Hb])
                for si, (t0, n) in enumerate(segs):
                    o0 = t0 if dr == 0 else (t0 - NCTX if t0 >= NCTX else S)
                    pb = it % 2
                    it += 1
                    kb.op("pe", lambda e: e.matmul(py[pb][:, 0:n], lhsT=CpR[:, dc, j * 128:(j + 1) * 128],
                                                   rhs=HbR[:, o0:o0 + n], start=True, stop=False),
                          reads=[d_C, d_Hb], writes=[d_py[pb]])
                    kb.op("pe", lambda e: e.matmul(py[pb][:, 0:n], lhsT=CpI[:, dc, j * 128:(j + 1) * 128],
                                                   rhs=HbI[:, o0:o0 + n], start=False, stop=True),
                          reads=[d_C, d_Hb], writes=[d_py[pb]])
                    kb.op("dve", lambda e: e.tensor_tensor(out=Yacc[:, t0:t0 + n], in0=py[pb][:, 0:n],
                                                           in1=Yacc[:, t0:t0 + n], op=ALU.add),
                          reads=[d_py[pb]], writes=[d_Y])
            g1 = RA[0]
            allRI = d_R[0] + d_R[1] + d_I[0] + d_I[1]
            kb.op("dve", lambda e: e.tensor_tensor(out=g1[:], in0=Yacc[:], in1=Yacc[:], op=ALU.mult),
                  reads=[d_Y], writes=allRI)
            kb.op("dve", lambda e: e.tensor_scalar(out=g1[:], in0=g1[:], scalar1=0.044715, scalar2=1.0, op0=ALU.mult,
                                                   op1=ALU.add), reads=[], writes=allRI)
            kb.op("dve", lambda e: e.tensor_tensor(out=g1[:], in0=g1[:], in1=Yacc[:], op=ALU.mult),
                  reads=[d_Y], writes=allRI)
            kb.op("act", lambda e: e.activation(out=g1[:], in_=g1[:], func=AF.Sigmoid, scale=1.5957691216057308),
                  reads=[], writes=allRI)
            kb.op("dve", lambda e: e.tensor_tensor(out=z[:], in0=g1[:], in1=Yacc[:], op=ALU.mult),
                  reads=allRI + [d_Y], writes=[d_z])
            kb.dma("sp", c.zT_d[ck * 128:(ck + 1) * 128, :], z[:], reads=[d_z], writes=[c.d_zT_d])
        if "pS1" in dbg:
            o = nc.dram_tensor("dbg_z", [512, T], BF16, kind="ExternalOutput").ap()
            tk = kb.dma("sp", o, c.zT_d[:, :], reads=[c.d_zT_d])
            kb.wait("sp", tk)
            kb.barrier()
            return
        kb.barrier()
    phaseS2(c, L, dbg, segs)


def phaseS2(c, L, dbg, segs):
    nc, kb = c.nc, c.kb
    with ExitStack() as ps_:
        def tsb(name, shape, dt=F32):
            return ps_.enter_context(nc.sbuf_tensor(f"{name}_{L}", list(shape), dt))

        def tps(name, shape, dt=F32):
            return ps_.enter_context(nc.psum_tensor(f"{name}_{L}", list(shape), dt))
        wglu = tsb("wglu", [128, 4, 512], BF16)
        d_w = Dep()
        wsrc = c.w_glu[L].rearrange("(kt p) n -> p kt n", p=128)
        for kt in range(4):
            kb.dma("pool", wglu[:, kt, :], wsrc[:, kt, :], writes=[d_w])
        bgl, gs = tsb("bgl", [128, 4]), tsb("gs", [128, 4])
        kb.dma("sp", bgl[:], c.bgluT[L], writes=[d_w])
        kb.dma("sp", gs[:], c.ongT[L], writes=[d_w])
        zt = [tsb(f"zt{i}", [128, 4, 512], BF16) for i in range(2)]
        d_zt = [Dep(), Dep()]
        pg = [tps(f"pg{i}", [128, 512]) for i in range(2)]
        d_pg = [Dep(), Dep()]
        pss = tps("pss", [128, 512])
        d_pss = Dep()
        sg = [tsb(f"sg{i}", [128, 512]) for i in range(2)]
        d_sg = [Dep(), Dep()]
        o = tsb("og", [128, 4, 512])
        d_o = Dep()
        sq = tsb("sqg", [128, 4, 512])
        d_sq = Dep()
        rs = tsb("rsg", [128, 512])
        d_rs = Dep()
        sb_ = [tsb(f"sbg{i}", [128, 4, 512], BF16) for i in range(2)]
        d_sb = [Dep(), Dep()]
        zv = c.zT_d.rearrange("(k p) t -> p k t", p=128)
        sv = c.sT_d.rearrange("(k p) t -> p k t", p=128)
        it = 0
        for si, (t0, n) in enumerate(segs):
            b = si % 2
            kb.dma("sp", zt[b][:, :, 0:n], zv[:, :, t0:t0 + n], reads=[c.d_zT_d], writes=[d_zt[b]])
            for m in range(4):
                pb = it % 2
                it += 1
                for k in range(4):
                    kb.op("pe", lambda e: e.matmul(pg[pb][:, 0:n], lhsT=wglu[:, k, m * 128:(m + 1) * 128], rhs=zt[b][:, k, 0:n],
                                                   start=(k == 0), stop=(k == 3)), reads=[d_w, d_zt[b]], writes=[d_pg[pb]])
                kb.op("act", lambda e: e.activation(out=sg[pb][:, 0:n], in_=pg[pb][:, 0:n], func=AF.Sigmoid,
                                                    bias=bgl[:, m:m + 1]), reads=[d_pg[pb], d_w], writes=[d_sg[pb]])
                kb.op("dve", lambda e: e.tensor_tensor(out=o[:, m, 0:n], in0=sg[pb][:, 0:n], in1=zt[b][:, m, 0:n], op=ALU.mult),
                      reads=[d_sg[pb], d_zt[b]], writes=[d_o])
                kb.op("pool", lambda e: e.tensor_tensor(out=sq[:, m, 0:n], in0=o[:, m, 0:n], in1=o[:, m, 0:n], op=ALU.mult),
                      reads=[d_o], writes=[d_sq])
            for m in range(4):
                kb.op("pe", lambda e: e.matmul(pss[:, 0:n], lhsT=c.onesf[:], rhs=sq[:, m, 0:n], start=(m == 0), stop=(m == 3)),
                      reads=[d_sq, c.d_const], writes=[d_pss])
            kb.op("act", lambda e: e.activation(out=rs[:, 0:n], in_=pss[:, 0:n], func=AF.Identity, scale=1.0 / 512.0,
                                                bias=c.cst[:, 1:2]), reads=[d_pss, c.d_const], writes=[d_rs])
            kb.op("act", lambda e: e.activation(out=rs[:, 0:n], in_=rs[:, 0:n], func=AF.Sqrt), reads=[d_rs], writes=[d_rs])
            kb.op("dve", lambda e: e.reciprocal(out=rs[:, 0:n], in_=rs[:, 0:n]), reads=[d_rs], writes=[d_rs])
            for m in range(4):
                kb.op("dve", lambda e: e.scalar_tensor_tensor(out=sb_[b][:, m, 0:n], in0=o[:, m, 0:n], scalar=gs[:, m:m + 1],
                                                              in1=rs[:, 0:n], op0=ALU.mult, op1=ALU.mult),
                      reads=[d_o, d_rs, d_w], writes=[d_sb[b]])
            kb.dma("sp", sv[:, :, t0:t0 + n], sb_[b][:, :, 0:n], reads=[d_sb[b]], writes=[c.d_sT_d])
        kb.barrier()


def kernel(**inputs):
    nc, c = build(DEPTH)
    sh = prep_shared(inputs)
    in_maps = []
    for i in range(8):
        m = dict(sh)
        m.update(prep_core(inputs, i % 4))
        in_maps.append(m)
    res = run_bass_kernel_spmd(nc, in_maps, core_ids=list(range(8)))
    out = np.stack([np.asarray(res.results[b]["out"]) for b in range(4)], axis=0)
    return np.ascontiguousarray(out, dtype=np.float32)
```
